# Optimizing a Trainium2 kernel written in Bass

```python
import math
import jax, jax.numpy as jnp
from jax import lax
import numpy as np

D_MODEL = 1024
BATCH = 32
SEQ = 256
DEPTH = 4
DEC_BATCH = 4
DEC_SEQ = 2048
PAST_LEN = 256

GRID_W = 64
BLOCK = 128
WINDOW = 128
ROPE_BASE = 10000.0
EPS = 1e-6
DA_HEADS = 4
DA_QK = 32
DA_V = 2 * DA_QK
SW_HEADS = 8
SW_KV_HEADS = 2
SW_DIM = 64
MLA_HEADS = 4
MLA_Q_RANK = 256
MLA_KV_RANK = 128
MLA_NOPE = 64
MLA_ROPE = 32
MLA_V = 64
D_FF = 4 * D_MODEL
IN_SPLITS = (DA_HEADS * 2 * DA_QK, DA_HEADS * 2 * DA_QK, DA_HEADS * DA_V,
             SW_HEADS * SW_DIM, SW_KV_HEADS * SW_DIM, SW_KV_HEADS * SW_DIM,
             MLA_Q_RANK, MLA_KV_RANK + MLA_ROPE)
P_IN = sum(IN_SPLITS)
MIX_WIDTH = DA_HEADS * DA_V + SW_HEADS * SW_DIM + MLA_HEADS * MLA_V

kernel_name = "hybrid_diffusion_prefix_step"


def rmsnorm(x, g):
    xf = x.astype(jnp.float32)
    y = xf * lax.rsqrt(jnp.mean(xf * xf, axis=-1, keepdims=True) + EPS)
    return (y * g.astype(jnp.float32)).astype(x.dtype)


def rope_1d(x, pos):
    m = x.shape[-1]
    half = m // 2
    freqs = ROPE_BASE ** (-jnp.arange(half, dtype=jnp.float32) / half)
    ang = pos.astype(jnp.float32)[:, None] * freqs[None, :]
    bshape = (ang.shape[0],) + (1,) * (x.ndim - 3) + (half,)
    cos = jnp.cos(ang).reshape(bshape)
    sin = jnp.sin(ang).reshape(bshape)
    xf = x.astype(jnp.float32)
    x1, x2 = xf[..., :half], xf[..., half:]
    return jnp.concatenate([x1 * cos - x2 * sin, x1 * sin + x2 * cos], axis=-1).astype(x.dtype)


def axial_rope(x):
    S = x.shape[1]
    n_rows = S // GRID_W
    t = jnp.arange(n_rows * GRID_W)
    rows, cols = t // GRID_W, t % GRID_W
    m = x.shape[-1] // 2
    return jnp.concatenate([rope_1d(x[..., :m], rows), rope_1d(x[..., m:], cols)], axis=-1)


def _split_blocks(q):
    B, S = q.shape[:2]
    return jnp.moveaxis(q.reshape((B, S // BLOCK, BLOCK) + q.shape[2:]), 1, 0)


def _merge_blocks(o):
    nb, B = o.shape[:2]
    return jnp.moveaxis(o, 0, 1).reshape((B, nb * BLOCK) + o.shape[3:])


def gqa_attention(q, k, v, sink=None):
    B, S, H, d = q.shape
    G = k.shape[2]
    R = H // G
    scale = d ** -0.5
    qb = _split_blocks(q.reshape(B, S, G, R, d))

    def one_block(qi):
        s = jnp.einsum('bqgrd,btgd->bgrqt', qi, k).astype(jnp.float32) * scale
        if sink is not None:
            s_sink = jnp.broadcast_to(sink.reshape(1, G, R, 1, 1).astype(jnp.float32), s.shape[:-1] + (1,))
            s = jnp.concatenate([s, s_sink], axis=-1)
        p = jax.nn.softmax(s, axis=-1)
        if sink is not None:
            p = p[..., :-1]
        return jnp.einsum('bgrqt,btgv->bqgrv', p.astype(v.dtype), v)

    o = lax.map(one_block, qb)
    return _merge_blocks(o).reshape(B, S, H, v.shape[-1])


def diff_attention(q1, q2, k1, k2, v, lam):
    scale = q1.shape[-1] ** -0.5

    def one_block(qs):
        a, b = qs
        p1 = jax.nn.softmax(jnp.einsum('bqhd,bthd->bhqt', a, k1).astype(jnp.float32) * scale, axis=-1)
        p2 = jax.nn.softmax(jnp.einsum('bqhd,bthd->bhqt', b, k2).astype(jnp.float32) * scale, axis=-1)
        return jnp.einsum('bhqt,bthv->bqhv', (p1 - lam * p2).astype(v.dtype), v)

    o = lax.map(one_block, (_split_blocks(q1), _split_blocks(q2)))
    return _merge_blocks(o)


def swa_latent_attention(q, k, v, ck, cv, sink):
    B, S, H, d = q.shape
    G = k.shape[2]
    R = H // G
    T = ck.shape[1]
    nb = S // BLOCK
    scale = d ** -0.5
    qb = q.reshape(B, nb, BLOCK, G, R, d)
    pad = ((0, 0), (BLOCK, BLOCK), (0, 0), (0, 0))
    kp, vp = jnp.pad(k, pad), jnp.pad(v, pad)
    idx = jnp.arange(nb)[:, None] * BLOCK + jnp.arange(3 * BLOCK)[None, :]
    kb, vb = kp[:, idx], vp[:, idx]
    qpos = jnp.arange(nb)[:, None] * BLOCK + jnp.arange(BLOCK)[None, :]
    kpos = idx - BLOCK
    valid = ((kpos[:, None, :] >= 0) & (kpos[:, None, :] < S)
             & (jnp.abs(qpos[:, :, None] - kpos[:, None, :]) <= WINDOW))
    s_band = jnp.einsum('bnqgrd,bnkgd->bngrqk', qb, kb).astype(jnp.float32) * scale
    s_band = jnp.where(valid[None, :, None, None], s_band, -1e30)
    s_ctx = jnp.einsum('bnqgrd,btgd->bngrqt', qb, ck).astype(jnp.float32) * scale
    s_sink = jnp.broadcast_to(sink.reshape(1, 1, G, R, 1, 1).astype(jnp.float32), s_ctx.shape[:-1] + (1,))
    p = jax.nn.softmax(jnp.concatenate([s_ctx, s_band, s_sink], axis=-1), axis=-1).astype(v.dtype)
    o = (jnp.einsum('bngrqt,btgv->bnqgrv', p[..., :T], cv)
         + jnp.einsum('bngrqk,bnkgv->bnqgrv', p[..., T:T + 3 * BLOCK], vb))
    return o.reshape(B, S, H, v.shape[-1])


def mla_kv(ckv, krope, w_kvb):
    B, T, _ = ckv.shape
    kv = (ckv @ w_kvb).reshape(B, T, MLA_HEADS, MLA_NOPE + MLA_V)
    k = jnp.concatenate([kv[..., :MLA_NOPE],
                         jnp.broadcast_to(krope[:, :, None, :], (B, T, MLA_HEADS, MLA_ROPE))], axis=-1)
    return k, kv[..., MLA_NOPE:]


def trunk_layer(x, cvec, lw, l, ctx_cache):
    B, S, _ = x.shape
    sh1, sc1, g1, sh2, sc2, g2 = jnp.split(jax.nn.silu(cvec) @ lw['w_ada'] + lw['b_ada'], 6, axis=-1)
    h = rmsnorm(x, lw['norm1_g']) * (1 + sc1) + sh1
    offs = [int(o) for o in np.cumsum(IN_SPLITS)[:-1]]
    da_q, da_k, da_v, sw_q, sw_k, sw_v, q_a, kv_a = jnp.split(h @ lw['w_in'], offs, axis=-1)
    da_q = da_q.reshape(B, S, DA_HEADS, 2, DA_QK)
    da_k = da_k.reshape(B, S, DA_HEADS, 2 * DA_QK)
    da_v = da_v.reshape(B, S, DA_HEADS, DA_V)
    sw_q = sw_q.reshape(B, S, SW_HEADS, SW_DIM)
    sw_k = sw_k.reshape(B, S, SW_KV_HEADS, SW_DIM)
    sw_v = sw_v.reshape(B, S, SW_KV_HEADS, SW_DIM)
    mla_q = (rmsnorm(q_a, lw['mla_q_norm_g']) @ lw['mla_w_qb']).reshape(B, S, MLA_HEADS, MLA_NOPE + MLA_ROPE)
    ckv = rmsnorm(kv_a[..., :MLA_KV_RANK], lw['mla_kv_norm_g'])
    krope = kv_a[..., MLA_KV_RANK:]
    lam_init = 0.8 - 0.6 * math.exp(-0.3 * l)
    lam = (jnp.exp(jnp.sum(lw['lq1'] * lw['lk1'])) - jnp.exp(jnp.sum(lw['lq2'] * lw['lk2']))).astype(jnp.float32) + lam_init
    if ctx_cache is None:
        state = (da_k, da_v, sw_k, sw_v, ckv, krope)
        dk = da_k.reshape(B, S, DA_HEADS, 2, DA_QK)
        da_o = diff_attention(da_q[..., 0, :], da_q[..., 1, :], dk[..., 0, :], dk[..., 1, :], da_v, lam)
        sw_o = gqa_attention(sw_q, sw_k, sw_v, lw['swa_sink'])
        mk, mv = mla_kv(ckv, krope, lw['mla_w_kvb'])
        mla_o = gqa_attention(mla_q, mk, mv)
    else:
        c_dk, c_dv, c_sk, c_sv, c_ckv, c_kr = ctx_cache
        T = c_dk.shape[1]
        da_q = axial_rope(da_q)
        dk = jnp.concatenate([c_dk.reshape(B, T, DA_HEADS, 2, DA_QK),
                              axial_rope(da_k.reshape(B, S, DA_HEADS, 2, DA_QK))], axis=1)
        dv = jnp.concatenate([c_dv, da_v], axis=1)
        da_o = diff_attention(da_q[..., 0, :], da_q[..., 1, :], dk[..., 0, :], dk[..., 1, :], dv, lam)
        sw_o = swa_latent_attention(axial_rope(sw_q), axial_rope(sw_k), sw_v, c_sk, c_sv, lw['swa_sink'])
        mla_q = jnp.concatenate([mla_q[..., :MLA_NOPE], axial_rope(mla_q[..., MLA_NOPE:])], axis=-1)
        kr_lat = axial_rope(krope[:, :, None, :])[:, :, 0, :]
        mk_c, mv_c = mla_kv(c_ckv, c_kr, lw['mla_w_kvb'])
        mk_l, mv_l = mla_kv(ckv, kr_lat, lw['mla_w_kvb'])
        mla_o = gqa_attention(mla_q, jnp.concatenate([mk_c, mk_l], axis=1), jnp.concatenate([mv_c, mv_l], axis=1))
        state = None
    da_o = rmsnorm(da_o, lw['diff_subln_g']) * (1.0 - lam_init)
    mix = jnp.concatenate([da_o.reshape(B, S, -1), sw_o.reshape(B, S, -1), mla_o.reshape(B, S, -1)], axis=-1)
    x = x + g1 * (mix @ lw['w_out'])
    h2 = rmsnorm(x, lw['norm2_g']) * (1 + sc2) + sh2
    x = x + g2 * (jnp.square(jax.nn.relu(h2 @ lw['w_up'])) @ lw['w_down'])
    return x, state


def setup_inputs(seed: int = 0) -> dict:
    key = jax.random.key(seed)
    ks = iter(jax.random.split(key, 40))

    def nrm(shape, scale=1.0):
        return jax.random.normal(next(ks), shape, jnp.float32) * scale

    def gain(shape):
        return 1.0 + 0.05 * nrm(shape)

    return {
        'x_prompt': nrm((BATCH, SEQ, D_MODEL)),
        'x_sample': nrm((DEC_BATCH, DEC_SEQ, D_MODEL)),
        'c': nrm((DEC_BATCH, D_MODEL)),
        'cache_diff_k': nrm((DEC_BATCH, DEPTH, PAST_LEN, DA_HEADS, 2 * DA_QK)),
        'cache_diff_v': nrm((DEC_BATCH, DEPTH, PAST_LEN, DA_HEADS, DA_V)),
        'cache_swa_k': nrm((DEC_BATCH, DEPTH, PAST_LEN, SW_KV_HEADS, SW_DIM)),
        'cache_swa_v': nrm((DEC_BATCH, DEPTH, PAST_LEN, SW_KV_HEADS, SW_DIM)),
        'cache_mla_ckv': nrm((DEC_BATCH, DEPTH, PAST_LEN, MLA_KV_RANK)),
        'cache_mla_krope': nrm((DEC_BATCH, DEPTH, PAST_LEN, MLA_ROPE)),
        'c_ctx': nrm((D_MODEL,)),
        'norm1_g': gain((DEPTH, D_MODEL)),
        'norm2_g': gain((DEPTH, D_MODEL)),
        'w_ada': nrm((DEPTH, D_MODEL, 6 * D_MODEL), D_MODEL ** -0.5),
        'b_ada': nrm((DEPTH, 6 * D_MODEL), 0.02),
        'w_in': nrm((DEPTH, D_MODEL, P_IN), D_MODEL ** -0.5),
        'w_out': nrm((DEPTH, MIX_WIDTH, D_MODEL), MIX_WIDTH ** -0.5),
        'diff_lambda_q1': nrm((DEPTH, DA_QK), 0.1),
        'diff_lambda_k1': nrm((DEPTH, DA_QK), 0.1),
        'diff_lambda_q2': nrm((DEPTH, DA_QK), 0.1),
        'diff_lambda_k2': nrm((DEPTH, DA_QK), 0.1),
        'diff_subln_g': gain((DEPTH, DA_V)),
        'swa_sink': nrm((DEPTH, SW_HEADS), 0.5),
        'mla_q_norm_g': gain((DEPTH, MLA_Q_RANK)),
        'mla_w_qb': nrm((DEPTH, MLA_Q_RANK, MLA_HEADS * (MLA_NOPE + MLA_ROPE)), MLA_Q_RANK ** -0.5),
        'mla_kv_norm_g': gain((DEPTH, MLA_KV_RANK)),
        'mla_w_kvb': nrm((DEPTH, MLA_KV_RANK, MLA_HEADS * (MLA_NOPE + MLA_V)), MLA_KV_RANK ** -0.5),
        'w_up': nrm((DEPTH, D_MODEL, D_FF), D_MODEL ** -0.5),
        'w_down': nrm((DEPTH, D_FF, D_MODEL), D_FF ** -0.5),
        'final_g': gain((D_MODEL,)),
    }


def reference(x_prompt, x_sample, c, cache_diff_k, cache_diff_v, cache_swa_k, cache_swa_v,
              cache_mla_ckv, cache_mla_krope, c_ctx, norm1_g, norm2_g, w_ada, b_ada, w_in, w_out,
              diff_lambda_q1, diff_lambda_k1, diff_lambda_q2, diff_lambda_k2, diff_subln_g, swa_sink,
              mla_q_norm_g, mla_w_qb, mla_kv_norm_g, mla_w_kvb, w_up, w_down, final_g):
    x_ctx, x_lat = x_prompt, x_sample
    c_ctx_vec = c_ctx[None, None, :]
    c_lat = c[:, None, :]
    new_dk, new_dv, new_sk, new_sv, new_ckv, new_kr = [], [], [], [], [], []
    for l in range(DEPTH):
        lw = {
            'norm1_g': norm1_g[l], 'norm2_g': norm2_g[l], 'w_ada': w_ada[l], 'b_ada': b_ada[l],
            'w_in': w_in[l], 'w_out': w_out[l],
            'lq1': diff_lambda_q1[l], 'lk1': diff_lambda_k1[l], 'lq2': diff_lambda_q2[l], 'lk2': diff_lambda_k2[l],
            'diff_subln_g': diff_subln_g[l], 'swa_sink': swa_sink[l],
            'mla_q_norm_g': mla_q_norm_g[l], 'mla_w_qb': mla_w_qb[l],
            'mla_kv_norm_g': mla_kv_norm_g[l], 'mla_w_kvb': mla_w_kvb[l],
            'w_up': w_up[l], 'w_down': w_down[l],
        }
        x_ctx, st = trunk_layer(x_ctx, c_ctx_vec, lw, l, None)
        new_dk.append(st[0]); new_dv.append(st[1]); new_sk.append(st[2])
        new_sv.append(st[3]); new_ckv.append(st[4]); new_kr.append(st[5])
        cache_l = (cache_diff_k[:, l], cache_diff_v[:, l], cache_swa_k[:, l], cache_swa_v[:, l],
                   cache_mla_ckv[:, l], cache_mla_krope[:, l])
        x_lat, _ = trunk_layer(x_lat, c_lat, lw, l, cache_l)
    y_prompt = rmsnorm(x_ctx, final_g)
    y_sample = rmsnorm(x_lat, final_g)
    new_diff_k = jnp.stack(new_dk, axis=1)
    new_diff_v = jnp.stack(new_dv, axis=1)
    new_swa_k = jnp.stack(new_sk, axis=1)
    new_swa_v = jnp.stack(new_sv, axis=1)
    new_mla_ckv = jnp.stack(new_ckv, axis=1)
    new_mla_krope = jnp.stack(new_kr, axis=1)
    return (y_prompt, y_sample, new_diff_k, new_diff_v, new_swa_k, new_swa_v, new_mla_ckv, new_mla_krope)
```

```python
import math
import numpy as np
import concourse.bass as bass
import concourse.mybir as mybir
from concourse.bass_utils import run_bass_kernel_spmd

F32 = mybir.dt.float32
BF16 = mybir.dt.bfloat16
AF = mybir.ActivationFunctionType
ALU = mybir.AluOpType
AX = mybir.AxisListType

L = 4
D = 1024
NT = 2048
NKEY = 2304
KC = 2306
BIG = 30000.0
EPS = 1e-6
SC_DA = 32 ** -0.5
SC_SW = 0.125
SC_MLA = 96 ** -0.5
NFT = 31
N_LAYERS_RUN = L
SEM_ROTATE = 800
WARM_N = 0
LOOK = 2
INLINE_WAIT = True
COALESCE = 4
STOP_AT = None
DA_PAIRS = (0, 1)
SKIP = set()
DEBUG_DUMP = False


class _Stop(Exception):
    pass


_CKN = {}


def ck(name):
    if STOP_AT is None:
        return
    base, _, n = STOP_AT.partition('#')
    if base == name:
        _CKN[name] = _CKN.get(name, 0) + 1
        if _CKN[name] >= int(n or 1):
            raise _Stop()


class Op:
    __slots__ = ("eng", "fn", "deps", "need", "sem", "val", "dma", "epoch", "idx", "pe_pos")


class Prog:
    ENGS = ("pe", "act", "dve", "pool", "sp")

    def __init__(self):
        self.ops = []
        self.last_w = {}
        self.readers = {}
        self.epoch = 0
        self.all_last = {}
        self.pe_ops = []

    def add(self, eng, fn, r=(), w=(), dma=None):
        op = Op()
        op.eng, op.fn, op.need, op.sem, op.val, op.dma, op.epoch = eng, fn, dma is not None, None, 0, dma, self.epoch
        op.idx = len(self.ops)
        deps = {}
        for k in r:
            o = self.last_w.get(k)
            if o is not None:
                deps[o.idx] = o
        for k in w:
            o = self.last_w.get(k)
            if o is not None:
                deps[o.idx] = o
            for o in self.readers.get(k, ()):
                deps[o.idx] = o
        bar = self.all_last.get("barrier")
        if bar is not None:
            for o in bar:
                deps[o.idx] = o
        best = {}
        dl = []
        for o in deps.values():
            if o is op:
                continue
            if o.dma is not None:
                dl.append(o)
                continue
            if o.eng == "pe" and eng == "pe" and dma is None:
                continue
            if o.eng == "pe" and not o.need and COALESCE:
                pl = self.pe_ops
                j = o.pe_pos + 1
                lim = min(len(pl), j + COALESCE)
                while j < lim:
                    if pl[j].need and pl[j].epoch == o.epoch:
                        o = pl[j]
                        break
                    j += 1
            b = best.get(o.eng)
            if b is None or o.idx > b.idx:
                best[o.eng] = o
        dl += list(best.values())
        for o in dl:
            o.need = True
        op.deps = dl
        for k in w:
            self.last_w[k] = op
            self.readers[k] = []
        for k in r:
            self.readers.setdefault(k, []).append(op)
        self.ops.append(op)
        if eng == "pe" and dma is None:
            op.pe_pos = len(self.pe_ops)
            self.pe_ops.append(op)
        self.all_last[(eng, dma)] = op
        return op

    def barrier(self):
        self.all_last["barrier"] = [o for k, o in self.all_last.items() if k != "barrier"]

    def new_epoch(self):
        self.epoch += 1

    def emit(self, nc, es):
        sems = {}
        counts = {}
        subs = {}
        for op in self.ops:
            if not op.need:
                continue
            if op.dma is None:
                base = (op.eng, op.epoch)
                sub = subs.get(base, 0)
                if counts.get(base + (sub,), 0) >= SEM_ROTATE:
                    sub += 1
                    subs[base] = sub
                key = base + (sub,)
            else:
                key = ("dma_" + op.dma, 0, 0)
            if key not in sems:
                sems[key] = es.enter_context(nc.semaphore("s%d" % len(sems)))
                counts[key] = 0
            counts[key] += 16 if op.dma is not None else 1
            op.sem, op.val = sems[key], counts[key]
        self.counts = counts
        block = es.enter_context(nc.Block())
        by_eng = {e: [o for o in self.ops if o.eng == e] for e in self.ENGS}

        def run(e, ops):
            waited = {}
            for op in ops:
                todo = {}
                for d in op.deps:
                    sid = id(d.sem)
                    if waited.get(sid, 0) >= d.val:
                        continue
                    if sid not in todo or todo[sid].val < d.val:
                        todo[sid] = d
                todo = list(todo.values())
                attach = None
                if INLINE_WAIT and todo and op.dma is None:
                    attach = todo.pop()
                for d in todo:
                    e.wait_ge(d.sem, d.val)
                    waited[id(d.sem)] = d.val
                ins = op.fn(e)
                if attach is not None:
                    ins._wait_ge(attach.sem, attach.val)
                    waited[id(attach.sem)] = attach.val
                if op.need:
                    ins.then_inc(op.sem, 16 if op.dma is not None else 1)

        block.tensor(lambda e: run(e, by_eng["pe"]))
        block.scalar(lambda e: run(e, by_eng["act"]))
        block.vector(lambda e: run(e, by_eng["dve"]))
        block.gpsimd(lambda e: run(e, by_eng["pool"]))
        block.sync(lambda e: run(e, by_eng["sp"]))


def _partner(d):
    m = d // 2
    h = m // 2
    p = np.zeros(d, np.int64)
    sg = np.zeros(d, np.float32)
    for base in (0, m):
        for i in range(h):
            p[base + i] = base + i + h
            sg[base + i] = -1.0
            p[base + i + h] = base + i
            sg[base + i + h] = 1.0
    return p, sg


def _rope_tab(d, latent):
    m = d // 2
    h = m // 2
    _, sg = _partner(d)
    if not latent:
        return np.ones((d, NT), np.float32), np.zeros((d, NT), np.float32)
    freqs = (np.float32(10000.0) ** (-np.arange(h, dtype=np.float32) / np.float32(h))).astype(np.float32)
    t = np.arange(NT)
    rows = (t // 64).astype(np.float32)
    cols = (t % 64).astype(np.float32)
    ang = np.zeros((d, NT), np.float32)
    for i in range(d):
        pos = rows if i < m else cols
        ang[i] = pos * freqs[(i % m) % h]
    return np.cos(ang).astype(np.float32), (np.sin(ang) * sg[:, None]).astype(np.float32)


def _fm(cols):
    K = cols.shape[0]
    return np.ascontiguousarray(cols.reshape(K // 128, 128, cols.shape[1]).transpose(1, 0, 2))


def _prep_shared(inp):
    w_in = inp["w_in"]
    sh = {}
    p32, _ = _partner(32)
    p64, _ = _partner(64)
    WF = np.zeros((L, NFT, 128, 8, 128), np.float32)
    WQB = np.zeros((L, 8, 128, 2, 128), np.float32)
    WKN = np.zeros((L, 4, 128, 128), np.float32)
    WKV = np.zeros((L, 128, 256), np.float32)
    for l in range(L):
        w = w_in[l]
        tiles = []
        for base_off, comp_of in ((256, lambda h, c: 256 + h * 64 + c * 32), (0, lambda h, c: h * 64 + c * 32)):
            for h in range(4):
                A = np.zeros((D, 128), np.float32)
                B = np.zeros((D, 128), np.float32)
                for c in range(2):
                    o = comp_of(h, c)
                    A[:, c * 64:c * 64 + 32] = w[:, o:o + 32]
                    B[:, c * 64:c * 64 + 32] = w[:, o + p32]
                tiles += [A, B]
        A = np.zeros((D, 128), np.float32)
        B = np.zeros((D, 128), np.float32)
        for g in range(2):
            o = 1280 + g * 64
            A[:, g * 64:(g + 1) * 64] = w[:, o:o + 64]
            B[:, g * 64:(g + 1) * 64] = w[:, o + p64]
        tiles += [A, B]
        for pr in range(4):
            A = np.zeros((D, 128), np.float32)
            B = np.zeros((D, 128), np.float32)
            for j in range(2):
                o = 768 + (2 * pr + j) * 64
                A[:, j * 64:(j + 1) * 64] = w[:, o:o + 64]
                B[:, j * 64:(j + 1) * 64] = w[:, o + p64]
            tiles += [A, B]
        tiles += [w[:, 1536:1664], w[:, 1664:1792]]
        A = np.zeros((D, 128), np.float32)
        B = np.zeros((D, 128), np.float32)
        A[:, 32:64] = w[:, 1920:1952]
        B[:, 32:64] = w[:, 1920 + p32]
        tiles += [A, B]
        tiles += [w[:, 1792:1920]]
        assert len(tiles) == NFT
        for i, t in enumerate(tiles):
            WF[l, i] = _fm(np.ascontiguousarray(t))
        wqb = inp["mla_w_qb"][l]
        for h in range(4):
            A = np.zeros((256, 128), np.float32)
            B = np.zeros((256, 128), np.float32)
            A[:, 32:64] = wqb[:, h * 96 + 64:h * 96 + 96]
            A[:, 64:128] = wqb[:, h * 96:h * 96 + 64]
            B[:, 32:64] = wqb[:, h * 96 + 64 + p32]
            WQB[l, 2 * h] = _fm(A)
            WQB[l, 2 * h + 1] = _fm(B)
        wkvb = inp["mla_w_kvb"][l]
        for h in range(4):
            WKN[l, h, :, 64:128] = wkvb[:, h * 128:h * 128 + 64]
            WKV[l, :, h * 64:(h + 1) * 64] = wkvb[:, h * 128 + 64:h * 128 + 128]
    sh["WF"] = WF
    sh["WQB"] = WQB
    sh["WKN"] = WKN
    sh["WKV"] = WKV
    WSD = np.zeros((L, 2, 128, 8, 256), np.float32)
    WSS = np.zeros((L, 2, 128, 8, 128), np.float32)
    WSM = np.zeros((L, 128, 8, 160), np.float32)
    for l in range(L):
        w = w_in[l]
        for pr in range(2):
            c = np.concatenate([w[:, 256 + pr * 128:256 + pr * 128 + 128], w[:, 512 + pr * 128:512 + pr * 128 + 128]], 1)
            WSD[l, pr] = c.reshape(8, 128, 256).transpose(1, 0, 2)
        for g in range(2):
            c = np.concatenate([w[:, 1280 + g * 64:1280 + g * 64 + 64], w[:, 1408 + g * 64:1408 + g * 64 + 64]], 1)
            WSS[l, g] = c.reshape(8, 128, 128).transpose(1, 0, 2)
        WSM[l] = w[:, 1792:1952].reshape(8, 128, 160).transpose(1, 0, 2)
    sh["WSD"], sh["WSS"], sh["WSM"] = WSD, WSS, WSM
    sh["WOUT"] = np.ascontiguousarray(inp["w_out"].reshape(L, 8, 128, D).transpose(0, 2, 1, 3))
    sh["WUP"] = np.ascontiguousarray(inp["w_up"].reshape(L, 8, 128, 8, 512).transpose(0, 3, 2, 1, 4))
    sh["WDN"] = np.ascontiguousarray(inp["w_down"].reshape(L, 32, 128, 8, 128).transpose(0, 3, 2, 1, 4))
    sh["WADA"] = np.ascontiguousarray(inp["w_ada"].reshape(L, 8, 128, 12, 512).transpose(0, 3, 2, 1, 4))
    def fmv(v):
        return v.reshape(-1, 128).T

    cols = [fmv(inp["norm1_g"][l]) for l in range(L)] + [fmv(inp["norm2_g"][l]) for l in range(L)]
    cols += [fmv(inp["b_ada"][l]) for l in range(L)]
    cols += [fmv(inp["final_g"])]
    cols += [fmv(inp["mla_q_norm_g"][l]) for l in range(L)]
    cols += [np.tile(inp["diff_subln_g"][l], 2)[:, None] for l in range(L)]
    cols += [inp["mla_kv_norm_g"][l][:, None] for l in range(L)]
    sh["VEC"] = np.ascontiguousarray(np.concatenate(cols, 1).astype(np.float32))
    assert sh["VEC"].shape == (128, VEC_W)
    sh["GKV"] = np.ascontiguousarray(inp["mla_kv_norm_g"].reshape(1, L * 128).astype(np.float32))
    lam = np.concatenate([inp["diff_lambda_q1"], inp["diff_lambda_k1"], inp["diff_lambda_q2"], inp["diff_lambda_k2"]], 1)
    sh["LAM"] = np.ascontiguousarray(lam.reshape(1, L * 128).astype(np.float32))
    sink = inp["swa_sink"]
    sh["SINK"] = np.ascontiguousarray(np.repeat(sink.reshape(L, 8, 1), 512, axis=2).astype(np.float32))
    CM = np.zeros((128, 5, 128), np.float32)
    CM[:, 0, :] = np.eye(128)
    CM[:, 1, :] = 1.0
    CM[0:32, 2, 32] = 1.0
    CM[64:96, 2, 96] = 1.0
    CM[0:64, 3, 64] = 1.0
    CM[32:128, 4, 0] = 1.0
    sh["CM"] = CM
    return sh


VEC_N1, VEC_N2, VEC_BADA, VEC_FG, VEC_GQ, VEC_SUB, VEC_W = 0, 32, 64, 256, 264, 272, 280


def _prep_core(inp, core):
    latent = core >= 4
    d = {}
    if latent:
        b = core - 4
        x = inp["x_sample"][b]
        cv = inp["c"][b]
    else:
        x = inp["x_prompt"][core * 8:(core + 1) * 8].reshape(NT, D)
        cv = inp["c_ctx"]
    d["XT"] = np.ascontiguousarray(x.T)
    d["CT"] = np.ascontiguousarray(cv.reshape(8, 128).T.astype(np.float32))
    CKD = np.zeros((L, 4, 128, 256), np.float32)
    CVD = np.zeros((L, 2, 2, 128, 192), np.float32)
    CKS = np.zeros((L, 2, 128, 256), np.float32)
    CVS = np.zeros((L, 2, 2, 128, 128), np.float32)
    CCK = np.zeros((L, 128, 256), np.float32)
    CKR = np.zeros((L, 128, 256), np.float32)
    CVD[:, :, :, :, 64:128] = 1.0
    CVS[:, :, :, :, 64:128] = 1.0
    AUG = np.zeros((128, KC), np.float32)
    AUG[32, :] = -1.0
    AUG[96, :] = -1.0
    AUG[97, NKEY] = 8.0
    AUG[98, 0:256] = 1.0
    if not latent:
        t = np.arange(NT)
        for s in range(8):
            AUG[1 + s, 0:NT] = (t // 256 == s)
            AUG[33 + s, 0:256] = -BIG
            AUG[33 + s, 256:NKEY] = np.where(t // 256 == s, 0.0, -BIG)
        AUG[66, :] = -BIG
    if latent:
        b = core - 4
        for l in range(L):
            dk = inp["cache_diff_k"][b, l]
            for h in range(4):
                CKD[l, h, 0:32] = dk[:, h, 0:32].T
                CKD[l, h, 64:96] = dk[:, h, 32:64].T
            dv = inp["cache_diff_v"][b, l]
            for pr in range(2):
                for blk in range(2):
                    CVD[l, pr, blk, :, 0:64] = dv[blk * 128:(blk + 1) * 128, 2 * pr]
                    CVD[l, pr, blk, :, 128:192] = dv[blk * 128:(blk + 1) * 128, 2 * pr + 1]
            sk = inp["cache_swa_k"][b, l]
            sv = inp["cache_swa_v"][b, l]
            for g in range(2):
                CKS[l, g, 0:64] = sk[:, g].T
                for blk in range(2):
                    CVS[l, g, blk, :, 0:64] = sv[blk * 128:(blk + 1) * 128, g]
            CCK[l] = inp["cache_mla_ckv"][b, l].T
            CKR[l, 32:64] = inp["cache_mla_krope"][b, l].T
    for l in range(L):
        for h in range(4):
            CKD[l, h, 32:64] = AUG[32:64, 0:256]
            CKD[l, h, 96:128] = AUG[32:64, 0:256]
        for g in range(2):
            CKS[l, g, 64:96] = AUG[96:128, 0:256]
        CKR[l, 0:32] = AUG[32:64, 0:256]
    d["CKD"], d["CVD"], d["CKS"], d["CVS"], d["CCK"], d["CKR"], d["AUG"] = CKD, CVD, CKS, CVS, CCK, CKR, AUG
    c32, s32 = _rope_tab(32, latent)
    c64, s64 = _rope_tab(64, latent)
    ROPE = np.zeros((3, 128, 2, NT), np.float32)
    ROPE[:, :, 0, :] = 1.0
    for o in (0, 64):
        ROPE[0, o:o + 32, 0] = c32
        ROPE[0, o:o + 32, 1] = s32
        ROPE[1, o:o + 64, 0] = c64
        ROPE[1, o:o + 64, 1] = s64
    ROPE[2, 32:64, 0] = c32
    ROPE[2, 32:64, 1] = s32
    d["ROPE"] = ROPE
    BM = np.zeros((128, 4, 4, 128), np.float32)
    bb = np.arange(128)[:, None]
    aa = np.arange(128)[None, :]
    if latent:
        m_lo = np.where(aa <= bb, 0.0, -BIG)
        m_hi = np.where(bb <= aa, 0.0, -BIG)
        for r in range(4):
            BM[:, 0, r] = m_lo
            BM[:, 1, r] = m_lo
            BM[:, 2, r] = m_hi
            BM[:, 3, r] = m_hi
    else:
        BM[:, 0] = -BIG
        BM[:, 3] = -BIG
    d["BM"] = BM.reshape(128, 4, 512)
    return d


def build_program():
    from contextlib import ExitStack
    nc = bass.Bass("TRN2", target_bir_lowering=False)
    P = Prog()

    def din(name, shape):
        return nc.dram_tensor(name, list(shape), F32, kind="ExternalInput").ap()

    def dout(name, shape):
        return nc.dram_tensor(name, list(shape), F32, kind="ExternalOutput").ap()

    XT = din("XT", [D, NT]); CT = din("CT", [128, 8])
    CKD = din("CKD", [L, 4, 128, 256]); CVD = din("CVD", [L, 2, 2, 128, 192])
    CKS = din("CKS", [L, 2, 128, 256]); CVS = din("CVS", [L, 2, 2, 128, 128])
    CCK = din("CCK", [L, 128, 256]); CKR = din("CKR", [L, 128, 256])
    AUGd = din("AUG", [128, KC]); ROPEd = din("ROPE", [3, 128, 2, NT]); BMd = din("BM", [128, 4, 512])
    WF = din("WF", [L, NFT, 128, 8, 128]); WQB = din("WQB", [L, 8, 128, 2, 128])
    WKN = din("WKN", [L, 4, 128, 128]); WKV = din("WKV", [L, 128, 256])
    WSD = din("WSD", [L, 2, 128, 8, 256]); WSS = din("WSS", [L, 2, 128, 8, 128]); WSM = din("WSM", [L, 128, 8, 160])
    WOUT = din("WOUT", [L, 128, 8, D]); WUP = din("WUP", [L, 8, 128, 8, 512]); WDN = din("WDN", [L, 8, 128, 32, 128])
    WADA = din("WADA", [L, 12, 128, 8, 512])
    VECd = din("VEC", [128, VEC_W]); GKVd = din("GKV", [1, L * 128]); LAMd = din("LAM", [1, L * 128])
    SINKd = din("SINK", [L, 8, 512]); CMd = din("CM", [128, 5, 128])
    YT = dout("YT", [D, NT])
    STD = dout("STD", [L, 2, NT, 256]); STS = dout("STS", [L, 2, NT, 128]); STM = dout("STM", [L, NT, 160])

    off = [18432]

    def sb(name, shape, dt, at=None):
        nbytes = int(np.prod(shape[1:])) * (4 if dt == F32 else 2)
        nbytes = (nbytes + 31) // 32 * 32
        if at is None:
            o = off[0]
            off[0] += nbytes
        else:
            o = at
        assert o + nbytes <= 229376, (name, o, nbytes)
        return nc.alloc_sbuf_tensor_at(name, list(shape), dt, offset=o)

    xT = sb("xT", [128, 8, NT], F32)
    AUG = sb("AUG", [128, KC], BF16)
    BM = sb("BM", [128, 4, 512], BF16)
    CM = sb("CM", [128, 5, 128], BF16)
    VEC = sb("VEC", [128, VEC_W], F32)
    MOD = sb("MOD", [128, L * 48], F32)
    GM1 = sb("GM1", [128, L * 8], F32)
    GM2 = sb("GM2", [128, L * 8], F32)
    GKV = sb("GKV", [128, L * 128], F32)
    NLAM = sb("NLAM", [128, 8], F32)
    KMAX = sb("KMAX", [128, 8], F32)
    KPART = sb("KPART", [128, 16], F32)
    SMALL = sb("SMALL", [128, 16], F32)
    SILC = sb("SILC", [128, 8], BF16)
    region0 = off[0]
    hT = sb("hT", [128, 8, NT], BF16)
    KT = sb("KT", [128, 2, KC], BF16)
    Vb = sb("Vb", [128, 18, 192], BF16)
    QT = sb("QT", [128, 2048], BF16)
    MIX = sb("MIX", [128, 2, 512], BF16)
    ROPEb = sb("ROPEb", [128, 2, 2, 512], F32)
    WR = sb("WR", [128, 4, 8, 128], BF16)
    WO = sb("WO", [128, 2, D], BF16)
    WS = sb("WS", [128, 8, 256], BF16)
    PT = sb("PT", [128, 6, 512], BF16)
    TA = sb("TA", [128, 512], F32)
    TB = sb("TB", [128, 512], F32)
    TC = sb("TC", [128, 512], F32)
    RSTD = sb("RSTD", [128, 512], F32)
    SQ = sb("SQ", [128, 2, 512], BF16)
    RR = sb("RR", [128, 1024], F32)
    STG = sb("STG", [128, 2, 256], F32)
    CKVT = sb("CKVT", [128, KC], BF16)
    QAG = sb("QAG", [128, 2, 512], BF16)
    CSR = sb("CSR", [128, 2, 512], F32)
    WQBb = sb("WQBb", [128, 4, 2, 128], BF16)
    WKNb = sb("WKNb", [128, 2, 128], BF16)
    WKVb = sb("WKVb", [128, 256], BF16)
    att_end = off[0]
    off[0] = region0
    H2 = sb("H2", [128, 8, 1024], BF16)
    AT = sb("AT", [128, 32, 1024], BF16)
    WM = sb("WM", [128, 3, 4096], BF16)
    SQ2 = sb("SQ2", [128, 2, 512], BF16)
    RSTD2 = sb("RSTD2", [128, 512], F32)
    TA2 = sb("TA2", [128, 512], F32)
    TB2 = sb("TB2", [128, 512], F32)
    mlp_end = off[0]
    off[0] = region0
    WAb = sb("WAb", [128, 2, 8, 512], BF16)
    ROW = sb("ROW", [128, 6144], F32)
    LAMb = sb("LAMb", [128, L * 128], F32)
    LAMt = sb("LAMt", [128, L * 128], F32)
    LAMs = sb("LAMs", [128, 16], F32)
    off[0] = max(att_end, off[0], mlp_end)
    print('SBUF layout: region0', region0, 'att_end', att_end, 'mlp_end', mlp_end, 'final', off[0])

    es = ExitStack()
    PS = [es.enter_context(nc.psum_tensor("ps%d" % i, [128, 512], F32)) for i in range(8)]
    rr = {"proj": 0, "S": 0, "O": 0}

    def bank(cls):
        if cls == "proj":
            b = rr["proj"] % 3
        elif cls == "S":
            b = 3 + rr["S"] % 3
        else:
            b = 6 + rr["O"] % 2
        rr[cls] += 1
        return b

    def mm(out, lhsT, rhs, start, stop, r, w):
        P.add("pe", lambda e: e.matmul(out, lhsT, rhs, start=start, stop=stop), r=r, w=w)

    def _cls(k):
        return "".join(ch for ch in str(k) if ch.isalnum())

    def dma_cast(out, in_, r=(), w=()):
        P.add("pool", lambda e: e.dma_start(out=out, in_=in_), r=r, w=w, dma="w" + _cls(w[0]))

    def dma_ld(out, in_, r=(), w=()):
        P.add("sp", lambda e: e.dma_start(out=out, in_=in_), r=r, w=w, dma="l" + _cls(w[0]))

    def dma_st(out, in_, r=(), w=()):
        return P.add("sp", lambda e: e.dma_start(out=out, in_=in_), r=r, w=w, dma="s" + _cls(r[0]))

    def act(out, in_, func, r, w, scale=1.0, bias=0.0):
        P.add("act", lambda e: e.activation(out=out, in_=in_, func=func, bias=bias, scale=scale), r=r, w=w)

    def _e(eng):
        return "dve" if eng == "pool" else eng

    def tt(eng, out, in0, in1, op, r, w):
        eng = _e(eng)
        P.add(eng, lambda e: e.tensor_tensor(out=out, in0=in0, in1=in1, op=op), r=r, w=w)

    def ts(eng, out, in0, s1, s2, op0, op1, r, w):
        eng = _e(eng)
        if s2 is None:
            P.add(eng, lambda e: e.tensor_scalar(out=out, in0=in0, scalar1=s1, scalar2=None, op0=op0), r=r, w=w)
        else:
            P.add(eng, lambda e: e.tensor_scalar(out=out, in0=in0, scalar1=s1, scalar2=s2, op0=op0, op1=op1), r=r, w=w)

    def stt(eng, out, in0, scalar, in1, op0, op1, r, w):
        eng = _e(eng)
        P.add(eng, lambda e: e.scalar_tensor_tensor(out=out, in0=in0, scalar=scalar, in1=in1, op0=op0, op1=op1), r=r, w=w)

    def cp(eng, out, in_, r, w):
        eng = _e(eng)
        if eng == "act":
            P.add("act", lambda e: e.activation(out=out, in_=in_, func=AF.Copy), r=r, w=w)
        else:
            P.add(eng, lambda e: e.tensor_copy(out=out, in_=in_), r=r, w=w)

    def rsqrt_from_sum(out, in_, inv_n, r, w, tmp, tmpkey):
        act(tmp, in_, AF.Ln, r=r, w=[tmpkey], scale=inv_n, bias=EPS)
        act(out, tmp, AF.Exp, r=[tmpkey], w=w, scale=-0.5)

    ident = CM[:, 0, :]
    ones = CM[:, 1, :]
    SELB = {"da": CM[:, 2, :], "sw": CM[:, 3, :], "mla": CM[:, 4, :]}

    ONE1 = sb("ONE1", [128, 8], F32)
    VSK = sb("VSK", [128, 128], BF16)
    TD = sb("TD", [128, 512], F32)
    LNT = sb("LNT", [128, 512], F32)

    def tsl(t):
        return slice(t * 512, (t + 1) * 512)

    def dump(name, ap, keys):
        if not DEBUG_DUMP:
            return
        t = nc.dram_tensor("DBG_" + name, list(ap.shape), ap.dtype, kind="ExternalOutput").ap()
        P.add("sp", lambda e: e.dma_start(out=t, in_=ap), r=keys, dma="dbg" + name)

    dma_ld(xT[:, :, :], XT.rearrange("(c p) t -> p c t", p=128), w=[("x", c, t) for c in range(8) for t in range(4)])
    dma_cast(AUG[:, :], AUGd[:, :], w=["AUG"])
    dma_cast(BM[:, :, :], BMd[:, :, :], w=["BM"])
    dma_cast(CM[:, :, :], CMd[:, :, :], w=["CM"])
    dma_ld(VEC[:, :], VECd[:, :], w=["VEC"])
    dma_ld(GKV[:, :], GKVd[0:1, :].partition_broadcast(128), w=["GKV"])
    dma_ld(LAMb[:, :], LAMd[0:1, :].partition_broadcast(128), w=["LAMb"])
    dma_ld(LAMs[:, 0:8], CT[:, :], w=["CTf"])
    act(SILC[:, :], LAMs[:, 0:8], AF.Silu, r=["CTf"], w=["SILC"])
    P.add("pool", lambda e: e.memset(ONE1[:, :], 1.0), w=["ONE1"])
    P.add("pool", lambda e: e.memset(VSK[:, 0:64], 0.0), w=["VSK"])
    P.add("pool", lambda e: e.memset(VSK[:, 64:128], 1.0), w=["VSK"])
    lb = LAMb[:, :].rearrange("p (l f d) -> p l f d", l=L, f=4)
    lt = LAMt[:, :].rearrange("p (l f d) -> p l f d", l=L, f=4)
    tt("dve", lt[:, :, 0, :], lb[:, :, 0, :], lb[:, :, 1, :], ALU.mult, r=["LAMb"], w=["LAMt0"])
    tt("dve", lt[:, :, 1, :], lb[:, :, 2, :], lb[:, :, 3, :], ALU.mult, r=["LAMb"], w=["LAMt1"])
    P.add("dve", lambda e: e.reduce_sum(out=SMALL[:, 0:4], in_=lt[:, :, 0, :], axis=AX.X), r=["LAMt0"], w=["SM0"])
    P.add("dve", lambda e: e.reduce_sum(out=SMALL[:, 4:8], in_=lt[:, :, 1, :], axis=AX.X), r=["LAMt1"], w=["SM1"])
    act(SMALL[:, 8:16], SMALL[:, 0:8], AF.Exp, r=["SM0", "SM1"], w=["SM2"])
    tt("dve", NLAM[:, 0:4], SMALL[:, 12:16], SMALL[:, 8:12], ALU.subtract, r=["SM2"], w=["NLAM"])
    LAM_INIT = [0.8 - 0.6 * math.exp(-0.3 * l) for l in range(L)]
    for l in range(L):
        ts("dve", NLAM[:, l:l + 1], NLAM[:, l:l + 1], -LAM_INIT[l], None, ALU.add, None, r=["NLAM"], w=["NLAM"])
    for l in range(L):
        for s in range(12):
            slot = (l * 12 + s) % 2
            dma_cast(WAb[:, slot, :, :], WADA[l, s], w=[("WAb", slot)])
            b = bank("proj")
            for c in range(8):
                mm(PS[b][0:1, :], SILC[:, c:c + 1], WAb[:, slot, c, :], c == 0, c == 7, r=[("WAb", slot), "SILC"], w=[("ps", b)])
            cp("act", ROW[0:1, s * 512:(s + 1) * 512], PS[b][0:1, :], r=[("ps", b)], w=["ROW"])
        b = bank("proj")
        for j in range(48):
            def f(e, j=j, b=b):
                return e.matmul(PS[b][:, j:j + 1], ROW[0:1, j * 128:(j + 1) * 128], ONE1[0:1, 0:1], start=True, stop=True)
            P.add("pe", f, r=["ROW", "ONE1"], w=[("ps", b)])
        tt("dve", MOD[:, l * 48:(l + 1) * 48], PS[b][:, 0:48], VEC[:, VEC_BADA + l * 48:VEC_BADA + (l + 1) * 48], ALU.add,
           r=[("ps", b), "VEC"], w=["MOD"])
        for (GM, vo, so) in ((GM1, VEC_N1, 8), (GM2, VEC_N2, 32)):
            stt("dve", GM[:, l * 8:(l + 1) * 8], MOD[:, l * 48 + so:l * 48 + so + 8], 1.0, VEC[:, vo + l * 8:vo + (l + 1) * 8],
                ALU.add, ALU.mult, r=["MOD", "VEC"], w=["GM"])
    P.barrier()
    STOPPED = STOP_AT == 'pro'

    class WStream:
        def __init__(self, slot_ap, nslots, key, look):
            self.slot_ap, self.nslots, self.key, self.look = slot_ap, nslots, key, look
            self.ctr = 0
            self.srcs = []
            self.nxt = 0
            self.base = 0

        def plan(self, srcs):
            self.base = self.ctr
            self.srcs = list(srcs)
            self.nxt = 0

        def get(self, i):
            while self.nxt <= min(i + self.look, len(self.srcs) - 1):
                slot = (self.base + self.nxt) % self.nslots
                dma_cast(self.slot_ap(slot), self.srcs[self.nxt], w=[(self.key, slot)])
                self.nxt += 1
                self.ctr += 1
            slot = (self.base + i) % self.nslots
            return self.slot_ap(slot), (self.key, slot)

    wr = WStream(lambda s: WR[:, s, :, :], 4, "WR", 2)
    wm = WStream(lambda s: WM[:, s, :], 3, "WM", 1)
    cnt = {"sq": 0, "pt": 0, "stg": 0, "rope": 0, "tmp": 0}

    def nxt(k, n):
        v = cnt[k] % n
        cnt[k] += 1
        return v

    def norm_tile(l, t, gm_ap_fn, bias_fn, out_fn, out_key, sqb, rstdb, tmps, tmpkeys):
        b = bank("proj")
        for c in range(8):
            s = nxt("sq", 2)
            tt("dve", sqb[:, s, :], xT[:, c, tsl(t)], xT[:, c, tsl(t)], ALU.mult, r=[("x", c, t)], w=[("SQ", s)])
            mm(PS[b][:, :], ones, sqb[:, s, :], c == 0, c == 7, r=[("SQ", s), "CM"], w=[("ps", b)])
        rsqrt_from_sum(rstdb[:, :], PS[b][:, :], 1.0 / D, r=[("ps", b)], w=["RSTD"], tmp=tmps[0][:, :], tmpkey=tmpkeys[0])
        for c in range(8):
            k = c % 2
            stt("dve", tmps[k][:, :], xT[:, c, tsl(t)], gm_ap_fn(c), rstdb[:, :], ALU.mult, ALU.mult,
                r=[("x", c, t), "RSTD", "GM", "VEC"], w=[tmpkeys[k]])
            out_fn(c, tmps[k], tmpkeys[k])

    def rope_evac(bA, bB, cos, sin, rope_key, outs, rows=slice(0, 128)):
        k = nxt("tmp", 2)
        T0, T1 = (TA, TB) if k == 0 else (TC, TD)
        k0, k1 = ("TA", "TB") if k == 0 else ("TC", "TD")
        tt("dve", T0[rows, :], PS[bA][rows, :], cos[rows, :], ALU.mult, r=[("ps", bA)] + rope_key, w=[k0])
        tt("dve", T1[rows, :], PS[bB][rows, :], sin[rows, :], ALU.mult, r=[("ps", bB)] + rope_key, w=[k1])
        for (o, inr, wk, shp) in outs:
            a0, a1 = T0[inr, :], T1[inr, :]
            if shp is not None:
                a0 = a0.rearrange("p (b q) -> p b q", b=shp)
                a1 = a1.rearrange("p (b q) -> p b q", b=shp)
            tt("pool", o, a0, a1, ALU.add, r=[k0, k1], w=wk)

    def proj2(wA, kA, wB, kB, rhs_fn, rkeys, nk=8, n=512):
        bA = bank("proj")
        bB = bank("proj")
        for c in range(nk):
            mm(PS[bA][:, 0:n], wA[:, c, :], rhs_fn(c), c == 0, c == nk - 1, r=[kA] + rkeys, w=[("ps", bA)])
        for c in range(nk):
            mm(PS[bB][:, 0:n], wB[:, c, :], rhs_fn(c), c == 0, c == nk - 1, r=[kB] + rkeys, w=[("ps", bB)])
        return bA, bB

    def ktkeys(hh, kb):
        return [("KT", hh, "c")] if kb < 2 else [("KT", hh, (kb - 2) // 4)]

    def all_kt(hh):
        return [("KT", hh, s) for s in ("c", 0, 1, 2, 3)]

    def kmax(hh, kind, rows):
        chunks = [(0, 512), (512, 1024), (1024, 1536), (1536, 2048), (2048, 2304)]
        for ci, (a, bb) in enumerate(chunks):
            n = bb - a
            s = nxt("sq", 2)
            kr = 96 if kind == "sw" else 128
            tt("dve", SQ[0:kr, s, 0:n], KT[0:kr, hh, a:bb], KT[0:kr, hh, a:bb], ALU.mult, r=all_kt(hh), w=[("SQ", s)])
            b = bank("proj")
            mm(PS[b][:, 0:n], SELB[kind][0:kr, :], SQ[0:kr, s, 0:n], True, True, r=[("SQ", s), "CM"], w=[("ps", b)])
            for row in rows:
                P.add("dve", (lambda row=row, ci=ci, b=b, n=n: lambda e: e.reduce_max(out=KPART[row:row + 1, ci:ci + 1], in_=PS[b][row:row + 1, 0:n], axis=AX.X))(),
                      r=[("ps", b)], w=["KPART"])
        for row in rows:
            P.add("dve", (lambda row=row: lambda e: e.reduce_max(out=KMAX[row:row + 1, hh:hh + 1], in_=KPART[row:row + 1, 0:5], axis=AX.X))(),
                  r=["KPART"], w=[("KMAX", hh)])

    def bound_rows(qv, qkey, hh, kind, rows, n=512):
        s = nxt("sq", 2)
        kr = 96 if kind == "sw" else 128
        tt("dve", SQ[0:kr, s, 0:n], qv[0:kr, :], qv[0:kr, :], ALU.mult, r=[qkey], w=[("SQ", s)])
        b = bank("proj")
        mm(PS[b][:, 0:n], SELB[kind][0:kr, :], SQ[0:kr, s, 0:n], True, True, r=[("SQ", s), "CM"], w=[("ps", b)])
        for row in rows:
            act(LNT[row:row + 1, 0:n], PS[b][row:row + 1, 0:n], AF.Ln, r=[("ps", b), ("KMAX", hh)], w=["LNT"],
                scale=KMAX[row:row + 1, hh:hh + 1], bias=1e-30)
            act(qv[row:row + 1, :], LNT[row:row + 1, 0:n], AF.Exp, r=["LNT"], w=[qkey], scale=0.5)

    def attend(hh, krows, qv, qkey, vcols, scale, kbs=range(18)):
        bo = bank("O")
        kbs = list(kbs)
        pend = []
        n = len(kbs)
        for i, kb in enumerate(kbs):
            bs = bank("S")
            mm(PS[bs][:, :], KT[krows, hh, kb * 128:(kb + 1) * 128], qv[krows, :], True, True,
               r=ktkeys(hh, kb) + [qkey], w=[("ps", bs)])
            s = nxt("pt", 6)
            act(PT[:, s, :], PS[bs][:, :], AF.Exp, r=[("ps", bs)], w=[("PT", s)], scale=scale)
            pend.append((kb, s, i))
            if len(pend) > LOOK:
                pkb, ps_, pi = pend.pop(0)
                mm(PS[bo][:, :], Vb[:, pkb, vcols], PT[:, ps_, :], pi == 0, False, r=[("PT", ps_), ("Vb", pkb)], w=[("ps", bo)])
        while pend:
            pkb, ps_, pi = pend.pop(0)
            mm(PS[bo][:, :], Vb[:, pkb, vcols], PT[:, ps_, :], pi == 0, pi == n - 1, r=[("PT", ps_), ("Vb", pkb)], w=[("ps", bo)])
        return bo

    s4 = {"i": 0}

    def bank_s4():
        b = (2, 3, 4, 5)[s4["i"] % 4]
        s4["i"] += 1
        return b

    def attend2(streams, scale):
        bos = [bank("O") for _ in streams]
        pend = []
        n = 18

        def flush(last):
            pkb, slots, pi = pend.pop(0)
            for (hh, krows, qv, qkey, vcols), sl, bo in zip(streams, slots, bos):
                mm(PS[bo][:, :], Vb[:, pkb, vcols], PT[:, sl, :], pi == 0, last, r=[("PT", sl), ("Vb", pkb)], w=[("ps", bo)])

        for i in range(n):
            kb = i
            bss = [bank_s4() for _ in streams]
            for (hh, krows, qv, qkey, vcols), bs in zip(streams, bss):
                mm(PS[bs][:, :], KT[krows, hh, kb * 128:(kb + 1) * 128], qv[krows, :], True, True, r=ktkeys(hh, kb) + [qkey], w=[("ps", bs)])
            slots = [nxt("pt", 6) for _ in streams]
            for bs, sl in zip(bss, slots):
                act(PT[:, sl, :], PS[bs][:, :], AF.Exp, r=[("ps", bs)], w=[("PT", sl)], scale=scale)
            pend.append((kb, slots, i))
            if len(pend) > 1:
                flush(False)
        while pend:
            flush(len(pend) == 1)
        return bos

    def attend_da(hh, qv, qkey, vcols, scale):
        return attend2([(hh, slice(0, 64), qv, qkey, vcols), (hh, slice(64, 128), qv, qkey, vcols)], scale)

    def out_proj(l, j, nch, wo_key):
        for c in range(8):
            b = bank("proj")
            for k in range(nch):
                mm(PS[b][:, :], WO[:, k, c * 128:(c + 1) * 128], MIX[:, k, :], k == 0, k == nch - 1,
                   r=[wo_key, ("MIX", k)], w=[("ps", b)])
            stt("dve", xT[:, c, tsl(j)], PS[b][:, :], MOD[:, l * 48 + 16 + c:l * 48 + 17 + c], xT[:, c, tsl(j)], ALU.mult, ALU.add,
                r=[("ps", b), "MOD", ("x", c, j)], w=[("x", c, j)])

    def load_rope(kind, t):
        s = nxt("rope", 2)
        dma_ld(ROPEb[:, s, :, :], ROPEd[kind, :, :, t * 512:(t + 1) * 512], w=[("ROPEb", s)])
        return s

    def state_proj(l, ncol, dst_fn, vcopies):
        for blk in range(16):
            b = bank("proj")
            for c in range(8):
                mm(PS[b][:, 0:ncol], hT[:, c, blk * 128:(blk + 1) * 128], WS[:, c, 0:ncol], c == 0, c == 7,
                   r=[("h", blk // 4), "WS"], w=[("ps", b)])
            s = nxt("stg", 2)
            cp("act", STG[:, s, 0:ncol], PS[b][:, 0:ncol], r=[("ps", b)], w=[("STG", s)])
            dst_fn(blk, s)
            for (dst, src) in vcopies:
                if True:
                    cp("act", Vb[:, 2 + blk, dst], PS[b][:, src], r=[("ps", b)], w=[("Vb", 2 + blk)])
                elif 'vdummy' in SKIP and _CKN.get('_pairs', 0) > 1:
                    cp("dve", RR[:, 0:64], PS[b][:, src], r=[("ps", b)], w=["RR0"])
                else:
                    cp("dve", Vb[:, 2 + blk, dst], PS[b][:, src], r=[("ps", b)], w=[("Vb", 2 + blk)])

    try:
        ck('pro')
        for l in range(N_LAYERS_RUN):
            P.new_epoch()
            P.add("pool", lambda e: e.memset(Vb[:, :, 64:128], 1.0), w=[("Vb", k) for k in range(18)])
            for t in range(4):
                def o1(c, tmp, tk, t=t, l=l):
                    act(hT[:, c, tsl(t)], tmp[:, :], AF.Identity, r=[tk, "MOD"], w=[("h", t)], bias=MOD[:, l * 48 + c:l * 48 + c + 1])
                norm_tile(l, t, lambda c, l=l: GM1[:, l * 8 + c:l * 8 + c + 1], None, o1, None, SQ, RSTD, (TA, TB), ("TA", "TB"))

            ck('n1')
            for pr in DA_PAIRS:
                _sec = _CKN.get('_pairs', 0) > 0
                _CKN['_pairs'] = _CKN.get('_pairs', 0) + 1
                for hh in range(2):
                    if not (_sec and 'ld_kt' in SKIP):
                        dma_cast(KT[:, hh, 0:256], CKD[l, 2 * pr + hh], w=[("KT", hh, "c")])
                for blk in range(2):
                    if not (_sec and 'ld_vb' in SKIP):
                        dma_cast(Vb[:, blk, :], CVD[l, pr, blk], w=[("Vb", blk)])
                if not (_sec and 'ld_ws' in SKIP):
                    dma_cast(WS[:, :, 0:256], WSD[l, pr], w=["WS"])
                if not (_sec and 'ld_wo' in SKIP):
                    dma_cast(WO[:, 0, :], WOUT[l, :, pr, :], w=["WO"])
                ck('da_loads')
                state_proj(l, 256, lambda blk, s, l=l, pr=pr, _sec=_sec: (None if (_sec and 'st' in SKIP) else dma_st(STD[l, pr, blk * 128:(blk + 1) * 128, :], STG[:, s, 0:256], r=[("STG", s)])),
                           [] if (_sec and 'vcp' in SKIP) else [(slice(0, 64), slice(128, 192)), (slice(128, 192), slice(192, 256))])
                ck('da_state')
                srcs = []
                for t in range(4):
                    for hh in range(2):
                        h = 2 * pr + hh
                        srcs += [WF[l, 2 * h], WF[l, 2 * h + 1]]
                wr.plan(srcs)
                wi = 0
                for t in range(4):
                    rs = load_rope(0, t)
                    for hh in range(2):
                        wA, kA = wr.get(wi)
                        wB, kB = wr.get(wi + 1)
                        wi += 2
                        bA, bB = proj2(wA, kA, wB, kB, lambda c, t=t: hT[:, c, tsl(t)], [("h", t)])
                        cols = slice(256 + t * 512, 256 + (t + 1) * 512)
                        rope_evac(bA, bB, ROPEb[:, rs, 0, :], ROPEb[:, rs, 1, :], [("ROPEb", rs)],
                                  [(KT[:, hh, cols], slice(0, 128), [("KT", hh, t)], None)])
                        cp("pool", KT[32:64, hh, cols], AUG[32:64, cols], r=["AUG"], w=[("KT", hh, t)])
                        cp("pool", KT[96:128, hh, cols], AUG[32:64, cols], r=["AUG"], w=[("KT", hh, t)])
                ck('da_k')
                for hh in range(2):
                    kmax(hh, "da", (32, 96))
                ck('da_kmax')
                for j in range(4):
                    srcs = []
                    for hh in range(2):
                        h = 2 * pr + hh
                        srcs += [WF[l, 8 + 2 * h], WF[l, 8 + 2 * h + 1]]
                    wr.plan(srcs)
                    rs = load_rope(0, j)
                    for hh in range(2):
                        wA, kA = wr.get(2 * hh)
                        wB, kB = wr.get(2 * hh + 1)
                        bA, bB = proj2(wA, kA, wB, kB, lambda c, j=j: hT[:, c, tsl(j)], [("h", j)])
                        qv = QT[:, hh * 512:(hh + 1) * 512]
                        rope_evac(bA, bB, ROPEb[:, rs, 0, :], ROPEb[:, rs, 1, :], [("ROPEb", rs)], [(qv, slice(0, 128), [("QT", hh)], None)])
                        cp("pool", qv[32:64, :], AUG[0:32, tsl(j)], r=["AUG"], w=[("QT", hh)])
                        cp("pool", qv[96:128, :], AUG[0:32, tsl(j)], r=["AUG"], w=[("QT", hh)])
                        bound_rows(qv, ("QT", hh), hh, "da", (32, 96))
                    ck('da_q')
                    for hh in range(2):
                        qv = QT[:, hh * 512:(hh + 1) * 512]
                        vcols = slice(0, 128) if hh == 0 else slice(64, 192)
                        n0 = 0 if hh == 0 else 64
                        d0 = 64 - n0
                        nr, dr = slice(n0, n0 + 64), slice(d0, d0 + 64)
                        b1, b2 = attend_da(hh, qv, ("QT", hh), vcols, SC_DA)
                        act(RR[dr, 0:512], PS[b1][dr, :], AF.Ln, r=[("ps", b1)], w=["RR0"]); act(RR[dr, 0:512], RR[dr, 0:512], AF.Exp, r=["RR0"], w=["RR0"], scale=-1.0)
                        act(RR[dr, 512:1024], PS[b2][dr, :], AF.Ln, r=[("ps", b2)], w=["RR1"]); act(RR[dr, 512:1024], RR[dr, 512:1024], AF.Exp, r=["RR1"], w=["RR1"], scale=-1.0)
                        tt("dve", TA[nr, :], PS[b1][nr, :], RR[dr, 0:512], ALU.mult, r=[("ps", b1), "RR0"], w=["TA"])
                        tt("dve", TB[nr, :], PS[b2][nr, :], RR[dr, 512:1024], ALU.mult, r=[("ps", b2), "RR1"], w=["TB"])
                        stt("dve", TA[nr, :], TB[nr, :], NLAM[nr, l:l + 1], TA[nr, :], ALU.mult, ALU.add, r=["TA", "TB", "NLAM"], w=["TA"])
                        s = nxt("sq", 2)
                        tt("pool", SQ[nr, s, :], TA[nr, :], TA[nr, :], ALU.mult, r=["TA"], w=[("SQ", s)])
                        b = bank("proj")
                        mm(PS[b][:, :], ones[nr, :], SQ[nr, s, :], True, True, r=[("SQ", s), "CM"], w=[("ps", b)])
                        rsqrt_from_sum(TC[nr, :], PS[b][nr, :], 1.0 / 64, r=[("ps", b)], w=["TC"], tmp=LNT[nr, :], tmpkey="LNT")
                        tt("dve", TA[nr, :], TA[nr, :], TC[nr, :], ALU.mult, r=["TA", "TC"], w=["TA"])
                        ts("dve", MIX[nr, 0, :], TA[nr, :], VEC[nr, VEC_SUB + l:VEC_SUB + l + 1], 1.0 - LAM_INIT[l], ALU.mult, ALU.mult,
                           r=["TA", "VEC"], w=[("MIX", 0)])
                        ck('da_fin')
                    if l == 0 and pr == DA_PAIRS[0] and j == 0:
                        dump("QT", QT[:, 0:1024], [("QT", 0), ("QT", 1)])
                        dump("KT0", KT[:, 0, 0:1024], all_kt(0))
                        dump("Vb", Vb[:, 0:4, :], [("Vb", k) for k in range(4)])
                        dump("MIX", MIX[:, 0, :], [("MIX", 0)])
                        dump("NLAM", NLAM[:, :], ["NLAM"])
                        dump("KMAX", KMAX[:, :], [("KMAX", 0), ("KMAX", 1)])
                        dump("TA", TA[:, :], ["TA"])
                        dump("TB", TB[:, :], ["TB"])
                        dump("TC", TC[:, :], ["TC"])
                        dump("RR", RR[:, :], ["RR0", "RR1"])
                        dump("MOD", MOD[:, :], ["MOD"])
                    out_proj(l, j, 1, "WO")
                    ck('da_att')
            ck('da')

            QTs = QT[:, :].rearrange("p (b r q) -> p b r q", b=4, r=4)
            for g in range(2):
                dma_cast(KT[:, 0, 0:256], CKS[l, g], w=[("KT", 0, "c")])
                P.add("pool", lambda e: e.memset(KT[:, 0, 2304:2306], 0.0), w=[("KT", 0, "s")])
                cp("pool", KT[64:96, 0, 2304:2305], AUG[96:128, 2304:2305], r=["AUG"], w=[("KT", 0, "s")])
                for blk in range(2):
                    dma_cast(Vb[:, blk, 0:128], CVS[l, g, blk], w=[("Vb", blk)])
                dma_cast(WS[:, :, 0:128], WSS[l, g], w=["WS"])
                dma_cast(WO[:, 0:2, :], WOUT[l, :, 2 + 2 * g:4 + 2 * g, :], w=["WO"])
                state_proj(l, 128, lambda blk, s, l=l, g=g: dma_st(STS[l, g, blk * 128:(blk + 1) * 128, :], STG[:, s, 0:128], r=[("STG", s)]),
                           [(slice(0, 64), slice(64, 128))])
                wr.plan([WF[l, 16], WF[l, 17]] * 4)
                gr = slice(g * 64, g * 64 + 64)
                for t in range(4):
                    rs = load_rope(1, t)
                    wA, kA = wr.get(2 * t)
                    wB, kB = wr.get(2 * t + 1)
                    bA, bB = proj2(wA, kA, wB, kB, lambda c, t=t: hT[:, c, tsl(t)], [("h", t)])
                    cols = slice(256 + t * 512, 256 + (t + 1) * 512)
                    rope_evac(bA, bB, ROPEb[:, rs, 0, :], ROPEb[:, rs, 1, :], [("ROPEb", rs)],
                              [(KT[0:64, 0, cols], gr, [("KT", 0, t)], None)], rows=gr)
                    cp("pool", KT[64:96, 0, cols], AUG[96:128, cols], r=["AUG"], w=[("KT", 0, t)])
                kmax(0, "sw", (64,))
                cp("pool", QT[64:96, :], AUG[64:96, 0:2048], r=["AUG"], w=[("QT", 0), ("QT", 1)])
                for r_ in range(4):
                    dma_cast(QTs[65:66, :, r_, :], SINKd[l, 4 * g + r_:4 * g + r_ + 1, :].rearrange("o (b q) -> o b q", b=4),
                             w=[("QT", 0), ("QT", 1)])
                for j in range(4):
                    srcs = []
                    for p2 in range(2):
                        ti = 18 + 2 * (2 * g + p2)
                        srcs += [WF[l, ti], WF[l, ti + 1]]
                    wr.plan(srcs)
                    rs = load_rope(1, j)
                    for p2 in range(2):
                        wA, kA = wr.get(2 * p2)
                        wB, kB = wr.get(2 * p2 + 1)
                        bA, bB = proj2(wA, kA, wB, kB, lambda c, j=j: hT[:, c, tsl(j)], [("h", j)])
                        outs = []
                        for half in range(2):
                            r_ = 2 * p2 + half
                            outs.append((QTs[0:64, :, r_, :], slice(half * 64, half * 64 + 64), [("QT", 0), ("QT", 1)], 4))
                        rope_evac(bA, bB, ROPEb[:, rs, 0, :], ROPEb[:, rs, 1, :], [("ROPEb", rs)], outs)
                    for blk in range(4):
                        bound_rows(QT[:, blk * 512:(blk + 1) * 512], ("QT", 0), 0, "sw", (64,))
                    for bp in range(2):
                        strs = []
                        for blk in (2 * bp, 2 * bp + 1):
                            i = 4 * j + blk
                            items = [(0, None), (1, None)]
                            if i > 0:
                                items.append((2 + i - 1, 0 + (i % 2)))
                            items.append((2 + i, None))
                            if i < 15:
                                items.append((2 + i + 1, 2 + (i % 2)))
                            strs.append({"blk": blk, "qv": QT[:, blk * 512:(blk + 1) * 512], "items": items, "bo": bank("O"), "pend": [], "first": True})

                        def sw_flush(st):
                            pkb, ps_ = st["pend"].pop(0)
                            mm(PS[st["bo"]][:, :], Vb[:, pkb, 0:128], PT[:, ps_, :], st["first"], False, r=[("PT", ps_), ("Vb", pkb)], w=[("ps", st["bo"])])
                            st["first"] = False
                        nmax = max(len(st["items"]) for st in strs)
                        for step in range(nmax):
                            for st in strs:
                                if step < len(st["items"]):
                                    kb, mi = st["items"][step]
                                    bs = bank_s4()
                                    mm(PS[bs][:, :], KT[0:96, 0, kb * 128:(kb + 1) * 128], st["qv"][0:96, :], True, mi is None,
                                       r=ktkeys(0, kb) + [("QT", 0)], w=[("ps", bs)])
                                    if mi is not None:
                                        mm(PS[bs][:, :], ident, BM[:, mi, :], False, True, r=["CM", "BM"], w=[("ps", bs)])
                                    sl = nxt("pt", 6)
                                    act(PT[:, sl, :], PS[bs][:, :], AF.Exp, r=[("ps", bs)], w=[("PT", sl)], scale=SC_SW)
                                    st["pend"].append((kb, sl))
                            for st in strs:
                                if len(st["pend"]) > 1:
                                    sw_flush(st)
                        for st in strs:
                            while st["pend"]:
                                sw_flush(st)
                            bs = bank_s4()
                            mm(PS[bs][0:1, :], KT[0:96, 0, 2304:2305], st["qv"][0:96, :], True, True, r=[("KT", 0, "s"), ("QT", 0)], w=[("ps", bs)])
                            sl = nxt("pt", 6)
                            act(PT[0:1, sl, :], PS[bs][0:1, :], AF.Exp, r=[("ps", bs)], w=[("PT", sl)], scale=SC_SW)
                            st["sink"] = sl
                        for st in strs:
                            bo, blk, sl = st["bo"], st["blk"], st["sink"]
                            mm(PS[bo][:, :], VSK[0:1, :], PT[0:1, sl, :], False, True, r=[("PT", sl), "VSK"], w=[("ps", bo)])
                            act(RR[64:128, 0:512], PS[bo][64:128, :], AF.Ln, r=[("ps", bo)], w=["RR0"]); act(RR[64:128, 0:512], RR[64:128, 0:512], AF.Exp, r=["RR0"], w=["RR0"], scale=-1.0)
                            for r_ in range(4):
                                ch, rb = r_ // 2, (r_ % 2) * 64
                                tt("dve", MIX[rb:rb + 64, ch, blk * 128:(blk + 1) * 128], PS[bo][0:64, r_ * 128:(r_ + 1) * 128],
                                   RR[64:128, r_ * 128:(r_ + 1) * 128], ALU.mult, r=[("ps", bo), "RR0"], w=[("MIX", ch)])
                    out_proj(l, j, 2, "WO")

            ck('swa')
            dma_cast(CKVT[:, 0:256], CCK[l], w=[("CKVT", "c")])
            dma_cast(WS[:, :, 0:160], WSM[l], w=["WS"])
            dma_cast(WKVb[:, :], WKV[l], w=["WKVb"])

            def mla_state(blk, s, l=l):
                tt("dve", TA[:, 0:128], STG[:, s, 0:128], STG[:, s, 0:128], ALU.mult, r=[("STG", s)], w=["TA"])
                P.add("dve", lambda e: e.reduce_sum(out=SMALL[:, 0:1], in_=TA[:, 0:128], axis=AX.X), r=["TA"], w=["SM0"])
                act(SMALL[:, 1:2], SMALL[:, 0:1], AF.Ln, r=["SM0"], w=["SM1"], scale=1.0 / 128, bias=EPS)
                act(SMALL[:, 2:3], SMALL[:, 1:2], AF.Exp, r=["SM1"], w=["SM2"], scale=-0.5)
                stt("dve", TB[:, 0:128], STG[:, s, 0:128], SMALL[:, 2:3], GKV[:, l * 128:(l + 1) * 128], ALU.mult, ALU.mult,
                    r=[("STG", s), "SM2", "GKV"], w=["TB"])
                dma_st(STM[l, blk * 128:(blk + 1) * 128, 0:128], TB[:, 0:128], r=["TB"])
                dma_st(STM[l, blk * 128:(blk + 1) * 128, 128:160], STG[:, s, 128:160], r=[("STG", s)])
            state_proj(l, 160, mla_state, [])
            wr.plan([WF[l, 30]] * 4)
            for t in range(4):
                wA, kA = wr.get(t)
                b = bank("proj")
                for c in range(8):
                    mm(PS[b][:, :], wA[:, c, :], hT[:, c, tsl(t)], c == 0, c == 7, r=[kA, ("h", t)], w=[("ps", b)])
                cp("act", TA[:, :], PS[b][:, :], r=[("ps", b)], w=["TA"])
                s = nxt("sq", 2)
                tt("pool", SQ[:, s, :], TA[:, :], TA[:, :], ALU.mult, r=["TA"], w=[("SQ", s)])
                b2 = bank("proj")
                mm(PS[b2][:, :], ones, SQ[:, s, :], True, True, r=[("SQ", s), "CM"], w=[("ps", b2)])
                rsqrt_from_sum(RSTD[:, :], PS[b2][:, :], 1.0 / 128, r=[("ps", b2)], w=["RSTD"], tmp=LNT[:, :], tmpkey="LNT")
                stt("dve", CKVT[:, 256 + t * 512:256 + (t + 1) * 512], TA[:, :], VEC[:, VEC_W - 4 + l:VEC_W - 3 + l], RSTD[:, :], ALU.mult, ALU.mult,
                    r=["TA", "RSTD", "VEC"], w=[("CKVT", t)])

            def ckeys(kb):
                return [("CKVT", "c")] if kb < 2 else [("CKVT", (kb - 2) // 4)]
            for pr in range(2):
                for hh in range(2):
                    h = 2 * pr + hh
                    if pr == 0:
                        dma_cast(KT[:, hh, 0:256], CKR[l], w=[("KT", hh, "c")])
                    dma_cast(WKNb[:, hh, :], WKN[l, h], w=[("WKNb", hh)])
                    dma_cast(WQBb[:, 2 * hh, :, :], WQB[l, 2 * h], w=[("WQBb", hh)])
                    dma_cast(WQBb[:, 2 * hh + 1, :, :], WQB[l, 2 * h + 1], w=[("WQBb", hh)])
                    if pr == 0:
                        cp("pool", KT[0:32, hh, 256:2304], AUG[32:64, 256:2304], r=["AUG"], w=[("KT", hh, t) for t in range(4)])
                    b = bank("proj")
                    mm(PS[b][:, 0:256], WKNb[:, hh, :], CKVT[:, 0:256], True, True, r=[("WKNb", hh), ("CKVT", "c")], w=[("ps", b)])
                    cp("act", KT[64:128, hh, 0:256], PS[b][64:128, 0:256], r=[("ps", b)], w=[("KT", hh, "c")])
                    for t in range(4):
                        b = bank("proj")
                        cols = slice(256 + t * 512, 256 + (t + 1) * 512)
                        mm(PS[b][:, :], WKNb[:, hh, :], CKVT[:, cols], True, True, r=[("WKNb", hh), ("CKVT", t)], w=[("ps", b)])
                        cp("act", KT[64:128, hh, cols], PS[b][64:128, :], r=[("ps", b)], w=[("KT", hh, t)])
                dma_cast(WO[:, 0, :], WOUT[l, :, 6 + pr, :], w=["WO"])
                for kb in range(18):
                    b = bank("proj")
                    mm(PS[b][:, 0:128], CKVT[:, kb * 128:(kb + 1) * 128], WKVb[:, pr * 128:(pr + 1) * 128], True, True,
                       r=ckeys(kb) + ["WKVb"], w=[("ps", b)])
                    cp("act", Vb[:, kb, 0:64], PS[b][:, 0:64], r=[("ps", b)], w=[("Vb", kb)])
                    cp("act", Vb[:, kb, 128:192], PS[b][:, 64:128], r=[("ps", b)], w=[("Vb", kb)])
                wr.plan([WF[l, 28], WF[l, 29]] * 4)
                for t in (range(4) if pr == 0 else []):
                    rs = load_rope(2, t)
                    wA, kA = wr.get(2 * t)
                    wB, kB = wr.get(2 * t + 1)
                    bA, bB = proj2(wA, kA, wB, kB, lambda c, t=t: hT[:, c, tsl(t)], [("h", t)])
                    cols = slice(256 + t * 512, 256 + (t + 1) * 512)
                    rope_evac(bA, bB, ROPEb[:, rs, 0, :], ROPEb[:, rs, 1, :], [("ROPEb", rs)],
                              [(KT[32:64, hh, cols], slice(32, 64), [("KT", hh, t)], None) for hh in range(2)], rows=slice(32, 64))
                for hh in range(2):
                    kmax(hh, "mla", (0,))
                for j in range(4):
                    wr.plan([WF[l, 26], WF[l, 27]])
                    rs = load_rope(2, j)
                    bq = bank("proj")
                    for jj in range(2):
                        wA, kA = wr.get(jj)
                        b = bank("proj")
                        if b == bq:
                            b = bank("proj")
                        for c in range(8):
                            mm(PS[b][:, :], wA[:, c, :], hT[:, c, tsl(j)], c == 0, c == 7, r=[kA, ("h", j)], w=[("ps", b)])
                        Tq, tk = (TA, "TA") if jj == 0 else (TB, "TB")
                        cp("act", Tq[:, :], PS[b][:, :], r=[("ps", b)], w=[tk])
                        tt("pool", SQ[:, jj, :], Tq[:, :], Tq[:, :], ALU.mult, r=[tk], w=[("SQ", jj)])
                        mm(PS[bq][:, :], ones, SQ[:, jj, :], jj == 0, jj == 1, r=[("SQ", jj), "CM"], w=[("ps", bq)])
                        ts("dve", QAG[:, jj, :], Tq[:, :], VEC[:, VEC_GQ + l * 2 + jj:VEC_GQ + l * 2 + jj + 1], None, ALU.mult, None,
                           r=[tk, "VEC"], w=[("QAG", jj)])
                    rsqrt_from_sum(RSTD[:, :], PS[bq][:, :], 1.0 / 256, r=[("ps", bq)], w=["RSTD"], tmp=LNT[:, :], tmpkey="LNT")
                    tt("pool", CSR[:, 0, :], ROPEb[:, rs, 0, :], RSTD[:, :], ALU.mult, r=[("ROPEb", rs), "RSTD"], w=["CSR"])
                    tt("pool", CSR[:, 1, :], ROPEb[:, rs, 1, :], RSTD[:, :], ALU.mult, r=[("ROPEb", rs), "RSTD"], w=["CSR"])
                    for hh in range(2):
                        bA, bB = proj2(WQBb[:, 2 * hh, :, :], ("WQBb", hh), WQBb[:, 2 * hh + 1, :, :], ("WQBb", hh),
                                       lambda c: QAG[:, c, :], [("QAG", 0), ("QAG", 1)], nk=2)
                        qv = QT[:, hh * 512:(hh + 1) * 512]
                        rope_evac(bA, bB, CSR[:, 0, :], CSR[:, 1, :], ["CSR"], [(qv, slice(0, 128), [("QT", hh)], None)])
                        cp("pool", qv[0:32, :], AUG[0:32, tsl(j)], r=["AUG"], w=[("QT", hh)])
                        bound_rows(qv, ("QT", hh), hh, "mla", (0,))
                    mla_bos = attend2([(hh, slice(0, 128), QT[:, hh * 512:(hh + 1) * 512], ("QT", hh), slice(0, 128) if hh == 0 else slice(64, 192)) for hh in range(2)], SC_MLA)
                    for hh in range(2):
                        n0 = 0 if hh == 0 else 64
                        d0 = 64 - n0
                        nr, dr = slice(n0, n0 + 64), slice(d0, d0 + 64)
                        bo = mla_bos[hh]
                        act(RR[dr, 0:512], PS[bo][dr, :], AF.Ln, r=[("ps", bo)], w=["RR0"]); act(RR[dr, 0:512], RR[dr, 0:512], AF.Exp, r=["RR0"], w=["RR0"], scale=-1.0)
                        tt("dve", MIX[nr, 0, :], PS[bo][nr, :], RR[dr, 0:512], ALU.mult, r=[("ps", bo), "RR0"], w=[("MIX", 0)])
                    if l == 0 and pr == 1 and j == 0:
                        dump("mQT", QT[:, 0:1024], [("QT", 0), ("QT", 1)])
                        dump("mKT0", KT[:, 0, 0:1024], all_kt(0))
                        dump("mVb", Vb[:, 0:4, :], [("Vb", k) for k in range(4)])
                        dump("mMIX", MIX[:, 0, :], [("MIX", 0)])
                        dump("mKMAX", KMAX[:, :], [("KMAX", 0), ("KMAX", 1)])
                        dump("mCSR", CSR[:, :, :], ["CSR"])
                        dump("mQAG", QAG[:, :, :], [("QAG", 0), ("QAG", 1)])
                        dump("mRSTD", RSTD[:, :], ["RSTD"])
                        dump("mCKVT", CKVT[:, 0:1024], [("CKVT", "c"), ("CKVT", 0), ("CKVT", 1)])
                        dump("mRR", RR[:, :], ["RR0"])
                    out_proj(l, j, 1, "WO")
                    if l == 0 and pr == 1 and j == 0:
                        ck('mla_att')

            ck('mla')
            P.barrier()
            for m in range(2):
                for sub in range(2):
                    t = 2 * m + sub
                    def o2(c, tmp, tk, sub=sub, l=l):
                        act(H2[:, c, sub * 512:(sub + 1) * 512], tmp[:, :], AF.Identity, r=[tk, "MOD"], w=[("H2", sub)],
                            bias=MOD[:, l * 48 + 24 + c:l * 48 + 25 + c])
                    norm_tile(l, t, lambda c, l=l: GM2[:, l * 8 + c:l * 8 + c + 1], None, o2, None, SQ2, RSTD2, (TA2, TB2), ("TA2", "TB2"))
                wm.plan([WUP[l, s].rearrange("p c n -> p (c n)") for s in range(8)] + [WDN[l, c].rearrange("p f n -> p (f n)") for c in range(8)])
                for s in range(8):
                    wv, wk = wm.get(s)
                    wv = wv.rearrange("p (c n) -> p c n", c=8)
                    for fi in range(4):
                        f = 4 * s + fi
                        bb = [bank("proj"), bank("proj")]
                        for c in range(8):
                            for half in range(2):
                                mm(PS[bb[half]][:, :], wv[:, c, fi * 128:(fi + 1) * 128], H2[:, c, half * 512:(half + 1) * 512], c == 0, c == 7,
                                   r=[wk, ("H2", half)], w=[("ps", bb[half])])
                        for half in range(2):
                            Tq, tk = (TA2, "TA2") if half == 0 else (TB2, "TB2")
                            act(Tq[:, :], PS[bb[half]][:, :], AF.Relu, r=[("ps", bb[half])], w=[tk])
                            tt("dve" if half == 0 else "pool", AT[:, f, half * 512:(half + 1) * 512], Tq[:, :], Tq[:, :], ALU.mult, r=[tk], w=[("AT", f)])
                for c in range(8):
                    wv, wk = wm.get(8 + c)
                    wv = wv.rearrange("p (f n) -> p f n", f=32)
                    bb = [bank("S"), bank("O")]
                    for f in range(32):
                        for half in range(2):
                            mm(PS[bb[half]][:, :], wv[:, f, :], AT[:, f, half * 512:(half + 1) * 512], f == 0, f == 31,
                               r=[wk, ("AT", f)], w=[("ps", bb[half])])
                    for half in range(2):
                        t = 2 * m + half
                        stt("dve", xT[:, c, tsl(t)], PS[bb[half]][:, :], MOD[:, l * 48 + 40 + c:l * 48 + 41 + c], xT[:, c, tsl(t)], ALU.mult, ALU.add,
                            r=[("ps", bb[half]), "MOD", ("x", c, t)], w=[("x", c, t)])
            P.barrier()
            ck('mlp')

    except _Stop:
        P.barrier()
    P.new_epoch()
    last_st = []
    for t in range(4):
        def o3(c, tmp, tk, t=t):
            last_st.append(dma_st(YT[c * 128:(c + 1) * 128, tsl(t)], tmp[:, :], r=[tk]))
        norm_tile(0, t, lambda c: VEC[:, VEC_FG + c:VEC_FG + c + 1], None, o3, None, SQ2, RSTD2, (TA2, TB2), ("TA2", "TB2"))

    P.barrier()
    P.add("sp", lambda e: e.dma_start(out=ONE1[:, 4:8], in_=CT[:, 0:4]), w=["FIN"], dma="l")
    P.add("pool", lambda e: e.memset(ONE1[:, 0:1], 1.0), r=["FIN"], w=["FIN2"])
    P.emit(nc, es)
    global LAST_COUNTS, LAST_NOPS
    LAST_NOPS = {e: sum(1 for o in P.ops if o.eng == e) for e in P.ENGS}
    LAST_NOPS['waits'] = sum(len(o.deps) for o in P.ops)
    LAST_COUNTS = {k: v for k, v in P.counts.items()}
    es.close()
    return nc


_CACHE = {}


def kernel(**inputs):
    inp = {k: np.asarray(v) for k, v in inputs.items()}
    if "nc" not in _CACHE:
        _CACHE["nc"] = build_program()
    nc = _CACHE["nc"]
    sh = _prep_shared(inp)
    in_maps = []
    for core in range(8):
        d = dict(sh)
        d.update(_prep_core(inp, core))
        in_maps.append({k: np.ascontiguousarray(v, dtype=np.float32) for k, v in d.items()})
    res = run_bass_kernel_spmd(nc, in_maps, core_ids=list(range(8)))
    R = res.results
    y_prompt = np.zeros((32, 256, D), np.float32)
    y_sample = np.zeros((4, NT, D), np.float32)
    ndk = np.zeros((32, L, 256, 4, 64), np.float32)
    ndv = np.zeros((32, L, 256, 4, 64), np.float32)
    nsk = np.zeros((32, L, 256, 2, 64), np.float32)
    nsv = np.zeros((32, L, 256, 2, 64), np.float32)
    nck = np.zeros((32, L, 256, 128), np.float32)
    nkr = np.zeros((32, L, 256, 32), np.float32)
    for core in range(8):
        y = np.asarray(R[core]["YT"]).T
        if core >= 4:
            y_sample[core - 4] = y
            continue
        sl = slice(core * 8, core * 8 + 8)
        y_prompt[sl] = y.reshape(8, 256, D)
        STD_ = np.asarray(R[core]["STD"])
        STS_ = np.asarray(R[core]["STS"])
        STM_ = np.asarray(R[core]["STM"])
        for l in range(L):
            for pr in range(2):
                ndk[sl, l, :, 2 * pr:2 * pr + 2, :] = STD_[l, pr, :, 0:128].reshape(8, 256, 2, 64)
                ndv[sl, l, :, 2 * pr:2 * pr + 2, :] = STD_[l, pr, :, 128:256].reshape(8, 256, 2, 64)
            for g in range(2):
                nsk[sl, l, :, g, :] = STS_[l, g, :, 0:64].reshape(8, 256, 64)
                nsv[sl, l, :, g, :] = STS_[l, g, :, 64:128].reshape(8, 256, 64)
            nck[sl, l] = STM_[l, :, 0:128].reshape(8, 256, 128)
            nkr[sl, l] = STM_[l, :, 128:160].reshape(8, 256, 32)
    return (y_prompt, y_sample, ndk, ndv, nsk, nsv, nck, nkr)
```

```python
import math
import numpy as np
import concourse.bass as bass
import concourse.mybir as mybir
from concourse.bass_utils import run_bass_kernel_spmd

F32 = mybir.dt.float32
BF16 = mybir.dt.bfloat16
AF = mybir.ActivationFunctionType
ALU = mybir.AluOpType
AX = mybir.AxisListType

L = 4
D = 1024
NT = 2048
NKEY = 2304
KC = 2306
BIG = 30000.0
EPS = 1e-6
SC_DA = 32 ** -0.5
SC_SW = 0.125
SC_MLA = 96 ** -0.5
NFT = 31
N_LAYERS_RUN = L
SEM_ROTATE = 800
WARM_N = 0
LOOK = 2
INLINE_WAIT = True
COALESCE = 4
STOP_AT = None
DA_PAIRS = (0, 1)
SKIP = set()
DEBUG_DUMP = False


class _Stop(Exception):
    pass


_CKN = {}


def ck(name):
    if STOP_AT is None:
        return
    base, _, n = STOP_AT.partition('#')
    if base == name:
        _CKN[name] = _CKN.get(name, 0) + 1
        if _CKN[name] >= int(n or 1):
            raise _Stop()


class Op:
    __slots__ = ("eng", "fn", "deps", "need", "sem", "val", "dma", "epoch", "idx", "pe_pos")


class Prog:
    ENGS = ("pe", "act", "dve", "pool", "sp")

    def __init__(self):
        self.ops = []
        self.last_w = {}
        self.readers = {}
        self.epoch = 0
        self.all_last = {}
        self.pe_ops = []

    def add(self, eng, fn, r=(), w=(), dma=None):
        op = Op()
        op.eng, op.fn, op.need, op.sem, op.val, op.dma, op.epoch = eng, fn, dma is not None, None, 0, dma, self.epoch
        op.idx = len(self.ops)
        deps = {}
        for k in r:
            o = self.last_w.get(k)
            if o is not None:
                deps[o.idx] = o
        for k in w:
            o = self.last_w.get(k)
            if o is not None:
                deps[o.idx] = o
            for o in self.readers.get(k, ()):
                deps[o.idx] = o
        bar = self.all_last.get("barrier")
        if bar is not None:
            for o in bar:
                deps[o.idx] = o
        best = {}
        dl = []
        for o in deps.values():
            if o is op:
                continue
            if o.dma is not None:
                dl.append(o)
                continue
            if o.eng == "pe" and eng == "pe" and dma is None:
                continue
            if o.eng == "pe" and not o.need and COALESCE:
                pl = self.pe_ops
                j = o.pe_pos + 1
                lim = min(len(pl), j + COALESCE)
                while j < lim:
                    if pl[j].need and pl[j].epoch == o.epoch:
                        o = pl[j]
                        break
                    j += 1
            b = best.get(o.eng)
            if b is None or o.idx > b.idx:
                best[o.eng] = o
        dl += list(best.values())
        for o in dl:
            o.need = True
        op.deps = dl
        for k in w:
            self.last_w[k] = op
            self.readers[k] = []
        for k in r:
            self.readers.setdefault(k, []).append(op)
        self.ops.append(op)
        if eng == "pe" and dma is None:
            op.pe_pos = len(self.pe_ops)
            self.pe_ops.append(op)
        self.all_last[(eng, dma)] = op
        return op

    def barrier(self):
        self.all_last["barrier"] = [o for k, o in self.all_last.items() if k != "barrier"]

    def new_epoch(self):
        self.epoch += 1

    def emit(self, nc, es):
        sems = {}
        counts = {}
        subs = {}
        for op in self.ops:
            if not op.need:
                continue
            if op.dma is None:
                base = (op.eng, op.epoch)
                sub = subs.get(base, 0)
                if counts.get(base + (sub,), 0) >= SEM_ROTATE:
                    sub += 1
                    subs[base] = sub
                key = base + (sub,)
            else:
                key = ("dma_" + op.dma, 0, 0)
            if key not in sems:
                sems[key] = es.enter_context(nc.semaphore("s%d" % len(sems)))
                counts[key] = 0
            counts[key] += 16 if op.dma is not None else 1
            op.sem, op.val = sems[key], counts[key]
        self.counts = counts
        block = es.enter_context(nc.Block())
        by_eng = {e: [o for o in self.ops if o.eng == e] for e in self.ENGS}

        def run(e, ops):
            waited = {}
            for op in ops:
                todo = {}
                for d in op.deps:
                    sid = id(d.sem)
                    if waited.get(sid, 0) >= d.val:
                        continue
                    if sid not in todo or todo[sid].val < d.val:
                        todo[sid] = d
                todo = list(todo.values())
                attach = None
                if INLINE_WAIT and todo and op.dma is None:
                    attach = todo.pop()
                for d in todo:
                    e.wait_ge(d.sem, d.val)
                    waited[id(d.sem)] = d.val
                ins = op.fn(e)
                if attach is not None:
                    ins._wait_ge(attach.sem, attach.val)
                    waited[id(attach.sem)] = attach.val
                if op.need:
                    ins.then_inc(op.sem, 16 if op.dma is not None else 1)

        block.tensor(lambda e: run(e, by_eng["pe"]))
        block.scalar(lambda e: run(e, by_eng["act"]))
        block.vector(lambda e: run(e, by_eng["dve"]))
        block.gpsimd(lambda e: run(e, by_eng["pool"]))
        block.sync(lambda e: run(e, by_eng["sp"]))


def _partner(d):
    m = d // 2
    h = m // 2
    p = np.zeros(d, np.int64)
    sg = np.zeros(d, np.float32)
    for base in (0, m):
        for i in range(h):
            p[base + i] = base + i + h
            sg[base + i] = -1.0
            p[base + i + h] = base + i
            sg[base + i + h] = 1.0
    return p, sg


def _rope_tab(d, latent):
    m = d // 2
    h = m // 2
    _, sg = _partner(d)
    if not latent:
        return np.ones((d, NT), np.float32), np.zeros((d, NT), np.float32)
    freqs = (np.float32(10000.0) ** (-np.arange(h, dtype=np.float32) / np.float32(h))).astype(np.float32)
    t = np.arange(NT)
    rows = (t // 64).astype(np.float32)
    cols = (t % 64).astype(np.float32)
    ang = np.zeros((d, NT), np.float32)
    for i in range(d):
        pos = rows if i < m else cols
        ang[i] = pos * freqs[(i % m) % h]
    return np.cos(ang).astype(np.float32), (np.sin(ang) * sg[:, None]).astype(np.float32)


def _fm(cols):
    K = cols.shape[0]
    return np.ascontiguousarray(cols.reshape(K // 128, 128, cols.shape[1]).transpose(1, 0, 2))


def _prep_shared(inp):
    w_in = inp["w_in"]
    sh = {}
    p32, _ = _partner(32)
    p64, _ = _partner(64)
    WF = np.zeros((L, NFT, 128, 8, 128), np.float32)
    WQB = np.zeros((L, 8, 128, 2, 128), np.float32)
    WKN = np.zeros((L, 4, 128, 128), np.float32)
    WKV = np.zeros((L, 128, 256), np.float32)
    for l in range(L):
        w = w_in[l]
        tiles = []
        for base_off, comp_of in ((256, lambda h, c: 256 + h * 64 + c * 32), (0, lambda h, c: h * 64 + c * 32)):
            for h in range(4):
                A = np.zeros((D, 128), np.float32)
                B = np.zeros((D, 128), np.float32)
                for c in range(2):
                    o = comp_of(h, c)
                    A[:, c * 64:c * 64 + 32] = w[:, o:o + 32]
                    B[:, c * 64:c * 64 + 32] = w[:, o + p32]
                tiles += [A, B]
        A = np.zeros((D, 128), np.float32)
        B = np.zeros((D, 128), np.float32)
        for g in range(2):
            o = 1280 + g * 64
            A[:, g * 64:(g + 1) * 64] = w[:, o:o + 64]
            B[:, g * 64:(g + 1) * 64] = w[:, o + p64]
        tiles += [A, B]
        for pr in range(4):
            A = np.zeros((D, 128), np.float32)
            B = np.zeros((D, 128), np.float32)
            for j in range(2):
                o = 768 + (2 * pr + j) * 64
                A[:, j * 64:(j + 1) * 64] = w[:, o:o + 64]
                B[:, j * 64:(j + 1) * 64] = w[:, o + p64]
            tiles += [A, B]
        tiles += [w[:, 1536:1664], w[:, 1664:1792]]
        A = np.zeros((D, 128), np.float32)
        B = np.zeros((D, 128), np.float32)
        A[:, 32:64] = w[:, 1920:1952]
        B[:, 32:64] = w[:, 1920 + p32]
        tiles += [A, B]
        tiles += [w[:, 1792:1920]]
        assert len(tiles) == NFT
        for i, t in enumerate(tiles):
            WF[l, i] = _fm(np.ascontiguousarray(t))
        wqb = inp["mla_w_qb"][l]
        for h in range(4):
            A = np.zeros((256, 128), np.float32)
            B = np.zeros((256, 128), np.float32)
            A[:, 32:64] = wqb[:, h * 96 + 64:h * 96 + 96]
            A[:, 64:128] = wqb[:, h * 96:h * 96 + 64]
            B[:, 32:64] = wqb[:, h * 96 + 64 + p32]
            WQB[l, 2 * h] = _fm(A)
            WQB[l, 2 * h + 1] = _fm(B)
        wkvb = inp["mla_w_kvb"][l]
        for h in range(4):
            WKN[l, h, :, 64:128] = wkvb[:, h * 128:h * 128 + 64]
            WKV[l, :, h * 64:(h + 1) * 64] = wkvb[:, h * 128 + 64:h * 128 + 128]
    sh["WF"] = WF
    sh["WQB"] = WQB
    sh["WKN"] = WKN
    sh["WKV"] = WKV
    WSD = np.zeros((L, 2, 128, 8, 256), np.float32)
    WSS = np.zeros((L, 2, 128, 8, 128), np.float32)
    WSM = np.zeros((L, 128, 8, 160), np.float32)
    for l in range(L):
        w = w_in[l]
        for pr in range(2):
            c = np.concatenate([w[:, 256 + pr * 128:256 + pr * 128 + 128], w[:, 512 + pr * 128:512 + pr * 128 + 128]], 1)
            WSD[l, pr] = c.reshape(8, 128, 256).transpose(1, 0, 2)
        for g in range(2):
            c = np.concatenate([w[:, 1280 + g * 64:1280 + g * 64 + 64], w[:, 1408 + g * 64:1408 + g * 64 + 64]], 1)
            WSS[l, g] = c.reshape(8, 128, 128).transpose(1, 0, 2)
        WSM[l] = w[:, 1792:1952].reshape(8, 128, 160).transpose(1, 0, 2)
    sh["WSD"], sh["WSS"], sh["WSM"] = WSD, WSS, WSM
    sh["WOUT"] = np.ascontiguousarray(inp["w_out"].reshape(L, 8, 128, D).transpose(0, 2, 1, 3))
    sh["WUP"] = np.ascontiguousarray(inp["w_up"].reshape(L, 8, 128, 8, 512).transpose(0, 3, 2, 1, 4))
    sh["WDN"] = np.ascontiguousarray(inp["w_down"].reshape(L, 32, 128, 8, 128).transpose(0, 3, 2, 1, 4))
    sh["WADA"] = np.ascontiguousarray(inp["w_ada"].reshape(L, 8, 128, 12, 512).transpose(0, 3, 2, 1, 4))
    def fmv(v):
        return v.reshape(-1, 128).T

    cols = [fmv(inp["norm1_g"][l]) for l in range(L)] + [fmv(inp["norm2_g"][l]) for l in range(L)]
    cols += [fmv(inp["b_ada"][l]) for l in range(L)]
    cols += [fmv(inp["final_g"])]
    cols += [fmv(inp["mla_q_norm_g"][l]) for l in range(L)]
    cols += [np.tile(inp["diff_subln_g"][l], 2)[:, None] for l in range(L)]
    cols += [inp["mla_kv_norm_g"][l][:, None] for l in range(L)]
    sh["VEC"] = np.ascontiguousarray(np.concatenate(cols, 1).astype(np.float32))
    assert sh["VEC"].shape == (128, VEC_W)
    sh["GKV"] = np.ascontiguousarray(inp["mla_kv_norm_g"].reshape(1, L * 128).astype(np.float32))
    lam = np.concatenate([inp["diff_lambda_q1"], inp["diff_lambda_k1"], inp["diff_lambda_q2"], inp["diff_lambda_k2"]], 1)
    sh["LAM"] = np.ascontiguousarray(lam.reshape(1, L * 128).astype(np.float32))
    sink = inp["swa_sink"]
    sh["SINK"] = np.ascontiguousarray(np.repeat(sink.reshape(L, 8, 1), 512, axis=2).astype(np.float32))
    CM = np.zeros((128, 5, 128), np.float32)
    CM[:, 0, :] = np.eye(128)
    CM[:, 1, :] = 1.0
    CM[0:32, 2, 32] = 1.0
    CM[64:96, 2, 96] = 1.0
    CM[0:64, 3, 64] = 1.0
    CM[32:128, 4, 0] = 1.0
    sh["CM"] = CM
    return sh


VEC_N1, VEC_N2, VEC_BADA, VEC_FG, VEC_GQ, VEC_SUB, VEC_W = 0, 32, 64, 256, 264, 272, 280


def _prep_core(inp, core):
    latent = core >= 4
    d = {}
    if latent:
        b = core - 4
        x = inp["x_sample"][b]
        cv = inp["c"][b]
    else:
        x = inp["x_prompt"][core * 8:(core + 1) * 8].reshape(NT, D)
        cv = inp["c_ctx"]
    d["XT"] = np.ascontiguousarray(x.T)
    d["CT"] = np.ascontiguousarray(cv.reshape(8, 128).T.astype(np.float32))
    CKD = np.zeros((L, 4, 128, 256), np.float32)
    CVD = np.zeros((L, 2, 2, 128, 192), np.float32)
    CKS = np.zeros((L, 2, 128, 256), np.float32)
    CVS = np.zeros((L, 2, 2, 128, 128), np.float32)
    CCK = np.zeros((L, 128, 256), np.float32)
    CKR = np.zeros((L, 128, 256), np.float32)
    CVD[:, :, :, :, 64:128] = 1.0
    CVS[:, :, :, :, 64:128] = 1.0
    AUG = np.zeros((128, KC), np.float32)
    AUG[32, :] = -1.0
    AUG[96, :] = -1.0
    AUG[97, NKEY] = 8.0
    AUG[98, 0:256] = 1.0
    if not latent:
        t = np.arange(NT)
        for s in range(8):
            AUG[1 + s, 0:NT] = (t // 256 == s)
            AUG[33 + s, 0:256] = -BIG
            AUG[33 + s, 256:NKEY] = np.where(t // 256 == s, 0.0, -BIG)
        AUG[66, :] = -BIG
    if latent:
        b = core - 4
        for l in range(L):
            dk = inp["cache_diff_k"][b, l]
            for h in range(4):
                CKD[l, h, 0:32] = dk[:, h, 0:32].T
                CKD[l, h, 64:96] = dk[:, h, 32:64].T
            dv = inp["cache_diff_v"][b, l]
            for pr in range(2):
                for blk in range(2):
                    CVD[l, pr, blk, :, 0:64] = dv[blk * 128:(blk + 1) * 128, 2 * pr]
                    CVD[l, pr, blk, :, 128:192] = dv[blk * 128:(blk + 1) * 128, 2 * pr + 1]
            sk = inp["cache_swa_k"][b, l]
            sv = inp["cache_swa_v"][b, l]
            for g in range(2):
                CKS[l, g, 0:64] = sk[:, g].T
                for blk in range(2):
                    CVS[l, g, blk, :, 0:64] = sv[blk * 128:(blk + 1) * 128, g]
            CCK[l] = inp["cache_mla_ckv"][b, l].T
            CKR[l, 32:64] = inp["cache_mla_krope"][b, l].T
    for l in range(L):
        for h in range(4):
            CKD[l, h, 32:64] = AUG[32:64, 0:256]
            CKD[l, h, 96:128] = AUG[32:64, 0:256]
        for g in range(2):
            CKS[l, g, 64:96] = AUG[96:128, 0:256]
        CKR[l, 0:32] = AUG[32:64, 0:256]
    d["CKD"], d["CVD"], d["CKS"], d["CVS"], d["CCK"], d["CKR"], d["AUG"] = CKD, CVD, CKS, CVS, CCK, CKR, AUG
    c32, s32 = _rope_tab(32, latent)
    c64, s64 = _rope_tab(64, latent)
    ROPE = np.zeros((3, 128, 2, NT), np.float32)
    ROPE[:, :, 0, :] = 1.0
    for o in (0, 64):
        ROPE[0, o:o + 32, 0] = c32
        ROPE[0, o:o + 32, 1] = s32
        ROPE[1, o:o + 64, 0] = c64
        ROPE[1, o:o + 64, 1] = s64
    ROPE[2, 32:64, 0] = c32
    ROPE[2, 32:64, 1] = s32
    d["ROPE"] = ROPE
    BM = np.zeros((128, 4, 4, 128), np.float32)
    bb = np.arange(128)[:, None]
    aa = np.arange(128)[None, :]
    if latent:
        m_lo = np.where(aa <= bb, 0.0, -BIG)
        m_hi = np.where(bb <= aa, 0.0, -BIG)
        for r in range(4):
            BM[:, 0, r] = m_lo
            BM[:, 1, r] = m_lo
            BM[:, 2, r] = m_hi
            BM[:, 3, r] = m_hi
    else:
        BM[:, 0] = -BIG
        BM[:, 3] = -BIG
    d["BM"] = BM.reshape(128, 4, 512)
    return d


def build_program():
    from contextlib import ExitStack
    nc = bass.Bass("TRN2", target_bir_lowering=False)
    P = Prog()

    def din(name, shape):
        return nc.dram_tensor(name, list(shape), F32, kind="ExternalInput").ap()

    def dout(name, shape):
        return nc.dram_tensor(name, list(shape), F32, kind="ExternalOutput").ap()

    XT = din("XT", [D, NT]); CT = din("CT", [128, 8])
    CKD = din("CKD", [L, 4, 128, 256]); CVD = din("CVD", [L, 2, 2, 128, 192])
    CKS = din("CKS", [L, 2, 128, 256]); CVS = din("CVS", [L, 2, 2, 128, 128])
    CCK = din("CCK", [L, 128, 256]); CKR = din("CKR", [L, 128, 256])
    AUGd = din("AUG", [128, KC]); ROPEd = din("ROPE", [3, 128, 2, NT]); BMd = din("BM", [128, 4, 512])
    WF = din("WF", [L, NFT, 128, 8, 128]); WQB = din("WQB", [L, 8, 128, 2, 128])
    WKN = din("WKN", [L, 4, 128, 128]); WKV = din("WKV", [L, 128, 256])
    WSD = din("WSD", [L, 2, 128, 8, 256]); WSS = din("WSS", [L, 2, 128, 8, 128]); WSM = din("WSM", [L, 128, 8, 160])
    WOUT = din("WOUT", [L, 128, 8, D]); WUP = din("WUP", [L, 8, 128, 8, 512]); WDN = din("WDN", [L, 8, 128, 32, 128])
    WADA = din("WADA", [L, 12, 128, 8, 512])
    VECd = din("VEC", [128, VEC_W]); GKVd = din("GKV", [1, L * 128]); LAMd = din("LAM", [1, L * 128])
    SINKd = din("SINK", [L, 8, 512]); CMd = din("CM", [128, 5, 128])
    YT = dout("YT", [D, NT])
    STD = dout("STD", [L, 2, NT, 256]); STS = dout("STS", [L, 2, NT, 128]); STM = dout("STM", [L, NT, 160])

    off = [18432]

    def sb(name, shape, dt, at=None):
        nbytes = int(np.prod(shape[1:])) * (4 if dt == F32 else 2)
        nbytes = (nbytes + 31) // 32 * 32
        if at is None:
            o = off[0]
            off[0] += nbytes
        else:
            o = at
        assert o + nbytes <= 229376, (name, o, nbytes)
        return nc.alloc_sbuf_tensor_at(name, list(shape), dt, offset=o)

    xT = sb("xT", [128, 8, NT], F32)
    AUG = sb("AUG", [128, KC], BF16)
    BM = sb("BM", [128, 4, 512], BF16)
    CM = sb("CM", [128, 5, 128], BF16)
    VEC = sb("VEC", [128, VEC_W], F32)
    MOD = sb("MOD", [128, L * 48], F32)
    GM1 = sb("GM1", [128, L * 8], F32)
    GM2 = sb("GM2", [128, L * 8], F32)
    GKV = sb("GKV", [128, L * 128], F32)
    NLAM = sb("NLAM", [128, 8], F32)
    KMAX = sb("KMAX", [128, 8], F32)
    KPART = sb("KPART", [128, 16], F32)
    SMALL = sb("SMALL", [128, 16], F32)
    SILC = sb("SILC", [128, 8], BF16)
    region0 = off[0]
    hT = sb("hT", [128, 8, NT], BF16)
    KT = sb("KT", [128, 2, KC], BF16)
    Vb = sb("Vb", [128, 18, 192], BF16)
    QT = sb("QT", [128, 2048], BF16)
    MIX = sb("MIX", [128, 2, 512], BF16)
    ROPEb = sb("ROPEb", [128, 2, 2, 512], F32)
    WR = sb("WR", [128, 4, 8, 128], BF16)
    WO = sb("WO", [128, 2, D], BF16)
    WS = sb("WS", [128, 8, 256], BF16)
    PT = sb("PT", [128, 6, 512], BF16)
    TA = sb("TA", [128, 512], F32)
    TB = sb("TB", [128, 512], F32)
    TC = sb("TC", [128, 512], F32)
    RSTD = sb("RSTD", [128, 512], F32)
    SQ = sb("SQ", [128, 2, 512], BF16)
    RR = sb("RR", [128, 1024], F32)
    STG = sb("STG", [128, 2, 256], F32)
    CKVT = sb("CKVT", [128, KC], BF16)
    QAG = sb("QAG", [128, 2, 512], BF16)
    CSR = sb("CSR", [128, 2, 512], F32)
    WQBb = sb("WQBb", [128, 4, 2, 128], BF16)
    WKNb = sb("WKNb", [128, 2, 128], BF16)
    WKVb = sb("WKVb", [128, 256], BF16)
    att_end = off[0]
    off[0] = region0
    H2 = sb("H2", [128, 8, 1024], BF16)
    AT = sb("AT", [128, 32, 1024], BF16)
    WM = sb("WM", [128, 3, 4096], BF16)
    SQ2 = sb("SQ2", [128, 2, 512], BF16)
    RSTD2 = sb("RSTD2", [128, 512], F32)
    TA2 = sb("TA2", [128, 512], F32)
    TB2 = sb("TB2", [128, 512], F32)
    mlp_end = off[0]
    off[0] = region0
    WAb = sb("WAb", [128, 4, 8, 512], BF16)
    ROW = sb("ROW", [128, 6144], F32)
    LAMb = sb("LAMb", [128, L * 128], F32)
    LAMt = sb("LAMt", [128, L * 128], F32)
    LAMs = sb("LAMs", [128, 16], F32)
    off[0] = max(att_end, off[0], mlp_end)
    print('SBUF layout: region0', region0, 'att_end', att_end, 'mlp_end', mlp_end, 'final', off[0])

    es = ExitStack()
    PS = [es.enter_context(nc.psum_tensor("ps%d" % i, [128, 512], F32)) for i in range(8)]
    rr = {"proj": 0, "S": 0, "O": 0}

    def bank(cls):
        if cls == "proj":
            b = rr["proj"] % 3
        elif cls == "S":
            b = 3 + rr["S"] % 3
        else:
            b = 6 + rr["O"] % 2
        rr[cls] += 1
        return b

    def mm(out, lhsT, rhs, start, stop, r, w):
        P.add("pe", lambda e: e.matmul(out, lhsT, rhs, start=start, stop=stop), r=r, w=w)

    def _cls(k):
        return "".join(ch for ch in str(k) if ch.isalnum())

    def dma_cast(out, in_, r=(), w=()):
        P.add("pool", lambda e: e.dma_start(out=out, in_=in_), r=r, w=w, dma="w" + _cls(w[0]))

    def dma_ld(out, in_, r=(), w=()):
        P.add("sp", lambda e: e.dma_start(out=out, in_=in_), r=r, w=w, dma="l" + _cls(w[0]))

    def dma_st(out, in_, r=(), w=()):
        return P.add("sp", lambda e: e.dma_start(out=out, in_=in_), r=r, w=w, dma="s" + _cls(r[0]))

    def act(out, in_, func, r, w, scale=1.0, bias=0.0):
        P.add("act", lambda e: e.activation(out=out, in_=in_, func=func, bias=bias, scale=scale), r=r, w=w)

    def _e(eng):
        return "dve" if eng == "pool" else eng

    def tt(eng, out, in0, in1, op, r, w):
        eng = _e(eng)
        P.add(eng, lambda e: e.tensor_tensor(out=out, in0=in0, in1=in1, op=op), r=r, w=w)

    def ts(eng, out, in0, s1, s2, op0, op1, r, w):
        eng = _e(eng)
        if s2 is None:
            P.add(eng, lambda e: e.tensor_scalar(out=out, in0=in0, scalar1=s1, scalar2=None, op0=op0), r=r, w=w)
        else:
            P.add(eng, lambda e: e.tensor_scalar(out=out, in0=in0, scalar1=s1, scalar2=s2, op0=op0, op1=op1), r=r, w=w)

    def stt(eng, out, in0, scalar, in1, op0, op1, r, w):
        eng = _e(eng)
        P.add(eng, lambda e: e.scalar_tensor_tensor(out=out, in0=in0, scalar=scalar, in1=in1, op0=op0, op1=op1), r=r, w=w)

    def cp(eng, out, in_, r, w):
        eng = _e(eng)
        if eng == "act":
            P.add("act", lambda e: e.activation(out=out, in_=in_, func=AF.Copy), r=r, w=w)
        else:
            P.add(eng, lambda e: e.tensor_copy(out=out, in_=in_), r=r, w=w)

    def rsqrt_from_sum(out, in_, inv_n, r, w, tmp, tmpkey):
        act(tmp, in_, AF.Ln, r=r, w=[tmpkey], scale=inv_n, bias=EPS)
        act(out, tmp, AF.Exp, r=[tmpkey], w=w, scale=-0.5)

    ident = CM[:, 0, :]
    ones = CM[:, 1, :]
    SELB = {"da": CM[:, 2, :], "sw": CM[:, 3, :], "mla": CM[:, 4, :]}

    ONE1 = sb("ONE1", [128, 8], F32)
    VSK = sb("VSK", [128, 128], BF16)
    TD = sb("TD", [128, 512], F32)
    LNT = sb("LNT", [128, 512], F32)

    def tsl(t):
        return slice(t * 512, (t + 1) * 512)

    def dump(name, ap, keys):
        if not DEBUG_DUMP:
            return
        t = nc.dram_tensor("DBG_" + name, list(ap.shape), ap.dtype, kind="ExternalOutput").ap()
        P.add("sp", lambda e: e.dma_start(out=t, in_=ap), r=keys, dma="dbg" + name)

    dma_ld(xT[:, :, :], XT.rearrange("(c p) t -> p c t", p=128), w=[("x", c, t) for c in range(8) for t in range(4)])
    dma_cast(AUG[:, :], AUGd[:, :], w=["AUG"])
    dma_cast(BM[:, :, :], BMd[:, :, :], w=["BM"])
    dma_cast(CM[:, :, :], CMd[:, :, :], w=["CM"])
    dma_ld(VEC[:, :], VECd[:, :], w=["VEC"])
    dma_ld(GKV[:, :], GKVd[0:1, :].partition_broadcast(128), w=["GKV"])
    dma_ld(LAMb[:, :], LAMd[0:1, :].partition_broadcast(128), w=["LAMb"])
    dma_ld(LAMs[:, 0:8], CT[:, :], w=["CTf"])
    act(SILC[:, :], LAMs[:, 0:8], AF.Silu, r=["CTf"], w=["SILC"])
    P.add("pool", lambda e: e.memset(ONE1[:, :], 1.0), w=["ONE1"])
    P.add("pool", lambda e: e.memset(VSK[:, 0:64], 0.0), w=["VSK"])
    P.add("pool", lambda e: e.memset(VSK[:, 64:128], 1.0), w=["VSK"])
    lb = LAMb[:, :].rearrange("p (l f d) -> p l f d", l=L, f=4)
    lt = LAMt[:, :].rearrange("p (l f d) -> p l f d", l=L, f=4)
    tt("dve", lt[:, :, 0, :], lb[:, :, 0, :], lb[:, :, 1, :], ALU.mult, r=["LAMb"], w=["LAMt0"])
    tt("dve", lt[:, :, 1, :], lb[:, :, 2, :], lb[:, :, 3, :], ALU.mult, r=["LAMb"], w=["LAMt1"])
    P.add("dve", lambda e: e.reduce_sum(out=SMALL[:, 0:4], in_=lt[:, :, 0, :], axis=AX.X), r=["LAMt0"], w=["SM0"])
    P.add("dve", lambda e: e.reduce_sum(out=SMALL[:, 4:8], in_=lt[:, :, 1, :], axis=AX.X), r=["LAMt1"], w=["SM1"])
    act(SMALL[:, 8:16], SMALL[:, 0:8], AF.Exp, r=["SM0", "SM1"], w=["SM2"])
    tt("dve", NLAM[:, 0:4], SMALL[:, 12:16], SMALL[:, 8:12], ALU.subtract, r=["SM2"], w=["NLAM"])
    LAM_INIT = [0.8 - 0.6 * math.exp(-0.3 * l) for l in range(L)]
    for l in range(L):
        ts("dve", NLAM[:, l:l + 1], NLAM[:, l:l + 1], -LAM_INIT[l], None, ALU.add, None, r=["NLAM"], w=["NLAM"])
    for l in range(L):
        for s in range(12):
            slot = (l * 12 + s) % 4
            dma_cast(WAb[:, slot, :, :], WADA[l, s], w=[("WAb", slot)])
            b = bank("proj")
            for c in range(8):
                mm(PS[b][0:1, :], SILC[:, c:c + 1], WAb[:, slot, c, :], c == 0, c == 7, r=[("WAb", slot), "SILC"], w=[("ps", b)])
            cp("act", ROW[0:1, s * 512:(s + 1) * 512], PS[b][0:1, :], r=[("ps", b)], w=["ROW"])
        b = bank("proj")
        for j in range(48):
            def f(e, j=j, b=b):
                return e.matmul(PS[b][:, j:j + 1], ROW[0:1, j * 128:(j + 1) * 128], ONE1[0:1, 0:1], start=True, stop=True)
            P.add("pe", f, r=["ROW", "ONE1"], w=[("ps", b)])
        tt("dve", MOD[:, l * 48:(l + 1) * 48], PS[b][:, 0:48], VEC[:, VEC_BADA + l * 48:VEC_BADA + (l + 1) * 48], ALU.add,
           r=[("ps", b), "VEC"], w=["MOD"])
        for (GM, vo, so) in ((GM1, VEC_N1, 8), (GM2, VEC_N2, 32)):
            stt("dve", GM[:, l * 8:(l + 1) * 8], MOD[:, l * 48 + so:l * 48 + so + 8], 1.0, VEC[:, vo + l * 8:vo + (l + 1) * 8],
                ALU.add, ALU.mult, r=["MOD", "VEC"], w=["GM"])
    P.barrier()
    STOPPED = STOP_AT == 'pro'

    class WStream:
        def __init__(self, slot_ap, nslots, key, look):
            self.slot_ap, self.nslots, self.key, self.look = slot_ap, nslots, key, look
            self.ctr = 0
            self.srcs = []
            self.nxt = 0
            self.base = 0

        def plan(self, srcs):
            self.base = self.ctr
            self.srcs = list(srcs)
            self.nxt = 0

        def get(self, i):
            while self.nxt <= min(i + self.look, len(self.srcs) - 1):
                slot = (self.base + self.nxt) % self.nslots
                dma_cast(self.slot_ap(slot), self.srcs[self.nxt], w=[(self.key, slot)])
                self.nxt += 1
                self.ctr += 1
            slot = (self.base + i) % self.nslots
            return self.slot_ap(slot), (self.key, slot)

    wr = WStream(lambda s: WR[:, s, :, :], 4, "WR", 2)
    wm = WStream(lambda s: WM[:, s, :], 3, "WM", 1)
    cnt = {"sq": 0, "pt": 0, "stg": 0, "rope": 0, "tmp": 0}

    def nxt(k, n):
        v = cnt[k] % n
        cnt[k] += 1
        return v

    def norm_tile(l, t, gm_ap_fn, bias_fn, out_fn, out_key, sqb, rstdb, tmps, tmpkeys):
        b = bank("proj")
        for c in range(8):
            s = nxt("sq", 2)
            tt("dve", sqb[:, s, :], xT[:, c, tsl(t)], xT[:, c, tsl(t)], ALU.mult, r=[("x", c, t)], w=[("SQ", s)])
            mm(PS[b][:, :], ones, sqb[:, s, :], c == 0, c == 7, r=[("SQ", s), "CM"], w=[("ps", b)])
        rsqrt_from_sum(rstdb[:, :], PS[b][:, :], 1.0 / D, r=[("ps", b)], w=["RSTD"], tmp=tmps[0][:, :], tmpkey=tmpkeys[0])
        for c in range(8):
            k = c % 2
            stt("dve", tmps[k][:, :], xT[:, c, tsl(t)], gm_ap_fn(c), rstdb[:, :], ALU.mult, ALU.mult,
                r=[("x", c, t), "RSTD", "GM", "VEC"], w=[tmpkeys[k]])
            out_fn(c, tmps[k], tmpkeys[k])

    def rope_evac(bA, bB, cos, sin, rope_key, outs, rows=slice(0, 128)):
        k = nxt("tmp", 2)
        T0, T1 = (TA, TB) if k == 0 else (TC, TD)
        k0, k1 = ("TA", "TB") if k == 0 else ("TC", "TD")
        tt("dve", T0[rows, :], PS[bA][rows, :], cos[rows, :], ALU.mult, r=[("ps", bA)] + rope_key, w=[k0])
        tt("dve", T1[rows, :], PS[bB][rows, :], sin[rows, :], ALU.mult, r=[("ps", bB)] + rope_key, w=[k1])
        for (o, inr, wk, shp) in outs:
            a0, a1 = T0[inr, :], T1[inr, :]
            if shp is not None:
                a0 = a0.rearrange("p (b q) -> p b q", b=shp)
                a1 = a1.rearrange("p (b q) -> p b q", b=shp)
            tt("pool", o, a0, a1, ALU.add, r=[k0, k1], w=wk)

    def proj2(wA, kA, wB, kB, rhs_fn, rkeys, nk=8, n=512):
        bA = bank("proj")
        bB = bank("proj")
        for c in range(nk):
            mm(PS[bA][:, 0:n], wA[:, c, :], rhs_fn(c), c == 0, c == nk - 1, r=[kA] + rkeys, w=[("ps", bA)])
        for c in range(nk):
            mm(PS[bB][:, 0:n], wB[:, c, :], rhs_fn(c), c == 0, c == nk - 1, r=[kB] + rkeys, w=[("ps", bB)])
        return bA, bB

    def ktkeys(hh, kb):
        return [("KT", hh, "c")] if kb < 2 else [("KT", hh, (kb - 2) // 4)]

    def all_kt(hh):
        return [("KT", hh, s) for s in ("c", 0, 1, 2, 3)]

    def kmax(hh, kind, rows):
        chunks = [(0, 512), (512, 1024), (1024, 1536), (1536, 2048), (2048, 2304)]
        for ci, (a, bb) in enumerate(chunks):
            n = bb - a
            s = nxt("sq", 2)
            kr = 96 if kind == "sw" else 128
            tt("dve", SQ[0:kr, s, 0:n], KT[0:kr, hh, a:bb], KT[0:kr, hh, a:bb], ALU.mult, r=all_kt(hh), w=[("SQ", s)])
            b = bank("proj")
            mm(PS[b][:, 0:n], SELB[kind][0:kr, :], SQ[0:kr, s, 0:n], True, True, r=[("SQ", s), "CM"], w=[("ps", b)])
            for row in rows:
                P.add("dve", (lambda row=row, ci=ci, b=b, n=n: lambda e: e.reduce_max(out=KPART[row:row + 1, ci:ci + 1], in_=PS[b][row:row + 1, 0:n], axis=AX.X))(),
                      r=[("ps", b)], w=["KPART"])
        for row in rows:
            P.add("dve", (lambda row=row: lambda e: e.reduce_max(out=KMAX[row:row + 1, hh:hh + 1], in_=KPART[row:row + 1, 0:5], axis=AX.X))(),
                  r=["KPART"], w=[("KMAX", hh)])

    def bound_rows(qv, qkey, hh, kind, rows, n=512):
        s = nxt("sq", 2)
        kr = 96 if kind == "sw" else 128
        tt("dve", SQ[0:kr, s, 0:n], qv[0:kr, :], qv[0:kr, :], ALU.mult, r=[qkey], w=[("SQ", s)])
        b = bank("proj")
        mm(PS[b][:, 0:n], SELB[kind][0:kr, :], SQ[0:kr, s, 0:n], True, True, r=[("SQ", s), "CM"], w=[("ps", b)])
        for row in rows:
            act(LNT[row:row + 1, 0:n], PS[b][row:row + 1, 0:n], AF.Ln, r=[("ps", b), ("KMAX", hh)], w=["LNT"],
                scale=KMAX[row:row + 1, hh:hh + 1], bias=1e-30)
            act(qv[row:row + 1, :], LNT[row:row + 1, 0:n], AF.Exp, r=["LNT"], w=[qkey], scale=0.5)

    def attend(hh, krows, qv, qkey, vcols, scale, kbs=range(18)):
        bo = bank("O")
        kbs = list(kbs)
        pend = []
        n = len(kbs)
        for i, kb in enumerate(kbs):
            bs = bank("S")
            mm(PS[bs][:, :], KT[krows, hh, kb * 128:(kb + 1) * 128], qv[krows, :], True, True,
               r=ktkeys(hh, kb) + [qkey], w=[("ps", bs)])
            s = nxt("pt", 6)
            act(PT[:, s, :], PS[bs][:, :], AF.Exp, r=[("ps", bs)], w=[("PT", s)], scale=scale)
            pend.append((kb, s, i))
            if len(pend) > LOOK:
                pkb, ps_, pi = pend.pop(0)
                mm(PS[bo][:, :], Vb[:, pkb, vcols], PT[:, ps_, :], pi == 0, False, r=[("PT", ps_), ("Vb", pkb)], w=[("ps", bo)])
        while pend:
            pkb, ps_, pi = pend.pop(0)
            mm(PS[bo][:, :], Vb[:, pkb, vcols], PT[:, ps_, :], pi == 0, pi == n - 1, r=[("PT", ps_), ("Vb", pkb)], w=[("ps", bo)])
        return bo

    s4 = {"i": 0}

    def bank_s4():
        b = (2, 3, 4, 5)[s4["i"] % 4]
        s4["i"] += 1
        return b

    def attend2(streams, scale):
        bos = [bank("O") for _ in streams]
        pend = []
        n = 18

        def flush(last):
            pkb, slots, pi = pend.pop(0)
            for (hh, krows, qv, qkey, vcols), sl, bo in zip(streams, slots, bos):
                mm(PS[bo][:, :], Vb[:, pkb, vcols], PT[:, sl, :], pi == 0, last, r=[("PT", sl), ("Vb", pkb)], w=[("ps", bo)])

        for i in range(n):
            kb = i
            bss = [bank_s4() for _ in streams]
            for (hh, krows, qv, qkey, vcols), bs in zip(streams, bss):
                mm(PS[bs][:, :], KT[krows, hh, kb * 128:(kb + 1) * 128], qv[krows, :], True, True, r=ktkeys(hh, kb) + [qkey], w=[("ps", bs)])
            slots = [nxt("pt", 6) for _ in streams]
            for bs, sl in zip(bss, slots):
                act(PT[:, sl, :], PS[bs][:, :], AF.Exp, r=[("ps", bs)], w=[("PT", sl)], scale=scale)
            pend.append((kb, slots, i))
            if len(pend) > 1:
                flush(False)
        while pend:
            flush(len(pend) == 1)
        return bos

    def attend_da(hh, qv, qkey, vcols, scale):
        return attend2([(hh, slice(0, 64), qv, qkey, vcols), (hh, slice(64, 128), qv, qkey, vcols)], scale)

    def out_proj(l, j, nch, wo_key):
        for c in range(8):
            b = bank("proj")
            for k in range(nch):
                mm(PS[b][:, :], WO[:, k, c * 128:(c + 1) * 128], MIX[:, k, :], k == 0, k == nch - 1,
                   r=[wo_key, ("MIX", k)], w=[("ps", b)])
            stt("dve", xT[:, c, tsl(j)], PS[b][:, :], MOD[:, l * 48 + 16 + c:l * 48 + 17 + c], xT[:, c, tsl(j)], ALU.mult, ALU.add,
                r=[("ps", b), "MOD", ("x", c, j)], w=[("x", c, j)])

    def load_rope(kind, t):
        s = nxt("rope", 2)
        dma_ld(ROPEb[:, s, :, :], ROPEd[kind, :, :, t * 512:(t + 1) * 512], w=[("ROPEb", s)])
        return s

    def state_proj(l, ncol, dst_fn, vcopies):
        for blk in range(16):
            b = bank("proj")
            for c in range(8):
                mm(PS[b][:, 0:ncol], hT[:, c, blk * 128:(blk + 1) * 128], WS[:, c, 0:ncol], c == 0, c == 7,
                   r=[("h", blk // 4), "WS"], w=[("ps", b)])
            s = nxt("stg", 2)
            cp("act", STG[:, s, 0:ncol], PS[b][:, 0:ncol], r=[("ps", b)], w=[("STG", s)])
            dst_fn(blk, s)
            for (dst, src) in vcopies:
                if True:
                    cp("act", Vb[:, 2 + blk, dst], PS[b][:, src], r=[("ps", b)], w=[("Vb", 2 + blk)])
                elif 'vdummy' in SKIP and _CKN.get('_pairs', 0) > 1:
                    cp("dve", RR[:, 0:64], PS[b][:, src], r=[("ps", b)], w=["RR0"])
                else:
                    cp("dve", Vb[:, 2 + blk, dst], PS[b][:, src], r=[("ps", b)], w=[("Vb", 2 + blk)])

    try:
        ck('pro')
        for l in range(N_LAYERS_RUN):
            P.new_epoch()
            P.add("pool", lambda e: e.memset(Vb[:, :, 64:128], 1.0), w=[("Vb", k) for k in range(18)])
            for t in range(4):
                def o1(c, tmp, tk, t=t, l=l):
                    act(hT[:, c, tsl(t)], tmp[:, :], AF.Identity, r=[tk, "MOD"], w=[("h", t)], bias=MOD[:, l * 48 + c:l * 48 + c + 1])
                norm_tile(l, t, lambda c, l=l: GM1[:, l * 8 + c:l * 8 + c + 1], None, o1, None, SQ, RSTD, (TA, TB), ("TA", "TB"))

            ck('n1')
            for pr in DA_PAIRS:
                _sec = _CKN.get('_pairs', 0) > 0
                _CKN['_pairs'] = _CKN.get('_pairs', 0) + 1
                for hh in range(2):
                    if not (_sec and 'ld_kt' in SKIP):
                        dma_cast(KT[:, hh, 0:256], CKD[l, 2 * pr + hh], w=[("KT", hh, "c")])
                for blk in range(2):
                    if not (_sec and 'ld_vb' in SKIP):
                        dma_cast(Vb[:, blk, :], CVD[l, pr, blk], w=[("Vb", blk)])
                if not (_sec and 'ld_ws' in SKIP):
                    dma_cast(WS[:, :, 0:256], WSD[l, pr], w=["WS"])
                if not (_sec and 'ld_wo' in SKIP):
                    dma_cast(WO[:, 0, :], WOUT[l, :, pr, :], w=["WO"])
                ck('da_loads')
                state_proj(l, 256, lambda blk, s, l=l, pr=pr, _sec=_sec: (None if (_sec and 'st' in SKIP) else dma_st(STD[l, pr, blk * 128:(blk + 1) * 128, :], STG[:, s, 0:256], r=[("STG", s)])),
                           [] if (_sec and 'vcp' in SKIP) else [(slice(0, 64), slice(128, 192)), (slice(128, 192), slice(192, 256))])
                ck('da_state')
                srcs = []
                for t in range(4):
                    for hh in range(2):
                        h = 2 * pr + hh
                        srcs += [WF[l, 2 * h], WF[l, 2 * h + 1]]
                wr.plan(srcs)
                wi = 0
                for t in range(4):
                    rs = load_rope(0, t)
                    for hh in range(2):
                        wA, kA = wr.get(wi)
                        wB, kB = wr.get(wi + 1)
                        wi += 2
                        bA, bB = proj2(wA, kA, wB, kB, lambda c, t=t: hT[:, c, tsl(t)], [("h", t)])
                        cols = slice(256 + t * 512, 256 + (t + 1) * 512)
                        rope_evac(bA, bB, ROPEb[:, rs, 0, :], ROPEb[:, rs, 1, :], [("ROPEb", rs)],
                                  [(KT[:, hh, cols], slice(0, 128), [("KT", hh, t)], None)])
                        cp("pool", KT[32:64, hh, cols], AUG[32:64, cols], r=["AUG"], w=[("KT", hh, t)])
                        cp("pool", KT[96:128, hh, cols], AUG[32:64, cols], r=["AUG"], w=[("KT", hh, t)])
                ck('da_k')
                for hh in range(2):
                    kmax(hh, "da", (32, 96))
                ck('da_kmax')
                for j in range(4):
                    srcs = []
                    for hh in range(2):
                        h = 2 * pr + hh
                        srcs += [WF[l, 8 + 2 * h], WF[l, 8 + 2 * h + 1]]
                    wr.plan(srcs)
                    rs = load_rope(0, j)
                    for hh in range(2):
                        wA, kA = wr.get(2 * hh)
                        wB, kB = wr.get(2 * hh + 1)
                        bA, bB = proj2(wA, kA, wB, kB, lambda c, j=j: hT[:, c, tsl(j)], [("h", j)])
                        qv = QT[:, hh * 512:(hh + 1) * 512]
                        rope_evac(bA, bB, ROPEb[:, rs, 0, :], ROPEb[:, rs, 1, :], [("ROPEb", rs)], [(qv, slice(0, 128), [("QT", hh)], None)])
                        cp("pool", qv[32:64, :], AUG[0:32, tsl(j)], r=["AUG"], w=[("QT", hh)])
                        cp("pool", qv[96:128, :], AUG[0:32, tsl(j)], r=["AUG"], w=[("QT", hh)])
                        bound_rows(qv, ("QT", hh), hh, "da", (32, 96))
                    ck('da_q')
                    for hh in range(2):
                        qv = QT[:, hh * 512:(hh + 1) * 512]
                        vcols = slice(0, 128) if hh == 0 else slice(64, 192)
                        n0 = 0 if hh == 0 else 64
                        d0 = 64 - n0
                        nr, dr = slice(n0, n0 + 64), slice(d0, d0 + 64)
                        b1, b2 = attend_da(hh, qv, ("QT", hh), vcols, SC_DA)
                        act(RR[dr, 0:512], PS[b1][dr, :], AF.Ln, r=[("ps", b1)], w=["RR0"]); act(RR[dr, 0:512], RR[dr, 0:512], AF.Exp, r=["RR0"], w=["RR0"], scale=-1.0)
                        act(RR[dr, 512:1024], PS[b2][dr, :], AF.Ln, r=[("ps", b2)], w=["RR1"]); act(RR[dr, 512:1024], RR[dr, 512:1024], AF.Exp, r=["RR1"], w=["RR1"], scale=-1.0)
                        tt("dve", TA[nr, :], PS[b1][nr, :], RR[dr, 0:512], ALU.mult, r=[("ps", b1), "RR0"], w=["TA"])
                        tt("dve", TB[nr, :], PS[b2][nr, :], RR[dr, 512:1024], ALU.mult, r=[("ps", b2), "RR1"], w=["TB"])
                        stt("dve", TA[nr, :], TB[nr, :], NLAM[nr, l:l + 1], TA[nr, :], ALU.mult, ALU.add, r=["TA", "TB", "NLAM"], w=["TA"])
                        s = nxt("sq", 2)
                        tt("pool", SQ[nr, s, :], TA[nr, :], TA[nr, :], ALU.mult, r=["TA"], w=[("SQ", s)])
                        b = bank("proj")
                        mm(PS[b][:, :], ones[nr, :], SQ[nr, s, :], True, True, r=[("SQ", s), "CM"], w=[("ps", b)])
                        rsqrt_from_sum(TC[nr, :], PS[b][nr, :], 1.0 / 64, r=[("ps", b)], w=["TC"], tmp=LNT[nr, :], tmpkey="LNT")
                        tt("dve", TA[nr, :], TA[nr, :], TC[nr, :], ALU.mult, r=["TA", "TC"], w=["TA"])
                        ts("dve", MIX[nr, 0, :], TA[nr, :], VEC[nr, VEC_SUB + l:VEC_SUB + l + 1], 1.0 - LAM_INIT[l], ALU.mult, ALU.mult,
                           r=["TA", "VEC"], w=[("MIX", 0)])
                        ck('da_fin')
                    if l == 0 and pr == DA_PAIRS[0] and j == 0:
                        dump("QT", QT[:, 0:1024], [("QT", 0), ("QT", 1)])
                        dump("KT0", KT[:, 0, 0:1024], all_kt(0))
                        dump("Vb", Vb[:, 0:4, :], [("Vb", k) for k in range(4)])
                        dump("MIX", MIX[:, 0, :], [("MIX", 0)])
                        dump("NLAM", NLAM[:, :], ["NLAM"])
                        dump("KMAX", KMAX[:, :], [("KMAX", 0), ("KMAX", 1)])
                        dump("TA", TA[:, :], ["TA"])
                        dump("TB", TB[:, :], ["TB"])
                        dump("TC", TC[:, :], ["TC"])
                        dump("RR", RR[:, :], ["RR0", "RR1"])
                        dump("MOD", MOD[:, :], ["MOD"])
                    out_proj(l, j, 1, "WO")
                    ck('da_att')
            ck('da')

            QTs = QT[:, :].rearrange("p (b r q) -> p b r q", b=4, r=4)
            for g in range(2):
                dma_cast(KT[:, 0, 0:256], CKS[l, g], w=[("KT", 0, "c")])
                P.add("pool", lambda e: e.memset(KT[:, 0, 2304:2306], 0.0), w=[("KT", 0, "s")])
                cp("pool", KT[64:96, 0, 2304:2305], AUG[96:128, 2304:2305], r=["AUG"], w=[("KT", 0, "s")])
                for blk in range(2):
                    dma_cast(Vb[:, blk, 0:128], CVS[l, g, blk], w=[("Vb", blk)])
                dma_cast(WS[:, :, 0:128], WSS[l, g], w=["WS"])
                dma_cast(WO[:, 0:2, :], WOUT[l, :, 2 + 2 * g:4 + 2 * g, :], w=["WO"])
                state_proj(l, 128, lambda blk, s, l=l, g=g: dma_st(STS[l, g, blk * 128:(blk + 1) * 128, :], STG[:, s, 0:128], r=[("STG", s)]),
                           [(slice(0, 64), slice(64, 128))])
                wr.plan([WF[l, 16], WF[l, 17]] * 4)
                gr = slice(g * 64, g * 64 + 64)
                for t in range(4):
                    rs = load_rope(1, t)
                    wA, kA = wr.get(2 * t)
                    wB, kB = wr.get(2 * t + 1)
                    bA, bB = proj2(wA, kA, wB, kB, lambda c, t=t: hT[:, c, tsl(t)], [("h", t)])
                    cols = slice(256 + t * 512, 256 + (t + 1) * 512)
                    rope_evac(bA, bB, ROPEb[:, rs, 0, :], ROPEb[:, rs, 1, :], [("ROPEb", rs)],
                              [(KT[0:64, 0, cols], gr, [("KT", 0, t)], None)], rows=gr)
                    cp("pool", KT[64:96, 0, cols], AUG[96:128, cols], r=["AUG"], w=[("KT", 0, t)])
                kmax(0, "sw", (64,))
                cp("pool", QT[64:96, :], AUG[64:96, 0:2048], r=["AUG"], w=[("QT", 0), ("QT", 1)])
                for r_ in range(4):
                    dma_cast(QTs[65:66, :, r_, :], SINKd[l, 4 * g + r_:4 * g + r_ + 1, :].rearrange("o (b q) -> o b q", b=4),
                             w=[("QT", 0), ("QT", 1)])
                for j in range(4):
                    srcs = []
                    for p2 in range(2):
                        ti = 18 + 2 * (2 * g + p2)
                        srcs += [WF[l, ti], WF[l, ti + 1]]
                    wr.plan(srcs)
                    rs = load_rope(1, j)
                    for p2 in range(2):
                        wA, kA = wr.get(2 * p2)
                        wB, kB = wr.get(2 * p2 + 1)
                        bA, bB = proj2(wA, kA, wB, kB, lambda c, j=j: hT[:, c, tsl(j)], [("h", j)])
                        outs = []
                        for half in range(2):
                            r_ = 2 * p2 + half
                            outs.append((QTs[0:64, :, r_, :], slice(half * 64, half * 64 + 64), [("QT", 0), ("QT", 1)], 4))
                        rope_evac(bA, bB, ROPEb[:, rs, 0, :], ROPEb[:, rs, 1, :], [("ROPEb", rs)], outs)
                    for blk in range(4):
                        bound_rows(QT[:, blk * 512:(blk + 1) * 512], ("QT", 0), 0, "sw", (64,))
                    for bp in range(2):
                        strs = []
                        for blk in (2 * bp, 2 * bp + 1):
                            i = 4 * j + blk
                            items = [(0, None), (1, None)]
                            if i > 0:
                                items.append((2 + i - 1, 0 + (i % 2)))
                            items.append((2 + i, None))
                            if i < 15:
                                items.append((2 + i + 1, 2 + (i % 2)))
                            strs.append({"blk": blk, "qv": QT[:, blk * 512:(blk + 1) * 512], "items": items, "bo": bank("O"), "pend": [], "first": True})

                        def sw_flush(st):
                            pkb, ps_ = st["pend"].pop(0)
                            mm(PS[st["bo"]][:, :], Vb[:, pkb, 0:128], PT[:, ps_, :], st["first"], False, r=[("PT", ps_), ("Vb", pkb)], w=[("ps", st["bo"])])
                            st["first"] = False
                        nmax = max(len(st["items"]) for st in strs)
                        for step in range(nmax):
                            for st in strs:
                                if step < len(st["items"]):
                                    kb, mi = st["items"][step]
                                    bs = bank_s4()
                                    mm(PS[bs][:, :], KT[0:96, 0, kb * 128:(kb + 1) * 128], st["qv"][0:96, :], True, mi is None,
                                       r=ktkeys(0, kb) + [("QT", 0)], w=[("ps", bs)])
                                    if mi is not None:
                                        mm(PS[bs][:, :], ident, BM[:, mi, :], False, True, r=["CM", "BM"], w=[("ps", bs)])
                                    sl = nxt("pt", 6)
                                    act(PT[:, sl, :], PS[bs][:, :], AF.Exp, r=[("ps", bs)], w=[("PT", sl)], scale=SC_SW)
                                    st["pend"].append((kb, sl))
                            for st in strs:
                                if len(st["pend"]) > 1:
                                    sw_flush(st)
                        for st in strs:
                            while st["pend"]:
                                sw_flush(st)
                            bs = bank_s4()
                            mm(PS[bs][0:1, :], KT[0:96, 0, 2304:2305], st["qv"][0:96, :], True, True, r=[("KT", 0, "s"), ("QT", 0)], w=[("ps", bs)])
                            sl = nxt("pt", 6)
                            act(PT[0:1, sl, :], PS[bs][0:1, :], AF.Exp, r=[("ps", bs)], w=[("PT", sl)], scale=SC_SW)
                            st["sink"] = sl
                        for st in strs:
                            bo, blk, sl = st["bo"], st["blk"], st["sink"]
                            mm(PS[bo][:, :], VSK[0:1, :], PT[0:1, sl, :], False, True, r=[("PT", sl), "VSK"], w=[("ps", bo)])
                            act(RR[64:128, 0:512], PS[bo][64:128, :], AF.Ln, r=[("ps", bo)], w=["RR0"]); act(RR[64:128, 0:512], RR[64:128, 0:512], AF.Exp, r=["RR0"], w=["RR0"], scale=-1.0)
                            for r_ in range(4):
                                ch, rb = r_ // 2, (r_ % 2) * 64
                                tt("dve", MIX[rb:rb + 64, ch, blk * 128:(blk + 1) * 128], PS[bo][0:64, r_ * 128:(r_ + 1) * 128],
                                   RR[64:128, r_ * 128:(r_ + 1) * 128], ALU.mult, r=[("ps", bo), "RR0"], w=[("MIX", ch)])
                    out_proj(l, j, 2, "WO")

            ck('swa')
            dma_cast(CKVT[:, 0:256], CCK[l], w=[("CKVT", "c")])
            dma_cast(WS[:, :, 0:160], WSM[l], w=["WS"])
            dma_cast(WKVb[:, :], WKV[l], w=["WKVb"])

            def mla_state(blk, s, l=l):
                tt("dve", TA[:, 0:128], STG[:, s, 0:128], STG[:, s, 0:128], ALU.mult, r=[("STG", s)], w=["TA"])
                P.add("dve", lambda e: e.reduce_sum(out=SMALL[:, 0:1], in_=TA[:, 0:128], axis=AX.X), r=["TA"], w=["SM0"])
                act(SMALL[:, 1:2], SMALL[:, 0:1], AF.Ln, r=["SM0"], w=["SM1"], scale=1.0 / 128, bias=EPS)
                act(SMALL[:, 2:3], SMALL[:, 1:2], AF.Exp, r=["SM1"], w=["SM2"], scale=-0.5)
                stt("dve", TB[:, 0:128], STG[:, s, 0:128], SMALL[:, 2:3], GKV[:, l * 128:(l + 1) * 128], ALU.mult, ALU.mult,
                    r=[("STG", s), "SM2", "GKV"], w=["TB"])
                dma_st(STM[l, blk * 128:(blk + 1) * 128, 0:128], TB[:, 0:128], r=["TB"])
                dma_st(STM[l, blk * 128:(blk + 1) * 128, 128:160], STG[:, s, 128:160], r=[("STG", s)])
            state_proj(l, 160, mla_state, [])
            wr.plan([WF[l, 30]] * 4)
            for t in range(4):
                wA, kA = wr.get(t)
                b = bank("proj")
                for c in range(8):
                    mm(PS[b][:, :], wA[:, c, :], hT[:, c, tsl(t)], c == 0, c == 7, r=[kA, ("h", t)], w=[("ps", b)])
                cp("act", TA[:, :], PS[b][:, :], r=[("ps", b)], w=["TA"])
                s = nxt("sq", 2)
                tt("pool", SQ[:, s, :], TA[:, :], TA[:, :], ALU.mult, r=["TA"], w=[("SQ", s)])
                b2 = bank("proj")
                mm(PS[b2][:, :], ones, SQ[:, s, :], True, True, r=[("SQ", s), "CM"], w=[("ps", b2)])
                rsqrt_from_sum(RSTD[:, :], PS[b2][:, :], 1.0 / 128, r=[("ps", b2)], w=["RSTD"], tmp=LNT[:, :], tmpkey="LNT")
                stt("dve", CKVT[:, 256 + t * 512:256 + (t + 1) * 512], TA[:, :], VEC[:, VEC_W - 4 + l:VEC_W - 3 + l], RSTD[:, :], ALU.mult, ALU.mult,
                    r=["TA", "RSTD", "VEC"], w=[("CKVT", t)])

            def ckeys(kb):
                return [("CKVT", "c")] if kb < 2 else [("CKVT", (kb - 2) // 4)]
            for pr in range(2):
                for hh in range(2):
                    h = 2 * pr + hh
                    if pr == 0:
                        dma_cast(KT[:, hh, 0:256], CKR[l], w=[("KT", hh, "c")])
                    dma_cast(WKNb[:, hh, :], WKN[l, h], w=[("WKNb", hh)])
                    dma_cast(WQBb[:, 2 * hh, :, :], WQB[l, 2 * h], w=[("WQBb", hh)])
                    dma_cast(WQBb[:, 2 * hh + 1, :, :], WQB[l, 2 * h + 1], w=[("WQBb", hh)])
                    if pr == 0:
                        cp("pool", KT[0:32, hh, 256:2304], AUG[32:64, 256:2304], r=["AUG"], w=[("KT", hh, t) for t in range(4)])
                    b = bank("proj")
                    mm(PS[b][:, 0:256], WKNb[:, hh, :], CKVT[:, 0:256], True, True, r=[("WKNb", hh), ("CKVT", "c")], w=[("ps", b)])
                    cp("act", KT[64:128, hh, 0:256], PS[b][64:128, 0:256], r=[("ps", b)], w=[("KT", hh, "c")])
                    for t in range(4):
                        b = bank("proj")
                        cols = slice(256 + t * 512, 256 + (t + 1) * 512)
                        mm(PS[b][:, :], WKNb[:, hh, :], CKVT[:, cols], True, True, r=[("WKNb", hh), ("CKVT", t)], w=[("ps", b)])
                        cp("act", KT[64:128, hh, cols], PS[b][64:128, :], r=[("ps", b)], w=[("KT", hh, t)])
                dma_cast(WO[:, 0, :], WOUT[l, :, 6 + pr, :], w=["WO"])
                for kb in range(18):
                    b = bank("proj")
                    mm(PS[b][:, 0:128], CKVT[:, kb * 128:(kb + 1) * 128], WKVb[:, pr * 128:(pr + 1) * 128], True, True,
                       r=ckeys(kb) + ["WKVb"], w=[("ps", b)])
                    cp("act", Vb[:, kb, 0:64], PS[b][:, 0:64], r=[("ps", b)], w=[("Vb", kb)])
                    cp("act", Vb[:, kb, 128:192], PS[b][:, 64:128], r=[("ps", b)], w=[("Vb", kb)])
                wr.plan([WF[l, 28], WF[l, 29]] * 4)
                for t in (range(4) if pr == 0 else []):
                    rs = load_rope(2, t)
                    wA, kA = wr.get(2 * t)
                    wB, kB = wr.get(2 * t + 1)
                    bA, bB = proj2(wA, kA, wB, kB, lambda c, t=t: hT[:, c, tsl(t)], [("h", t)])
                    cols = slice(256 + t * 512, 256 + (t + 1) * 512)
                    rope_evac(bA, bB, ROPEb[:, rs, 0, :], ROPEb[:, rs, 1, :], [("ROPEb", rs)],
                              [(KT[32:64, hh, cols], slice(32, 64), [("KT", hh, t)], None) for hh in range(2)], rows=slice(32, 64))
                for hh in range(2):
                    kmax(hh, "mla", (0,))
                for j in range(4):
                    wr.plan([WF[l, 26], WF[l, 27]])
                    rs = load_rope(2, j)
                    bq = bank("proj")
                    for jj in range(2):
                        wA, kA = wr.get(jj)
                        b = bank("proj")
                        if b == bq:
                            b = bank("proj")
                        for c in range(8):
                            mm(PS[b][:, :], wA[:, c, :], hT[:, c, tsl(j)], c == 0, c == 7, r=[kA, ("h", j)], w=[("ps", b)])
                        Tq, tk = (TA, "TA") if jj == 0 else (TB, "TB")
                        cp("act", Tq[:, :], PS[b][:, :], r=[("ps", b)], w=[tk])
                        tt("pool", SQ[:, jj, :], Tq[:, :], Tq[:, :], ALU.mult, r=[tk], w=[("SQ", jj)])
                        mm(PS[bq][:, :], ones, SQ[:, jj, :], jj == 0, jj == 1, r=[("SQ", jj), "CM"], w=[("ps", bq)])
                        ts("dve", QAG[:, jj, :], Tq[:, :], VEC[:, VEC_GQ + l * 2 + jj:VEC_GQ + l * 2 + jj + 1], None, ALU.mult, None,
                           r=[tk, "VEC"], w=[("QAG", jj)])
                    rsqrt_from_sum(RSTD[:, :], PS[bq][:, :], 1.0 / 256, r=[("ps", bq)], w=["RSTD"], tmp=LNT[:, :], tmpkey="LNT")
                    tt("pool", CSR[:, 0, :], ROPEb[:, rs, 0, :], RSTD[:, :], ALU.mult, r=[("ROPEb", rs), "RSTD"], w=["CSR"])
                    tt("pool", CSR[:, 1, :], ROPEb[:, rs, 1, :], RSTD[:, :], ALU.mult, r=[("ROPEb", rs), "RSTD"], w=["CSR"])
                    for hh in range(2):
                        bA, bB = proj2(WQBb[:, 2 * hh, :, :], ("WQBb", hh), WQBb[:, 2 * hh + 1, :, :], ("WQBb", hh),
                                       lambda c: QAG[:, c, :], [("QAG", 0), ("QAG", 1)], nk=2)
                        qv = QT[:, hh * 512:(hh + 1) * 512]
                        rope_evac(bA, bB, CSR[:, 0, :], CSR[:, 1, :], ["CSR"], [(qv, slice(0, 128), [("QT", hh)], None)])
                        cp("pool", qv[0:32, :], AUG[0:32, tsl(j)], r=["AUG"], w=[("QT", hh)])
                        bound_rows(qv, ("QT", hh), hh, "mla", (0,))
                    mla_bos = attend2([(hh, slice(0, 128), QT[:, hh * 512:(hh + 1) * 512], ("QT", hh), slice(0, 128) if hh == 0 else slice(64, 192)) for hh in range(2)], SC_MLA)
                    for hh in range(2):
                        n0 = 0 if hh == 0 else 64
                        d0 = 64 - n0
                        nr, dr = slice(n0, n0 + 64), slice(d0, d0 + 64)
                        bo = mla_bos[hh]
                        act(RR[dr, 0:512], PS[bo][dr, :], AF.Ln, r=[("ps", bo)], w=["RR0"]); act(RR[dr, 0:512], RR[dr, 0:512], AF.Exp, r=["RR0"], w=["RR0"], scale=-1.0)
                        tt("dve", MIX[nr, 0, :], PS[bo][nr, :], RR[dr, 0:512], ALU.mult, r=[("ps", bo), "RR0"], w=[("MIX", 0)])
                    if l == 0 and pr == 1 and j == 0:
                        dump("mQT", QT[:, 0:1024], [("QT", 0), ("QT", 1)])
                        dump("mKT0", KT[:, 0, 0:1024], all_kt(0))
                        dump("mVb", Vb[:, 0:4, :], [("Vb", k) for k in range(4)])
                        dump("mMIX", MIX[:, 0, :], [("MIX", 0)])
                        dump("mKMAX", KMAX[:, :], [("KMAX", 0), ("KMAX", 1)])
                        dump("mCSR", CSR[:, :, :], ["CSR"])
                        dump("mQAG", QAG[:, :, :], [("QAG", 0), ("QAG", 1)])
                        dump("mRSTD", RSTD[:, :], ["RSTD"])
                        dump("mCKVT", CKVT[:, 0:1024], [("CKVT", "c"), ("CKVT", 0), ("CKVT", 1)])
                        dump("mRR", RR[:, :], ["RR0"])
                    out_proj(l, j, 1, "WO")
                    if l == 0 and pr == 1 and j == 0:
                        ck('mla_att')

            ck('mla')
            P.barrier()
            for m in range(2):
                for sub in range(2):
                    t = 2 * m + sub
                    def o2(c, tmp, tk, sub=sub, l=l):
                        act(H2[:, c, sub * 512:(sub + 1) * 512], tmp[:, :], AF.Identity, r=[tk, "MOD"], w=[("H2", sub)],
                            bias=MOD[:, l * 48 + 24 + c:l * 48 + 25 + c])
                    norm_tile(l, t, lambda c, l=l: GM2[:, l * 8 + c:l * 8 + c + 1], None, o2, None, SQ2, RSTD2, (TA2, TB2), ("TA2", "TB2"))
                wm.plan([WUP[l, s].rearrange("p c n -> p (c n)") for s in range(8)] + [WDN[l, c].rearrange("p f n -> p (f n)") for c in range(8)])
                for s in range(8):
                    wv, wk = wm.get(s)
                    wv = wv.rearrange("p (c n) -> p c n", c=8)
                    for fi in range(4):
                        f = 4 * s + fi
                        bb = [bank("proj"), bank("proj")]
                        for c in range(8):
                            for half in range(2):
                                mm(PS[bb[half]][:, :], wv[:, c, fi * 128:(fi + 1) * 128], H2[:, c, half * 512:(half + 1) * 512], c == 0, c == 7,
                                   r=[wk, ("H2", half)], w=[("ps", bb[half])])
                        for half in range(2):
                            Tq, tk = (TA2, "TA2") if half == 0 else (TB2, "TB2")
                            act(Tq[:, :], PS[bb[half]][:, :], AF.Relu, r=[("ps", bb[half])], w=[tk])
                            tt("dve" if half == 0 else "pool", AT[:, f, half * 512:(half + 1) * 512], Tq[:, :], Tq[:, :], ALU.mult, r=[tk], w=[("AT", f)])
                for c in range(8):
                    wv, wk = wm.get(8 + c)
                    wv = wv.rearrange("p (f n) -> p f n", f=32)
                    bb = [bank("S"), bank("O")]
                    for f in range(32):
                        for half in range(2):
                            mm(PS[bb[half]][:, :], wv[:, f, :], AT[:, f, half * 512:(half + 1) * 512], f == 0, f == 31,
                               r=[wk, ("AT", f)], w=[("ps", bb[half])])
                    for half in range(2):
                        t = 2 * m + half
                        stt("dve", xT[:, c, tsl(t)], PS[bb[half]][:, :], MOD[:, l * 48 + 40 + c:l * 48 + 41 + c], xT[:, c, tsl(t)], ALU.mult, ALU.add,
                            r=[("ps", bb[half]), "MOD", ("x", c, t)], w=[("x", c, t)])
            P.barrier()
            ck('mlp')

    except _Stop:
        P.barrier()
    P.new_epoch()
    last_st = []
    for t in range(4):
        def o3(c, tmp, tk, t=t):
            last_st.append(dma_st(YT[c * 128:(c + 1) * 128, tsl(t)], tmp[:, :], r=[tk]))
        norm_tile(0, t, lambda c: VEC[:, VEC_FG + c:VEC_FG + c + 1], None, o3, None, SQ2, RSTD2, (TA2, TB2), ("TA2", "TB2"))

    P.barrier()
    P.add("sp", lambda e: e.dma_start(out=ONE1[:, 4:8], in_=CT[:, 0:4]), w=["FIN"], dma="l")
    P.add("pool", lambda e: e.memset(ONE1[:, 0:1], 1.0), r=["FIN"], w=["FIN2"])
    P.emit(nc, es)
    global LAST_COUNTS, LAST_NOPS
    LAST_NOPS = {e: sum(1 for o in P.ops if o.eng == e) for e in P.ENGS}
    LAST_NOPS['waits'] = sum(len(o.deps) for o in P.ops)
    LAST_COUNTS = {k: v for k, v in P.counts.items()}
    es.close()
    return nc


_CACHE = {}


def kernel(**inputs):
    inp = {k: np.asarray(v) for k, v in inputs.items()}
    if "nc" not in _CACHE:
        _CACHE["nc"] = build_program()
    nc = _CACHE["nc"]
    sh = _prep_shared(inp)
    in_maps = []
    for core in range(8):
        d = dict(sh)
        d.update(_prep_core(inp, core))
        in_maps.append({k: np.ascontiguousarray(v, dtype=np.float32) for k, v in d.items()})
    res = run_bass_kernel_spmd(nc, in_maps, core_ids=list(range(8)))
    R = res.results
    y_prompt = np.zeros((32, 256, D), np.float32)
    y_sample = np.zeros((4, NT, D), np.float32)
    ndk = np.zeros((32, L, 256, 4, 64), np.float32)
    ndv = np.zeros((32, L, 256, 4, 64), np.float32)
    nsk = np.zeros((32, L, 256, 2, 64), np.float32)
    nsv = np.zeros((32, L, 256, 2, 64), np.float32)
    nck = np.zeros((32, L, 256, 128), np.float32)
    nkr = np.zeros((32, L, 256, 32), np.float32)
    for core in range(8):
        y = np.asarray(R[core]["YT"]).T
        if core >= 4:
            y_sample[core - 4] = y
            continue
        sl = slice(core * 8, core * 8 + 8)
        y_prompt[sl] = y.reshape(8, 256, D)
        STD_ = np.asarray(R[core]["STD"])
        STS_ = np.asarray(R[core]["STS"])
        STM_ = np.asarray(R[core]["STM"])
        for l in range(L):
            for pr in range(2):
                ndk[sl, l, :, 2 * pr:2 * pr + 2, :] = STD_[l, pr, :, 0:128].reshape(8, 256, 2, 64)
                ndv[sl, l, :, 2 * pr:2 * pr + 2, :] = STD_[l, pr, :, 128:256].reshape(8, 256, 2, 64)
            for g in range(2):
                nsk[sl, l, :, g, :] = STS_[l, g, :, 0:64].reshape(8, 256, 64)
                nsv[sl, l, :, g, :] = STS_[l, g, :, 64:128].reshape(8, 256, 64)
            nck[sl, l] = STM_[l, :, 0:128].reshape(8, 256, 128)
            nkr[sl, l] = STM_[l, :, 128:160].reshape(8, 256, 32)
    return (y_prompt, y_sample, ndk, ndv, nsk, nsv, nck, nkr)
```

```python
import math
import numpy as np
import concourse.bass as bass
import concourse.mybir as mybir
from concourse.bass_utils import run_bass_kernel_spmd

F32 = mybir.dt.float32
BF16 = mybir.dt.bfloat16
AF = mybir.ActivationFunctionType
ALU = mybir.AluOpType
AX = mybir.AxisListType

L = 4
D = 1024
NT = 2048
NKEY = 2304
KC = 2306
BIG = 30000.0
EPS = 1e-6
SC_DA = 32 ** -0.5
SC_SW = 0.125
SC_MLA = 96 ** -0.5
NFT = 31
N_LAYERS_RUN = L
SEM_ROTATE = 800
WARM_N = 0
LOOK = 2
INLINE_WAIT = True
STOP_AT = None
DA_PAIRS = (0, 1)
SKIP = set()
DEBUG_DUMP = False


class _Stop(Exception):
    pass


_CKN = {}


def ck(name):
    if STOP_AT is None:
        return
    base, _, n = STOP_AT.partition('#')
    if base == name:
        _CKN[name] = _CKN.get(name, 0) + 1
        if _CKN[name] >= int(n or 1):
            raise _Stop()


class Op:
    __slots__ = ("eng", "fn", "deps", "need", "sem", "val", "dma", "epoch", "idx")


class Prog:
    ENGS = ("pe", "act", "dve", "pool", "sp")

    def __init__(self):
        self.ops = []
        self.last_w = {}
        self.readers = {}
        self.epoch = 0
        self.all_last = {}

    def add(self, eng, fn, r=(), w=(), dma=None):
        op = Op()
        op.eng, op.fn, op.need, op.sem, op.val, op.dma, op.epoch = eng, fn, dma is not None, None, 0, dma, self.epoch
        op.idx = len(self.ops)
        deps = {}
        for k in r:
            o = self.last_w.get(k)
            if o is not None:
                deps[o.idx] = o
        for k in w:
            o = self.last_w.get(k)
            if o is not None:
                deps[o.idx] = o
            for o in self.readers.get(k, ()):
                deps[o.idx] = o
        bar = self.all_last.get("barrier")
        if bar is not None:
            for o in bar:
                deps[o.idx] = o
        best = {}
        dl = []
        for o in deps.values():
            if o is op:
                continue
            if o.dma is not None:
                dl.append(o)
                continue
            if o.eng == "pe" and eng == "pe" and dma is None:
                continue
            b = best.get(o.eng)
            if b is None or o.idx > b.idx:
                best[o.eng] = o
        dl += list(best.values())
        for o in dl:
            o.need = True
        op.deps = dl
        for k in w:
            self.last_w[k] = op
            self.readers[k] = []
        for k in r:
            self.readers.setdefault(k, []).append(op)
        self.ops.append(op)
        self.all_last[(eng, dma)] = op
        return op

    def barrier(self):
        self.all_last["barrier"] = [o for k, o in self.all_last.items() if k != "barrier"]

    def new_epoch(self):
        self.epoch += 1

    def emit(self, nc, es):
        sems = {}
        counts = {}
        subs = {}
        for op in self.ops:
            if not op.need:
                continue
            if op.dma is None:
                base = (op.eng, op.epoch)
                sub = subs.get(base, 0)
                if counts.get(base + (sub,), 0) >= SEM_ROTATE:
                    sub += 1
                    subs[base] = sub
                key = base + (sub,)
            else:
                key = ("dma_" + op.dma, 0, 0)
            if key not in sems:
                sems[key] = es.enter_context(nc.semaphore("s%d" % len(sems)))
                counts[key] = 0
            counts[key] += 16 if op.dma is not None else 1
            op.sem, op.val = sems[key], counts[key]
        self.counts = counts
        block = es.enter_context(nc.Block())
        by_eng = {e: [o for o in self.ops if o.eng == e] for e in self.ENGS}

        def run(e, ops):
            waited = {}
            for op in ops:
                todo = {}
                for d in op.deps:
                    sid = id(d.sem)
                    if waited.get(sid, 0) >= d.val:
                        continue
                    if sid not in todo or todo[sid].val < d.val:
                        todo[sid] = d
                todo = list(todo.values())
                attach = None
                if INLINE_WAIT and todo and op.dma is None:
                    attach = todo.pop()
                for d in todo:
                    e.wait_ge(d.sem, d.val)
                    waited[id(d.sem)] = d.val
                ins = op.fn(e)
                if attach is not None:
                    ins._wait_ge(attach.sem, attach.val)
                    waited[id(attach.sem)] = attach.val
                if op.need:
                    ins.then_inc(op.sem, 16 if op.dma is not None else 1)

        block.tensor(lambda e: run(e, by_eng["pe"]))
        block.scalar(lambda e: run(e, by_eng["act"]))
        block.vector(lambda e: run(e, by_eng["dve"]))
        block.gpsimd(lambda e: run(e, by_eng["pool"]))
        block.sync(lambda e: run(e, by_eng["sp"]))


def _partner(d):
    m = d // 2
    h = m // 2
    p = np.zeros(d, np.int64)
    sg = np.zeros(d, np.float32)
    for base in (0, m):
        for i in range(h):
            p[base + i] = base + i + h
            sg[base + i] = -1.0
            p[base + i + h] = base + i
            sg[base + i + h] = 1.0
    return p, sg


def _rope_tab(d, latent):
    m = d // 2
    h = m // 2
    _, sg = _partner(d)
    if not latent:
        return np.ones((d, NT), np.float32), np.zeros((d, NT), np.float32)
    freqs = (np.float32(10000.0) ** (-np.arange(h, dtype=np.float32) / np.float32(h))).astype(np.float32)
    t = np.arange(NT)
    rows = (t // 64).astype(np.float32)
    cols = (t % 64).astype(np.float32)
    ang = np.zeros((d, NT), np.float32)
    for i in range(d):
        pos = rows if i < m else cols
        ang[i] = pos * freqs[(i % m) % h]
    return np.cos(ang).astype(np.float32), (np.sin(ang) * sg[:, None]).astype(np.float32)


def _fm(cols):
    K = cols.shape[0]
    return np.ascontiguousarray(cols.reshape(K // 128, 128, cols.shape[1]).transpose(1, 0, 2))


def _prep_shared(inp):
    w_in = inp["w_in"]
    sh = {}
    p32, _ = _partner(32)
    p64, _ = _partner(64)
    WF = np.zeros((L, NFT, 128, 8, 128), np.float32)
    WQB = np.zeros((L, 8, 128, 2, 128), np.float32)
    WKN = np.zeros((L, 4, 128, 128), np.float32)
    WKV = np.zeros((L, 128, 256), np.float32)
    for l in range(L):
        w = w_in[l]
        tiles = []
        for base_off, comp_of in ((256, lambda h, c: 256 + h * 64 + c * 32), (0, lambda h, c: h * 64 + c * 32)):
            for h in range(4):
                A = np.zeros((D, 128), np.float32)
                B = np.zeros((D, 128), np.float32)
                for c in range(2):
                    o = comp_of(h, c)
                    A[:, c * 64:c * 64 + 32] = w[:, o:o + 32]
                    B[:, c * 64:c * 64 + 32] = w[:, o + p32]
                tiles += [A, B]
        A = np.zeros((D, 128), np.float32)
        B = np.zeros((D, 128), np.float32)
        for g in range(2):
            o = 1280 + g * 64
            A[:, g * 64:(g + 1) * 64] = w[:, o:o + 64]
            B[:, g * 64:(g + 1) * 64] = w[:, o + p64]
        tiles += [A, B]
        for pr in range(4):
            A = np.zeros((D, 128), np.float32)
            B = np.zeros((D, 128), np.float32)
            for j in range(2):
                o = 768 + (2 * pr + j) * 64
                A[:, j * 64:(j + 1) * 64] = w[:, o:o + 64]
                B[:, j * 64:(j + 1) * 64] = w[:, o + p64]
            tiles += [A, B]
        tiles += [w[:, 1536:1664], w[:, 1664:1792]]
        A = np.zeros((D, 128), np.float32)
        B = np.zeros((D, 128), np.float32)
        A[:, 32:64] = w[:, 1920:1952]
        B[:, 32:64] = w[:, 1920 + p32]
        tiles += [A, B]
        tiles += [w[:, 1792:1920]]
        assert len(tiles) == NFT
        for i, t in enumerate(tiles):
            WF[l, i] = _fm(np.ascontiguousarray(t))
        wqb = inp["mla_w_qb"][l]
        for h in range(4):
            A = np.zeros((256, 128), np.float32)
            B = np.zeros((256, 128), np.float32)
            A[:, 32:64] = wqb[:, h * 96 + 64:h * 96 + 96]
            A[:, 64:128] = wqb[:, h * 96:h * 96 + 64]
            B[:, 32:64] = wqb[:, h * 96 + 64 + p32]
            WQB[l, 2 * h] = _fm(A)
            WQB[l, 2 * h + 1] = _fm(B)
        wkvb = inp["mla_w_kvb"][l]
        for h in range(4):
            WKN[l, h, :, 64:128] = wkvb[:, h * 128:h * 128 + 64]
            WKV[l, :, h * 64:(h + 1) * 64] = wkvb[:, h * 128 + 64:h * 128 + 128]
    sh["WF"] = WF
    sh["WQB"] = WQB
    sh["WKN"] = WKN
    sh["WKV"] = WKV
    WSD = np.zeros((L, 2, 128, 8, 256), np.float32)
    WSS = np.zeros((L, 2, 128, 8, 128), np.float32)
    WSM = np.zeros((L, 128, 8, 160), np.float32)
    for l in range(L):
        w = w_in[l]
        for pr in range(2):
            c = np.concatenate([w[:, 256 + pr * 128:256 + pr * 128 + 128], w[:, 512 + pr * 128:512 + pr * 128 + 128]], 1)
            WSD[l, pr] = c.reshape(8, 128, 256).transpose(1, 0, 2)
        for g in range(2):
            c = np.concatenate([w[:, 1280 + g * 64:1280 + g * 64 + 64], w[:, 1408 + g * 64:1408 + g * 64 + 64]], 1)
            WSS[l, g] = c.reshape(8, 128, 128).transpose(1, 0, 2)
        WSM[l] = w[:, 1792:1952].reshape(8, 128, 160).transpose(1, 0, 2)
    sh["WSD"], sh["WSS"], sh["WSM"] = WSD, WSS, WSM
    sh["WOUT"] = np.ascontiguousarray(inp["w_out"].reshape(L, 8, 128, D).transpose(0, 2, 1, 3))
    sh["WUP"] = np.ascontiguousarray(inp["w_up"].reshape(L, 8, 128, 8, 512).transpose(0, 3, 2, 1, 4))
    sh["WDN"] = np.ascontiguousarray(inp["w_down"].reshape(L, 32, 128, 8, 128).transpose(0, 3, 2, 1, 4))
    sh["WADA"] = np.ascontiguousarray(inp["w_ada"].reshape(L, 8, 128, 12, 512).transpose(0, 3, 2, 1, 4))
    def fmv(v):
        return v.reshape(-1, 128).T

    cols = [fmv(inp["norm1_g"][l]) for l in range(L)] + [fmv(inp["norm2_g"][l]) for l in range(L)]
    cols += [fmv(inp["b_ada"][l]) for l in range(L)]
    cols += [fmv(inp["final_g"])]
    cols += [fmv(inp["mla_q_norm_g"][l]) for l in range(L)]
    cols += [np.tile(inp["diff_subln_g"][l], 2)[:, None] for l in range(L)]
    cols += [inp["mla_kv_norm_g"][l][:, None] for l in range(L)]
    sh["VEC"] = np.ascontiguousarray(np.concatenate(cols, 1).astype(np.float32))
    assert sh["VEC"].shape == (128, VEC_W)
    sh["GKV"] = np.ascontiguousarray(inp["mla_kv_norm_g"].reshape(1, L * 128).astype(np.float32))
    lam = np.concatenate([inp["diff_lambda_q1"], inp["diff_lambda_k1"], inp["diff_lambda_q2"], inp["diff_lambda_k2"]], 1)
    sh["LAM"] = np.ascontiguousarray(lam.reshape(1, L * 128).astype(np.float32))
    sink = inp["swa_sink"]
    sh["SINK"] = np.ascontiguousarray(np.repeat(sink.reshape(L, 8, 1), 512, axis=2).astype(np.float32))
    CM = np.zeros((128, 5, 128), np.float32)
    CM[:, 0, :] = np.eye(128)
    CM[:, 1, :] = 1.0
    CM[0:32, 2, 32] = 1.0
    CM[64:96, 2, 96] = 1.0
    CM[0:64, 3, 64] = 1.0
    CM[32:128, 4, 0] = 1.0
    sh["CM"] = CM
    return sh


VEC_N1, VEC_N2, VEC_BADA, VEC_FG, VEC_GQ, VEC_SUB, VEC_W = 0, 32, 64, 256, 264, 272, 280


def _prep_core(inp, core):
    latent = core >= 4
    d = {}
    if latent:
        b = core - 4
        x = inp["x_sample"][b]
        cv = inp["c"][b]
    else:
        x = inp["x_prompt"][core * 8:(core + 1) * 8].reshape(NT, D)
        cv = inp["c_ctx"]
    d["XT"] = np.ascontiguousarray(x.T)
    d["CT"] = np.ascontiguousarray(cv.reshape(8, 128).T.astype(np.float32))
    CKD = np.zeros((L, 4, 128, 256), np.float32)
    CVD = np.zeros((L, 2, 2, 128, 192), np.float32)
    CKS = np.zeros((L, 2, 128, 256), np.float32)
    CVS = np.zeros((L, 2, 2, 128, 128), np.float32)
    CCK = np.zeros((L, 128, 256), np.float32)
    CKR = np.zeros((L, 128, 256), np.float32)
    CVD[:, :, :, :, 64:128] = 1.0
    CVS[:, :, :, :, 64:128] = 1.0
    AUG = np.zeros((128, KC), np.float32)
    AUG[32, :] = -1.0
    AUG[96, :] = -1.0
    AUG[97, NKEY] = 8.0
    AUG[98, 0:256] = 1.0
    if not latent:
        t = np.arange(NT)
        for s in range(8):
            AUG[1 + s, 0:NT] = (t // 256 == s)
            AUG[33 + s, 0:256] = -BIG
            AUG[33 + s, 256:NKEY] = np.where(t // 256 == s, 0.0, -BIG)
        AUG[66, :] = -BIG
    if latent:
        b = core - 4
        for l in range(L):
            dk = inp["cache_diff_k"][b, l]
            for h in range(4):
                CKD[l, h, 0:32] = dk[:, h, 0:32].T
                CKD[l, h, 64:96] = dk[:, h, 32:64].T
            dv = inp["cache_diff_v"][b, l]
            for pr in range(2):
                for blk in range(2):
                    CVD[l, pr, blk, :, 0:64] = dv[blk * 128:(blk + 1) * 128, 2 * pr]
                    CVD[l, pr, blk, :, 128:192] = dv[blk * 128:(blk + 1) * 128, 2 * pr + 1]
            sk = inp["cache_swa_k"][b, l]
            sv = inp["cache_swa_v"][b, l]
            for g in range(2):
                CKS[l, g, 0:64] = sk[:, g].T
                for blk in range(2):
                    CVS[l, g, blk, :, 0:64] = sv[blk * 128:(blk + 1) * 128, g]
            CCK[l] = inp["cache_mla_ckv"][b, l].T
            CKR[l, 32:64] = inp["cache_mla_krope"][b, l].T
    for l in range(L):
        for h in range(4):
            CKD[l, h, 32:64] = AUG[32:64, 0:256]
            CKD[l, h, 96:128] = AUG[32:64, 0:256]
        for g in range(2):
            CKS[l, g, 64:96] = AUG[96:128, 0:256]
        CKR[l, 0:32] = AUG[32:64, 0:256]
    d["CKD"], d["CVD"], d["CKS"], d["CVS"], d["CCK"], d["CKR"], d["AUG"] = CKD, CVD, CKS, CVS, CCK, CKR, AUG
    c32, s32 = _rope_tab(32, latent)
    c64, s64 = _rope_tab(64, latent)
    ROPE = np.zeros((3, 128, 2, NT), np.float32)
    ROPE[:, :, 0, :] = 1.0
    for o in (0, 64):
        ROPE[0, o:o + 32, 0] = c32
        ROPE[0, o:o + 32, 1] = s32
        ROPE[1, o:o + 64, 0] = c64
        ROPE[1, o:o + 64, 1] = s64
    ROPE[2, 32:64, 0] = c32
    ROPE[2, 32:64, 1] = s32
    d["ROPE"] = ROPE
    BM = np.zeros((128, 4, 4, 128), np.float32)
    bb = np.arange(128)[:, None]
    aa = np.arange(128)[None, :]
    if latent:
        m_lo = np.where(aa <= bb, 0.0, -BIG)
        m_hi = np.where(bb <= aa, 0.0, -BIG)
        for r in range(4):
            BM[:, 0, r] = m_lo
            BM[:, 1, r] = m_lo
            BM[:, 2, r] = m_hi
            BM[:, 3, r] = m_hi
    else:
        BM[:, 0] = -BIG
        BM[:, 3] = -BIG
    d["BM"] = BM.reshape(128, 4, 512)
    return d


def build_program():
    from contextlib import ExitStack
    nc = bass.Bass("TRN2", target_bir_lowering=False)
    P = Prog()

    def din(name, shape):
        return nc.dram_tensor(name, list(shape), F32, kind="ExternalInput").ap()

    def dout(name, shape):
        return nc.dram_tensor(name, list(shape), F32, kind="ExternalOutput").ap()

    XT = din("XT", [D, NT]); CT = din("CT", [128, 8])
    CKD = din("CKD", [L, 4, 128, 256]); CVD = din("CVD", [L, 2, 2, 128, 192])
    CKS = din("CKS", [L, 2, 128, 256]); CVS = din("CVS", [L, 2, 2, 128, 128])
    CCK = din("CCK", [L, 128, 256]); CKR = din("CKR", [L, 128, 256])
    AUGd = din("AUG", [128, KC]); ROPEd = din("ROPE", [3, 128, 2, NT]); BMd = din("BM", [128, 4, 512])
    WF = din("WF", [L, NFT, 128, 8, 128]); WQB = din("WQB", [L, 8, 128, 2, 128])
    WKN = din("WKN", [L, 4, 128, 128]); WKV = din("WKV", [L, 128, 256])
    WSD = din("WSD", [L, 2, 128, 8, 256]); WSS = din("WSS", [L, 2, 128, 8, 128]); WSM = din("WSM", [L, 128, 8, 160])
    WOUT = din("WOUT", [L, 128, 8, D]); WUP = din("WUP", [L, 8, 128, 8, 512]); WDN = din("WDN", [L, 8, 128, 32, 128])
    WADA = din("WADA", [L, 12, 128, 8, 512])
    VECd = din("VEC", [128, VEC_W]); GKVd = din("GKV", [1, L * 128]); LAMd = din("LAM", [1, L * 128])
    SINKd = din("SINK", [L, 8, 512]); CMd = din("CM", [128, 5, 128])
    YT = dout("YT", [D, NT])
    STD = dout("STD", [L, 2, NT, 256]); STS = dout("STS", [L, 2, NT, 128]); STM = dout("STM", [L, NT, 160])

    off = [18432]

    def sb(name, shape, dt, at=None):
        nbytes = int(np.prod(shape[1:])) * (4 if dt == F32 else 2)
        nbytes = (nbytes + 31) // 32 * 32
        if at is None:
            o = off[0]
            off[0] += nbytes
        else:
            o = at
        assert o + nbytes <= 229376, (name, o, nbytes)
        return nc.alloc_sbuf_tensor_at(name, list(shape), dt, offset=o)

    xT = sb("xT", [128, 8, NT], F32)
    AUG = sb("AUG", [128, KC], BF16)
    BM = sb("BM", [128, 4, 512], BF16)
    CM = sb("CM", [128, 5, 128], BF16)
    VEC = sb("VEC", [128, VEC_W], F32)
    MOD = sb("MOD", [128, L * 48], F32)
    GM1 = sb("GM1", [128, L * 8], F32)
    GM2 = sb("GM2", [128, L * 8], F32)
    GKV = sb("GKV", [128, L * 128], F32)
    NLAM = sb("NLAM", [128, 8], F32)
    KMAX = sb("KMAX", [128, 8], F32)
    KPART = sb("KPART", [128, 16], F32)
    SMALL = sb("SMALL", [128, 16], F32)
    SILC = sb("SILC", [128, 8], BF16)
    region0 = off[0]
    hT = sb("hT", [128, 8, NT], BF16)
    KT = sb("KT", [128, 2, KC], BF16)
    Vb = sb("Vb", [128, 18, 192], BF16)
    QT = sb("QT", [128, 2048], BF16)
    MIX = sb("MIX", [128, 2, 512], BF16)
    ROPEb = sb("ROPEb", [128, 2, 2, 512], F32)
    WR = sb("WR", [128, 4, 8, 128], BF16)
    WO = sb("WO", [128, 2, D], BF16)
    WS = sb("WS", [128, 8, 256], BF16)
    PT = sb("PT", [128, 4, 512], BF16)
    TA = sb("TA", [128, 512], F32)
    TB = sb("TB", [128, 512], F32)
    TC = sb("TC", [128, 512], F32)
    RSTD = sb("RSTD", [128, 512], F32)
    SQ = sb("SQ", [128, 2, 512], BF16)
    RR = sb("RR", [128, 1024], F32)
    STG = sb("STG", [128, 2, 256], F32)
    CKVT = sb("CKVT", [128, KC], BF16)
    QAG = sb("QAG", [128, 2, 512], BF16)
    CSR = sb("CSR", [128, 2, 512], F32)
    WQBb = sb("WQBb", [128, 4, 2, 128], BF16)
    WKNb = sb("WKNb", [128, 2, 128], BF16)
    WKVb = sb("WKVb", [128, 256], BF16)
    att_end = off[0]
    off[0] = region0
    H2 = sb("H2", [128, 8, 1024], BF16)
    AT = sb("AT", [128, 32, 1024], BF16)
    WM = sb("WM", [128, 3, 4096], BF16)
    SQ2 = sb("SQ2", [128, 2, 512], BF16)
    RSTD2 = sb("RSTD2", [128, 512], F32)
    TA2 = sb("TA2", [128, 512], F32)
    TB2 = sb("TB2", [128, 512], F32)
    mlp_end = off[0]
    off[0] = region0
    WAb = sb("WAb", [128, 4, 8, 512], BF16)
    ROW = sb("ROW", [128, 6144], F32)
    LAMb = sb("LAMb", [128, L * 128], F32)
    LAMt = sb("LAMt", [128, L * 128], F32)
    LAMs = sb("LAMs", [128, 16], F32)
    off[0] = max(att_end, off[0], mlp_end)
    print('SBUF layout: region0', region0, 'att_end', att_end, 'mlp_end', mlp_end, 'final', off[0])

    es = ExitStack()
    PS = [es.enter_context(nc.psum_tensor("ps%d" % i, [128, 512], F32)) for i in range(8)]
    rr = {"proj": 0, "S": 0, "O": 0}

    def bank(cls):
        if cls == "proj":
            b = rr["proj"] % 3
        elif cls == "S":
            b = 3 + rr["S"] % 3
        else:
            b = 6 + rr["O"] % 2
        rr[cls] += 1
        return b

    def mm(out, lhsT, rhs, start, stop, r, w):
        P.add("pe", lambda e: e.matmul(out, lhsT, rhs, start=start, stop=stop), r=r, w=w)

    def _cls(k):
        return "".join(ch for ch in str(k) if ch.isalnum())

    def dma_cast(out, in_, r=(), w=()):
        P.add("pool", lambda e: e.dma_start(out=out, in_=in_), r=r, w=w, dma="w" + _cls(w[0]))

    def dma_ld(out, in_, r=(), w=()):
        P.add("sp", lambda e: e.dma_start(out=out, in_=in_), r=r, w=w, dma="l" + _cls(w[0]))

    def dma_st(out, in_, r=(), w=()):
        return P.add("sp", lambda e: e.dma_start(out=out, in_=in_), r=r, w=w, dma="s" + _cls(r[0]))

    def act(out, in_, func, r, w, scale=1.0, bias=0.0):
        P.add("act", lambda e: e.activation(out=out, in_=in_, func=func, bias=bias, scale=scale), r=r, w=w)

    def _e(eng):
        return "dve" if eng == "pool" else eng

    def tt(eng, out, in0, in1, op, r, w):
        eng = _e(eng)
        P.add(eng, lambda e: e.tensor_tensor(out=out, in0=in0, in1=in1, op=op), r=r, w=w)

    def ts(eng, out, in0, s1, s2, op0, op1, r, w):
        eng = _e(eng)
        if s2 is None:
            P.add(eng, lambda e: e.tensor_scalar(out=out, in0=in0, scalar1=s1, scalar2=None, op0=op0), r=r, w=w)
        else:
            P.add(eng, lambda e: e.tensor_scalar(out=out, in0=in0, scalar1=s1, scalar2=s2, op0=op0, op1=op1), r=r, w=w)

    def stt(eng, out, in0, scalar, in1, op0, op1, r, w):
        eng = _e(eng)
        P.add(eng, lambda e: e.scalar_tensor_tensor(out=out, in0=in0, scalar=scalar, in1=in1, op0=op0, op1=op1), r=r, w=w)

    def cp(eng, out, in_, r, w):
        eng = _e(eng)
        if eng == "act":
            P.add("act", lambda e: e.activation(out=out, in_=in_, func=AF.Copy), r=r, w=w)
        else:
            P.add(eng, lambda e: e.tensor_copy(out=out, in_=in_), r=r, w=w)

    def rsqrt_from_sum(out, in_, inv_n, r, w, tmp, tmpkey):
        act(tmp, in_, AF.Ln, r=r, w=[tmpkey], scale=inv_n, bias=EPS)
        act(out, tmp, AF.Exp, r=[tmpkey], w=w, scale=-0.5)

    ident = CM[:, 0, :]
    ones = CM[:, 1, :]
    SELB = {"da": CM[:, 2, :], "sw": CM[:, 3, :], "mla": CM[:, 4, :]}

    ONE1 = sb("ONE1", [128, 8], F32)
    VSK = sb("VSK", [128, 128], BF16)
    TD = sb("TD", [128, 512], F32)
    LNT = sb("LNT", [128, 512], F32)

    def tsl(t):
        return slice(t * 512, (t + 1) * 512)

    def dump(name, ap, keys):
        if not DEBUG_DUMP:
            return
        t = nc.dram_tensor("DBG_" + name, list(ap.shape), ap.dtype, kind="ExternalOutput").ap()
        P.add("sp", lambda e: e.dma_start(out=t, in_=ap), r=keys, dma="dbg" + name)

    dma_ld(xT[:, :, :], XT.rearrange("(c p) t -> p c t", p=128), w=[("x", c, t) for c in range(8) for t in range(4)])
    dma_cast(AUG[:, :], AUGd[:, :], w=["AUG"])
    dma_cast(BM[:, :, :], BMd[:, :, :], w=["BM"])
    dma_cast(CM[:, :, :], CMd[:, :, :], w=["CM"])
    dma_ld(VEC[:, :], VECd[:, :], w=["VEC"])
    dma_ld(GKV[:, :], GKVd[0:1, :].partition_broadcast(128), w=["GKV"])
    dma_ld(LAMb[:, :], LAMd[0:1, :].partition_broadcast(128), w=["LAMb"])
    dma_ld(LAMs[:, 0:8], CT[:, :], w=["CTf"])
    act(SILC[:, :], LAMs[:, 0:8], AF.Silu, r=["CTf"], w=["SILC"])
    P.add("pool", lambda e: e.memset(ONE1[:, :], 1.0), w=["ONE1"])
    P.add("pool", lambda e: e.memset(VSK[:, 0:64], 0.0), w=["VSK"])
    P.add("pool", lambda e: e.memset(VSK[:, 64:128], 1.0), w=["VSK"])
    lb = LAMb[:, :].rearrange("p (l f d) -> p l f d", l=L, f=4)
    lt = LAMt[:, :].rearrange("p (l f d) -> p l f d", l=L, f=4)
    tt("dve", lt[:, :, 0, :], lb[:, :, 0, :], lb[:, :, 1, :], ALU.mult, r=["LAMb"], w=["LAMt0"])
    tt("dve", lt[:, :, 1, :], lb[:, :, 2, :], lb[:, :, 3, :], ALU.mult, r=["LAMb"], w=["LAMt1"])
    P.add("dve", lambda e: e.reduce_sum(out=SMALL[:, 0:4], in_=lt[:, :, 0, :], axis=AX.X), r=["LAMt0"], w=["SM0"])
    P.add("dve", lambda e: e.reduce_sum(out=SMALL[:, 4:8], in_=lt[:, :, 1, :], axis=AX.X), r=["LAMt1"], w=["SM1"])
    act(SMALL[:, 8:16], SMALL[:, 0:8], AF.Exp, r=["SM0", "SM1"], w=["SM2"])
    tt("dve", NLAM[:, 0:4], SMALL[:, 12:16], SMALL[:, 8:12], ALU.subtract, r=["SM2"], w=["NLAM"])
    LAM_INIT = [0.8 - 0.6 * math.exp(-0.3 * l) for l in range(L)]
    for l in range(L):
        ts("dve", NLAM[:, l:l + 1], NLAM[:, l:l + 1], -LAM_INIT[l], None, ALU.add, None, r=["NLAM"], w=["NLAM"])
    for l in range(L):
        for s in range(12):
            slot = (l * 12 + s) % 4
            dma_cast(WAb[:, slot, :, :], WADA[l, s], w=[("WAb", slot)])
            b = bank("proj")
            for c in range(8):
                mm(PS[b][0:1, :], SILC[:, c:c + 1], WAb[:, slot, c, :], c == 0, c == 7, r=[("WAb", slot), "SILC"], w=[("ps", b)])
            cp("act", ROW[0:1, s * 512:(s + 1) * 512], PS[b][0:1, :], r=[("ps", b)], w=["ROW"])
        b = bank("proj")
        for j in range(48):
            def f(e, j=j, b=b):
                return e.matmul(PS[b][:, j:j + 1], ROW[0:1, j * 128:(j + 1) * 128], ONE1[0:1, 0:1], start=True, stop=True)
            P.add("pe", f, r=["ROW", "ONE1"], w=[("ps", b)])
        tt("dve", MOD[:, l * 48:(l + 1) * 48], PS[b][:, 0:48], VEC[:, VEC_BADA + l * 48:VEC_BADA + (l + 1) * 48], ALU.add,
           r=[("ps", b), "VEC"], w=["MOD"])
        for (GM, vo, so) in ((GM1, VEC_N1, 8), (GM2, VEC_N2, 32)):
            stt("dve", GM[:, l * 8:(l + 1) * 8], MOD[:, l * 48 + so:l * 48 + so + 8], 1.0, VEC[:, vo + l * 8:vo + (l + 1) * 8],
                ALU.add, ALU.mult, r=["MOD", "VEC"], w=["GM"])
    P.barrier()
    STOPPED = STOP_AT == 'pro'

    class WStream:
        def __init__(self, slot_ap, nslots, key, look):
            self.slot_ap, self.nslots, self.key, self.look = slot_ap, nslots, key, look
            self.ctr = 0
            self.srcs = []
            self.nxt = 0
            self.base = 0

        def plan(self, srcs):
            self.base = self.ctr
            self.srcs = list(srcs)
            self.nxt = 0

        def get(self, i):
            while self.nxt <= min(i + self.look, len(self.srcs) - 1):
                slot = (self.base + self.nxt) % self.nslots
                dma_cast(self.slot_ap(slot), self.srcs[self.nxt], w=[(self.key, slot)])
                self.nxt += 1
                self.ctr += 1
            slot = (self.base + i) % self.nslots
            return self.slot_ap(slot), (self.key, slot)

    wr = WStream(lambda s: WR[:, s, :, :], 4, "WR", 2)
    wm = WStream(lambda s: WM[:, s, :], 3, "WM", 1)
    cnt = {"sq": 0, "pt": 0, "stg": 0, "rope": 0, "tmp": 0}

    def nxt(k, n):
        v = cnt[k] % n
        cnt[k] += 1
        return v

    def norm_tile(l, t, gm_ap_fn, bias_fn, out_fn, out_key, sqb, rstdb, tmps, tmpkeys):
        b = bank("proj")
        for c in range(8):
            s = nxt("sq", 2)
            tt("dve", sqb[:, s, :], xT[:, c, tsl(t)], xT[:, c, tsl(t)], ALU.mult, r=[("x", c, t)], w=[("SQ", s)])
            mm(PS[b][:, :], ones, sqb[:, s, :], c == 0, c == 7, r=[("SQ", s), "CM"], w=[("ps", b)])
        rsqrt_from_sum(rstdb[:, :], PS[b][:, :], 1.0 / D, r=[("ps", b)], w=["RSTD"], tmp=tmps[0][:, :], tmpkey=tmpkeys[0])
        for c in range(8):
            k = c % 2
            stt("dve", tmps[k][:, :], xT[:, c, tsl(t)], gm_ap_fn(c), rstdb[:, :], ALU.mult, ALU.mult,
                r=[("x", c, t), "RSTD", "GM", "VEC"], w=[tmpkeys[k]])
            out_fn(c, tmps[k], tmpkeys[k])

    def rope_evac(bA, bB, cos, sin, rope_key, outs, rows=slice(0, 128)):
        k = nxt("tmp", 2)
        T0, T1 = (TA, TB) if k == 0 else (TC, TD)
        k0, k1 = ("TA", "TB") if k == 0 else ("TC", "TD")
        tt("dve", T0[rows, :], PS[bA][rows, :], cos[rows, :], ALU.mult, r=[("ps", bA)] + rope_key, w=[k0])
        tt("dve", T1[rows, :], PS[bB][rows, :], sin[rows, :], ALU.mult, r=[("ps", bB)] + rope_key, w=[k1])
        for (o, inr, wk, shp) in outs:
            a0, a1 = T0[inr, :], T1[inr, :]
            if shp is not None:
                a0 = a0.rearrange("p (b q) -> p b q", b=shp)
                a1 = a1.rearrange("p (b q) -> p b q", b=shp)
            tt("pool", o, a0, a1, ALU.add, r=[k0, k1], w=wk)

    def proj2(wA, kA, wB, kB, rhs_fn, rkeys, nk=8, n=512):
        bA = bank("proj")
        bB = bank("proj")
        for c in range(nk):
            mm(PS[bA][:, 0:n], wA[:, c, :], rhs_fn(c), c == 0, c == nk - 1, r=[kA] + rkeys, w=[("ps", bA)])
        for c in range(nk):
            mm(PS[bB][:, 0:n], wB[:, c, :], rhs_fn(c), c == 0, c == nk - 1, r=[kB] + rkeys, w=[("ps", bB)])
        return bA, bB

    def ktkeys(hh, kb):
        return [("KT", hh, "c")] if kb < 2 else [("KT", hh, (kb - 2) // 4)]

    def all_kt(hh):
        return [("KT", hh, s) for s in ("c", 0, 1, 2, 3)]

    def kmax(hh, kind, rows):
        chunks = [(0, 512), (512, 1024), (1024, 1536), (1536, 2048), (2048, 2304)]
        for ci, (a, bb) in enumerate(chunks):
            n = bb - a
            s = nxt("sq", 2)
            kr = 96 if kind == "sw" else 128
            tt("dve", SQ[0:kr, s, 0:n], KT[0:kr, hh, a:bb], KT[0:kr, hh, a:bb], ALU.mult, r=all_kt(hh), w=[("SQ", s)])
            b = bank("proj")
            mm(PS[b][:, 0:n], SELB[kind][0:kr, :], SQ[0:kr, s, 0:n], True, True, r=[("SQ", s), "CM"], w=[("ps", b)])
            for row in rows:
                P.add("dve", (lambda row=row, ci=ci, b=b, n=n: lambda e: e.reduce_max(out=KPART[row:row + 1, ci:ci + 1], in_=PS[b][row:row + 1, 0:n], axis=AX.X))(),
                      r=[("ps", b)], w=["KPART"])
        for row in rows:
            P.add("dve", (lambda row=row: lambda e: e.reduce_max(out=KMAX[row:row + 1, hh:hh + 1], in_=KPART[row:row + 1, 0:5], axis=AX.X))(),
                  r=["KPART"], w=[("KMAX", hh)])

    def bound_rows(qv, qkey, hh, kind, rows, n=512):
        s = nxt("sq", 2)
        kr = 96 if kind == "sw" else 128
        tt("dve", SQ[0:kr, s, 0:n], qv[0:kr, :], qv[0:kr, :], ALU.mult, r=[qkey], w=[("SQ", s)])
        b = bank("proj")
        mm(PS[b][:, 0:n], SELB[kind][0:kr, :], SQ[0:kr, s, 0:n], True, True, r=[("SQ", s), "CM"], w=[("ps", b)])
        for row in rows:
            act(LNT[row:row + 1, 0:n], PS[b][row:row + 1, 0:n], AF.Ln, r=[("ps", b), ("KMAX", hh)], w=["LNT"],
                scale=KMAX[row:row + 1, hh:hh + 1], bias=1e-30)
            act(qv[row:row + 1, :], LNT[row:row + 1, 0:n], AF.Exp, r=["LNT"], w=[qkey], scale=0.5)

    def attend(hh, krows, qv, qkey, vcols, scale, kbs=range(18)):
        bo = bank("O")
        kbs = list(kbs)
        pend = []
        n = len(kbs)
        for i, kb in enumerate(kbs):
            bs = bank("S")
            mm(PS[bs][:, :], KT[krows, hh, kb * 128:(kb + 1) * 128], qv[krows, :], True, True,
               r=ktkeys(hh, kb) + [qkey], w=[("ps", bs)])
            s = nxt("pt", 4)
            act(PT[:, s, :], PS[bs][:, :], AF.Exp, r=[("ps", bs)], w=[("PT", s)], scale=scale)
            pend.append((kb, s, i))
            if len(pend) > LOOK:
                pkb, ps_, pi = pend.pop(0)
                mm(PS[bo][:, :], Vb[:, pkb, vcols], PT[:, ps_, :], pi == 0, False, r=[("PT", ps_), ("Vb", pkb)], w=[("ps", bo)])
        while pend:
            pkb, ps_, pi = pend.pop(0)
            mm(PS[bo][:, :], Vb[:, pkb, vcols], PT[:, ps_, :], pi == 0, pi == n - 1, r=[("PT", ps_), ("Vb", pkb)], w=[("ps", bo)])
        return bo

    s4 = {"i": 0}

    def bank_s4():
        b = (2, 3, 4, 5)[s4["i"] % 4]
        s4["i"] += 1
        return b

    def attend_da(hh, qv, qkey, vcols, scale):
        boa, bob = bank("O"), bank("O")
        pend = []
        n = 18
        ra, rb = slice(0, 64), slice(64, 128)

        def flush(last):
            pkb, sa, sb, pi = pend.pop(0)
            mm(PS[boa][:, :], Vb[:, pkb, vcols], PT[:, sa, :], pi == 0, last, r=[("PT", sa), ("Vb", pkb)], w=[("ps", boa)])
            mm(PS[bob][:, :], Vb[:, pkb, vcols], PT[:, sb, :], pi == 0, last, r=[("PT", sb), ("Vb", pkb)], w=[("ps", bob)])

        for i in range(n):
            kb = i
            bsa, bsb = bank_s4(), bank_s4()
            mm(PS[bsa][:, :], KT[ra, hh, kb * 128:(kb + 1) * 128], qv[ra, :], True, True, r=ktkeys(hh, kb) + [qkey], w=[("ps", bsa)])
            mm(PS[bsb][:, :], KT[rb, hh, kb * 128:(kb + 1) * 128], qv[rb, :], True, True, r=ktkeys(hh, kb) + [qkey], w=[("ps", bsb)])
            sa, sb_ = nxt("pt", 4), nxt("pt", 4)
            act(PT[:, sa, :], PS[bsa][:, :], AF.Exp, r=[("ps", bsa)], w=[("PT", sa)], scale=scale)
            act(PT[:, sb_, :], PS[bsb][:, :], AF.Exp, r=[("ps", bsb)], w=[("PT", sb_)], scale=scale)
            pend.append((kb, sa, sb_, i))
            if len(pend) > 1:
                flush(False)
        while pend:
            flush(len(pend) == 1)
        return boa, bob

    def out_proj(l, j, nch, wo_key):
        for c in range(8):
            b = bank("proj")
            for k in range(nch):
                mm(PS[b][:, :], WO[:, k, c * 128:(c + 1) * 128], MIX[:, k, :], k == 0, k == nch - 1,
                   r=[wo_key, ("MIX", k)], w=[("ps", b)])
            stt("dve", xT[:, c, tsl(j)], PS[b][:, :], MOD[:, l * 48 + 16 + c:l * 48 + 17 + c], xT[:, c, tsl(j)], ALU.mult, ALU.add,
                r=[("ps", b), "MOD", ("x", c, j)], w=[("x", c, j)])

    def load_rope(kind, t):
        s = nxt("rope", 2)
        dma_ld(ROPEb[:, s, :, :], ROPEd[kind, :, :, t * 512:(t + 1) * 512], w=[("ROPEb", s)])
        return s

    def state_proj(l, ncol, dst_fn, vcopies):
        for blk in range(16):
            b = bank("proj")
            for c in range(8):
                mm(PS[b][:, 0:ncol], hT[:, c, blk * 128:(blk + 1) * 128], WS[:, c, 0:ncol], c == 0, c == 7,
                   r=[("h", blk // 4), "WS"], w=[("ps", b)])
            s = nxt("stg", 2)
            cp("act", STG[:, s, 0:ncol], PS[b][:, 0:ncol], r=[("ps", b)], w=[("STG", s)])
            dst_fn(blk, s)
            for (dst, src) in vcopies:
                if True:
                    cp("act", Vb[:, 2 + blk, dst], PS[b][:, src], r=[("ps", b)], w=[("Vb", 2 + blk)])
                elif 'vdummy' in SKIP and _CKN.get('_pairs', 0) > 1:
                    cp("dve", RR[:, 0:64], PS[b][:, src], r=[("ps", b)], w=["RR0"])
                else:
                    cp("dve", Vb[:, 2 + blk, dst], PS[b][:, src], r=[("ps", b)], w=[("Vb", 2 + blk)])

    try:
        ck('pro')
        for l in range(N_LAYERS_RUN):
            P.new_epoch()
            P.add("pool", lambda e: e.memset(Vb[:, :, 64:128], 1.0), w=[("Vb", k) for k in range(18)])
            for t in range(4):
                def o1(c, tmp, tk, t=t, l=l):
                    act(hT[:, c, tsl(t)], tmp[:, :], AF.Identity, r=[tk, "MOD"], w=[("h", t)], bias=MOD[:, l * 48 + c:l * 48 + c + 1])
                norm_tile(l, t, lambda c, l=l: GM1[:, l * 8 + c:l * 8 + c + 1], None, o1, None, SQ, RSTD, (TA, TB), ("TA", "TB"))

            ck('n1')
            for pr in DA_PAIRS:
                _sec = _CKN.get('_pairs', 0) > 0
                _CKN['_pairs'] = _CKN.get('_pairs', 0) + 1
                for hh in range(2):
                    if not (_sec and 'ld_kt' in SKIP):
                        dma_cast(KT[:, hh, 0:256], CKD[l, 2 * pr + hh], w=[("KT", hh, "c")])
                for blk in range(2):
                    if not (_sec and 'ld_vb' in SKIP):
                        dma_cast(Vb[:, blk, :], CVD[l, pr, blk], w=[("Vb", blk)])
                if not (_sec and 'ld_ws' in SKIP):
                    dma_cast(WS[:, :, 0:256], WSD[l, pr], w=["WS"])
                if not (_sec and 'ld_wo' in SKIP):
                    dma_cast(WO[:, 0, :], WOUT[l, :, pr, :], w=["WO"])
                ck('da_loads')
                state_proj(l, 256, lambda blk, s, l=l, pr=pr, _sec=_sec: (None if (_sec and 'st' in SKIP) else dma_st(STD[l, pr, blk * 128:(blk + 1) * 128, :], STG[:, s, 0:256], r=[("STG", s)])),
                           [] if (_sec and 'vcp' in SKIP) else [(slice(0, 64), slice(128, 192)), (slice(128, 192), slice(192, 256))])
                ck('da_state')
                srcs = []
                for t in range(4):
                    for hh in range(2):
                        h = 2 * pr + hh
                        srcs += [WF[l, 2 * h], WF[l, 2 * h + 1]]
                wr.plan(srcs)
                wi = 0
                for t in range(4):
                    rs = load_rope(0, t)
                    for hh in range(2):
                        wA, kA = wr.get(wi)
                        wB, kB = wr.get(wi + 1)
                        wi += 2
                        bA, bB = proj2(wA, kA, wB, kB, lambda c, t=t: hT[:, c, tsl(t)], [("h", t)])
                        cols = slice(256 + t * 512, 256 + (t + 1) * 512)
                        rope_evac(bA, bB, ROPEb[:, rs, 0, :], ROPEb[:, rs, 1, :], [("ROPEb", rs)],
                                  [(KT[:, hh, cols], slice(0, 128), [("KT", hh, t)], None)])
                        cp("pool", KT[32:64, hh, cols], AUG[32:64, cols], r=["AUG"], w=[("KT", hh, t)])
                        cp("pool", KT[96:128, hh, cols], AUG[32:64, cols], r=["AUG"], w=[("KT", hh, t)])
                ck('da_k')
                for hh in range(2):
                    kmax(hh, "da", (32, 96))
                ck('da_kmax')
                for j in range(4):
                    srcs = []
                    for hh in range(2):
                        h = 2 * pr + hh
                        srcs += [WF[l, 8 + 2 * h], WF[l, 8 + 2 * h + 1]]
                    wr.plan(srcs)
                    rs = load_rope(0, j)
                    for hh in range(2):
                        wA, kA = wr.get(2 * hh)
                        wB, kB = wr.get(2 * hh + 1)
                        bA, bB = proj2(wA, kA, wB, kB, lambda c, j=j: hT[:, c, tsl(j)], [("h", j)])
                        qv = QT[:, hh * 512:(hh + 1) * 512]
                        rope_evac(bA, bB, ROPEb[:, rs, 0, :], ROPEb[:, rs, 1, :], [("ROPEb", rs)], [(qv, slice(0, 128), [("QT", hh)], None)])
                        cp("pool", qv[32:64, :], AUG[0:32, tsl(j)], r=["AUG"], w=[("QT", hh)])
                        cp("pool", qv[96:128, :], AUG[0:32, tsl(j)], r=["AUG"], w=[("QT", hh)])
                        bound_rows(qv, ("QT", hh), hh, "da", (32, 96))
                    ck('da_q')
                    for hh in range(2):
                        qv = QT[:, hh * 512:(hh + 1) * 512]
                        vcols = slice(0, 128) if hh == 0 else slice(64, 192)
                        n0 = 0 if hh == 0 else 64
                        d0 = 64 - n0
                        nr, dr = slice(n0, n0 + 64), slice(d0, d0 + 64)
                        b1, b2 = attend_da(hh, qv, ("QT", hh), vcols, SC_DA)
                        act(RR[dr, 0:512], PS[b1][dr, :], AF.Ln, r=[("ps", b1)], w=["RR0"]); act(RR[dr, 0:512], RR[dr, 0:512], AF.Exp, r=["RR0"], w=["RR0"], scale=-1.0)
                        act(RR[dr, 512:1024], PS[b2][dr, :], AF.Ln, r=[("ps", b2)], w=["RR1"]); act(RR[dr, 512:1024], RR[dr, 512:1024], AF.Exp, r=["RR1"], w=["RR1"], scale=-1.0)
                        tt("dve", TA[nr, :], PS[b1][nr, :], RR[dr, 0:512], ALU.mult, r=[("ps", b1), "RR0"], w=["TA"])
                        tt("dve", TB[nr, :], PS[b2][nr, :], RR[dr, 512:1024], ALU.mult, r=[("ps", b2), "RR1"], w=["TB"])
                        stt("dve", TA[nr, :], TB[nr, :], NLAM[nr, l:l + 1], TA[nr, :], ALU.mult, ALU.add, r=["TA", "TB", "NLAM"], w=["TA"])
                        s = nxt("sq", 2)
                        tt("pool", SQ[nr, s, :], TA[nr, :], TA[nr, :], ALU.mult, r=["TA"], w=[("SQ", s)])
                        b = bank("proj")
                        mm(PS[b][:, :], ones[nr, :], SQ[nr, s, :], True, True, r=[("SQ", s), "CM"], w=[("ps", b)])
                        rsqrt_from_sum(TC[nr, :], PS[b][nr, :], 1.0 / 64, r=[("ps", b)], w=["TC"], tmp=LNT[nr, :], tmpkey="LNT")
                        tt("dve", TA[nr, :], TA[nr, :], TC[nr, :], ALU.mult, r=["TA", "TC"], w=["TA"])
                        ts("dve", MIX[nr, 0, :], TA[nr, :], VEC[nr, VEC_SUB + l:VEC_SUB + l + 1], 1.0 - LAM_INIT[l], ALU.mult, ALU.mult,
                           r=["TA", "VEC"], w=[("MIX", 0)])
                        ck('da_fin')
                    if l == 0 and pr == DA_PAIRS[0] and j == 0:
                        dump("QT", QT[:, 0:1024], [("QT", 0), ("QT", 1)])
                        dump("KT0", KT[:, 0, 0:1024], all_kt(0))
                        dump("Vb", Vb[:, 0:4, :], [("Vb", k) for k in range(4)])
                        dump("MIX", MIX[:, 0, :], [("MIX", 0)])
                        dump("NLAM", NLAM[:, :], ["NLAM"])
                        dump("KMAX", KMAX[:, :], [("KMAX", 0), ("KMAX", 1)])
                        dump("TA", TA[:, :], ["TA"])
                        dump("TB", TB[:, :], ["TB"])
                        dump("TC", TC[:, :], ["TC"])
                        dump("RR", RR[:, :], ["RR0", "RR1"])
                        dump("MOD", MOD[:, :], ["MOD"])
                    out_proj(l, j, 1, "WO")
                    ck('da_att')
            ck('da')

            QTs = QT[:, :].rearrange("p (b r q) -> p b r q", b=4, r=4)
            for g in range(2):
                dma_cast(KT[:, 0, 0:256], CKS[l, g], w=[("KT", 0, "c")])
                P.add("pool", lambda e: e.memset(KT[:, 0, 2304:2306], 0.0), w=[("KT", 0, "s")])
                cp("pool", KT[64:96, 0, 2304:2305], AUG[96:128, 2304:2305], r=["AUG"], w=[("KT", 0, "s")])
                for blk in range(2):
                    dma_cast(Vb[:, blk, 0:128], CVS[l, g, blk], w=[("Vb", blk)])
                dma_cast(WS[:, :, 0:128], WSS[l, g], w=["WS"])
                dma_cast(WO[:, 0:2, :], WOUT[l, :, 2 + 2 * g:4 + 2 * g, :], w=["WO"])
                state_proj(l, 128, lambda blk, s, l=l, g=g: dma_st(STS[l, g, blk * 128:(blk + 1) * 128, :], STG[:, s, 0:128], r=[("STG", s)]),
                           [(slice(0, 64), slice(64, 128))])
                wr.plan([WF[l, 16], WF[l, 17]] * 4)
                gr = slice(g * 64, g * 64 + 64)
                for t in range(4):
                    rs = load_rope(1, t)
                    wA, kA = wr.get(2 * t)
                    wB, kB = wr.get(2 * t + 1)
                    bA, bB = proj2(wA, kA, wB, kB, lambda c, t=t: hT[:, c, tsl(t)], [("h", t)])
                    cols = slice(256 + t * 512, 256 + (t + 1) * 512)
                    rope_evac(bA, bB, ROPEb[:, rs, 0, :], ROPEb[:, rs, 1, :], [("ROPEb", rs)],
                              [(KT[0:64, 0, cols], gr, [("KT", 0, t)], None)], rows=gr)
                    cp("pool", KT[64:96, 0, cols], AUG[96:128, cols], r=["AUG"], w=[("KT", 0, t)])
                kmax(0, "sw", (64,))
                cp("pool", QT[64:96, :], AUG[64:96, 0:2048], r=["AUG"], w=[("QT", 0), ("QT", 1)])
                for r_ in range(4):
                    dma_cast(QTs[65:66, :, r_, :], SINKd[l, 4 * g + r_:4 * g + r_ + 1, :].rearrange("o (b q) -> o b q", b=4),
                             w=[("QT", 0), ("QT", 1)])
                for j in range(4):
                    srcs = []
                    for p2 in range(2):
                        ti = 18 + 2 * (2 * g + p2)
                        srcs += [WF[l, ti], WF[l, ti + 1]]
                    wr.plan(srcs)
                    rs = load_rope(1, j)
                    for p2 in range(2):
                        wA, kA = wr.get(2 * p2)
                        wB, kB = wr.get(2 * p2 + 1)
                        bA, bB = proj2(wA, kA, wB, kB, lambda c, j=j: hT[:, c, tsl(j)], [("h", j)])
                        outs = []
                        for half in range(2):
                            r_ = 2 * p2 + half
                            outs.append((QTs[0:64, :, r_, :], slice(half * 64, half * 64 + 64), [("QT", 0), ("QT", 1)], 4))
                        rope_evac(bA, bB, ROPEb[:, rs, 0, :], ROPEb[:, rs, 1, :], [("ROPEb", rs)], outs)
                    for blk in range(4):
                        bound_rows(QT[:, blk * 512:(blk + 1) * 512], ("QT", 0), 0, "sw", (64,))
                    for blk in range(4):
                        i = 4 * j + blk
                        qv = QT[:, blk * 512:(blk + 1) * 512]
                        items = [(0, None), (1, None)]
                        if i > 0:
                            items.append((2 + i - 1, 0 + (i % 2)))
                        items.append((2 + i, None))
                        if i < 15:
                            items.append((2 + i + 1, 2 + (i % 2)))
                        bo = bank("O")
                        pend = []
                        for ii, (kb, mi) in enumerate(items):
                            bs = bank("S")
                            mm(PS[bs][:, :], KT[0:96, 0, kb * 128:(kb + 1) * 128], qv[0:96, :], True, mi is None,
                               r=ktkeys(0, kb) + [("QT", 0)], w=[("ps", bs)])
                            if mi is not None:
                                mm(PS[bs][:, :], ident, BM[:, mi, :], False, True, r=["CM", "BM"], w=[("ps", bs)])
                            s = nxt("pt", 4)
                            act(PT[:, s, :], PS[bs][:, :], AF.Exp, r=[("ps", bs)], w=[("PT", s)], scale=SC_SW)
                            pend.append((kb, s, ii))
                            if len(pend) > LOOK:
                                pkb, ps_, pi = pend.pop(0)
                                mm(PS[bo][:, :], Vb[:, pkb, 0:128], PT[:, ps_, :], pi == 0, False, r=[("PT", ps_), ("Vb", pkb)], w=[("ps", bo)])
                        bs = bank("S")
                        mm(PS[bs][0:1, :], KT[0:96, 0, 2304:2305], qv[0:96, :], True, True, r=[("KT", 0, "s"), ("QT", 0)], w=[("ps", bs)])
                        s = nxt("pt", 4)
                        act(PT[0:1, s, :], PS[bs][0:1, :], AF.Exp, r=[("ps", bs)], w=[("PT", s)], scale=SC_SW)
                        while pend:
                            pkb, ps_, pi = pend.pop(0)
                            mm(PS[bo][:, :], Vb[:, pkb, 0:128], PT[:, ps_, :], pi == 0, False, r=[("PT", ps_), ("Vb", pkb)], w=[("ps", bo)])
                        mm(PS[bo][:, :], VSK[0:1, :], PT[0:1, s, :], False, True, r=[("PT", s), "VSK"], w=[("ps", bo)])
                        act(RR[64:128, 0:512], PS[bo][64:128, :], AF.Ln, r=[("ps", bo)], w=["RR0"]); act(RR[64:128, 0:512], RR[64:128, 0:512], AF.Exp, r=["RR0"], w=["RR0"], scale=-1.0)
                        for r_ in range(4):
                            ch, rb = r_ // 2, (r_ % 2) * 64
                            tt("dve", MIX[rb:rb + 64, ch, blk * 128:(blk + 1) * 128], PS[bo][0:64, r_ * 128:(r_ + 1) * 128],
                               RR[64:128, r_ * 128:(r_ + 1) * 128], ALU.mult, r=[("ps", bo), "RR0"], w=[("MIX", ch)])
                    out_proj(l, j, 2, "WO")

            ck('swa')
            dma_cast(CKVT[:, 0:256], CCK[l], w=[("CKVT", "c")])
            dma_cast(WS[:, :, 0:160], WSM[l], w=["WS"])
            dma_cast(WKVb[:, :], WKV[l], w=["WKVb"])

            def mla_state(blk, s, l=l):
                tt("dve", TA[:, 0:128], STG[:, s, 0:128], STG[:, s, 0:128], ALU.mult, r=[("STG", s)], w=["TA"])
                P.add("dve", lambda e: e.reduce_sum(out=SMALL[:, 0:1], in_=TA[:, 0:128], axis=AX.X), r=["TA"], w=["SM0"])
                act(SMALL[:, 1:2], SMALL[:, 0:1], AF.Ln, r=["SM0"], w=["SM1"], scale=1.0 / 128, bias=EPS)
                act(SMALL[:, 2:3], SMALL[:, 1:2], AF.Exp, r=["SM1"], w=["SM2"], scale=-0.5)
                stt("dve", TB[:, 0:128], STG[:, s, 0:128], SMALL[:, 2:3], GKV[:, l * 128:(l + 1) * 128], ALU.mult, ALU.mult,
                    r=[("STG", s), "SM2", "GKV"], w=["TB"])
                dma_st(STM[l, blk * 128:(blk + 1) * 128, 0:128], TB[:, 0:128], r=["TB"])
                dma_st(STM[l, blk * 128:(blk + 1) * 128, 128:160], STG[:, s, 128:160], r=[("STG", s)])
            state_proj(l, 160, mla_state, [])
            wr.plan([WF[l, 30]] * 4)
            for t in range(4):
                wA, kA = wr.get(t)
                b = bank("proj")
                for c in range(8):
                    mm(PS[b][:, :], wA[:, c, :], hT[:, c, tsl(t)], c == 0, c == 7, r=[kA, ("h", t)], w=[("ps", b)])
                cp("act", TA[:, :], PS[b][:, :], r=[("ps", b)], w=["TA"])
                s = nxt("sq", 2)
                tt("pool", SQ[:, s, :], TA[:, :], TA[:, :], ALU.mult, r=["TA"], w=[("SQ", s)])
                b2 = bank("proj")
                mm(PS[b2][:, :], ones, SQ[:, s, :], True, True, r=[("SQ", s), "CM"], w=[("ps", b2)])
                rsqrt_from_sum(RSTD[:, :], PS[b2][:, :], 1.0 / 128, r=[("ps", b2)], w=["RSTD"], tmp=LNT[:, :], tmpkey="LNT")
                stt("dve", CKVT[:, 256 + t * 512:256 + (t + 1) * 512], TA[:, :], VEC[:, VEC_W - 4 + l:VEC_W - 3 + l], RSTD[:, :], ALU.mult, ALU.mult,
                    r=["TA", "RSTD", "VEC"], w=[("CKVT", t)])

            def ckeys(kb):
                return [("CKVT", "c")] if kb < 2 else [("CKVT", (kb - 2) // 4)]
            for pr in range(2):
                for hh in range(2):
                    h = 2 * pr + hh
                    if pr == 0:
                        dma_cast(KT[:, hh, 0:256], CKR[l], w=[("KT", hh, "c")])
                    dma_cast(WKNb[:, hh, :], WKN[l, h], w=[("WKNb", hh)])
                    dma_cast(WQBb[:, 2 * hh, :, :], WQB[l, 2 * h], w=[("WQBb", hh)])
                    dma_cast(WQBb[:, 2 * hh + 1, :, :], WQB[l, 2 * h + 1], w=[("WQBb", hh)])
                    if pr == 0:
                        cp("pool", KT[0:32, hh, 256:2304], AUG[32:64, 256:2304], r=["AUG"], w=[("KT", hh, t) for t in range(4)])
                    b = bank("proj")
                    mm(PS[b][:, 0:256], WKNb[:, hh, :], CKVT[:, 0:256], True, True, r=[("WKNb", hh), ("CKVT", "c")], w=[("ps", b)])
                    cp("act", KT[64:128, hh, 0:256], PS[b][64:128, 0:256], r=[("ps", b)], w=[("KT", hh, "c")])
                    for t in range(4):
                        b = bank("proj")
                        cols = slice(256 + t * 512, 256 + (t + 1) * 512)
                        mm(PS[b][:, :], WKNb[:, hh, :], CKVT[:, cols], True, True, r=[("WKNb", hh), ("CKVT", t)], w=[("ps", b)])
                        cp("act", KT[64:128, hh, cols], PS[b][64:128, :], r=[("ps", b)], w=[("KT", hh, t)])
                dma_cast(WO[:, 0, :], WOUT[l, :, 6 + pr, :], w=["WO"])
                for kb in range(18):
                    b = bank("proj")
                    mm(PS[b][:, 0:128], CKVT[:, kb * 128:(kb + 1) * 128], WKVb[:, pr * 128:(pr + 1) * 128], True, True,
                       r=ckeys(kb) + ["WKVb"], w=[("ps", b)])
                    cp("act", Vb[:, kb, 0:64], PS[b][:, 0:64], r=[("ps", b)], w=[("Vb", kb)])
                    cp("act", Vb[:, kb, 128:192], PS[b][:, 64:128], r=[("ps", b)], w=[("Vb", kb)])
                wr.plan([WF[l, 28], WF[l, 29]] * 4)
                for t in (range(4) if pr == 0 else []):
                    rs = load_rope(2, t)
                    wA, kA = wr.get(2 * t)
                    wB, kB = wr.get(2 * t + 1)
                    bA, bB = proj2(wA, kA, wB, kB, lambda c, t=t: hT[:, c, tsl(t)], [("h", t)])
                    cols = slice(256 + t * 512, 256 + (t + 1) * 512)
                    rope_evac(bA, bB, ROPEb[:, rs, 0, :], ROPEb[:, rs, 1, :], [("ROPEb", rs)],
                              [(KT[32:64, hh, cols], slice(32, 64), [("KT", hh, t)], None) for hh in range(2)], rows=slice(32, 64))
                for hh in range(2):
                    kmax(hh, "mla", (0,))
                for j in range(4):
                    wr.plan([WF[l, 26], WF[l, 27]])
                    rs = load_rope(2, j)
                    bq = bank("proj")
                    for jj in range(2):
                        wA, kA = wr.get(jj)
                        b = bank("proj")
                        if b == bq:
                            b = bank("proj")
                        for c in range(8):
                            mm(PS[b][:, :], wA[:, c, :], hT[:, c, tsl(j)], c == 0, c == 7, r=[kA, ("h", j)], w=[("ps", b)])
                        Tq, tk = (TA, "TA") if jj == 0 else (TB, "TB")
                        cp("act", Tq[:, :], PS[b][:, :], r=[("ps", b)], w=[tk])
                        tt("pool", SQ[:, jj, :], Tq[:, :], Tq[:, :], ALU.mult, r=[tk], w=[("SQ", jj)])
                        mm(PS[bq][:, :], ones, SQ[:, jj, :], jj == 0, jj == 1, r=[("SQ", jj), "CM"], w=[("ps", bq)])
                        ts("dve", QAG[:, jj, :], Tq[:, :], VEC[:, VEC_GQ + l * 2 + jj:VEC_GQ + l * 2 + jj + 1], None, ALU.mult, None,
                           r=[tk, "VEC"], w=[("QAG", jj)])
                    rsqrt_from_sum(RSTD[:, :], PS[bq][:, :], 1.0 / 256, r=[("ps", bq)], w=["RSTD"], tmp=LNT[:, :], tmpkey="LNT")
                    tt("pool", CSR[:, 0, :], ROPEb[:, rs, 0, :], RSTD[:, :], ALU.mult, r=[("ROPEb", rs), "RSTD"], w=["CSR"])
                    tt("pool", CSR[:, 1, :], ROPEb[:, rs, 1, :], RSTD[:, :], ALU.mult, r=[("ROPEb", rs), "RSTD"], w=["CSR"])
                    for hh in range(2):
                        bA, bB = proj2(WQBb[:, 2 * hh, :, :], ("WQBb", hh), WQBb[:, 2 * hh + 1, :, :], ("WQBb", hh),
                                       lambda c: QAG[:, c, :], [("QAG", 0), ("QAG", 1)], nk=2)
                        qv = QT[:, hh * 512:(hh + 1) * 512]
                        rope_evac(bA, bB, CSR[:, 0, :], CSR[:, 1, :], ["CSR"], [(qv, slice(0, 128), [("QT", hh)], None)])
                        cp("pool", qv[0:32, :], AUG[0:32, tsl(j)], r=["AUG"], w=[("QT", hh)])
                        bound_rows(qv, ("QT", hh), hh, "mla", (0,))
                    for hh in range(2):
                        qv = QT[:, hh * 512:(hh + 1) * 512]
                        vcols = slice(0, 128) if hh == 0 else slice(64, 192)
                        n0 = 0 if hh == 0 else 64
                        d0 = 64 - n0
                        nr, dr = slice(n0, n0 + 64), slice(d0, d0 + 64)
                        bo = attend(hh, slice(0, 128), qv, ("QT", hh), vcols, SC_MLA)
                        act(RR[dr, 0:512], PS[bo][dr, :], AF.Ln, r=[("ps", bo)], w=["RR0"]); act(RR[dr, 0:512], RR[dr, 0:512], AF.Exp, r=["RR0"], w=["RR0"], scale=-1.0)
                        tt("dve", MIX[nr, 0, :], PS[bo][nr, :], RR[dr, 0:512], ALU.mult, r=[("ps", bo), "RR0"], w=[("MIX", 0)])
                    if l == 0 and pr == 1 and j == 0:
                        dump("mQT", QT[:, 0:1024], [("QT", 0), ("QT", 1)])
                        dump("mKT0", KT[:, 0, 0:1024], all_kt(0))
                        dump("mVb", Vb[:, 0:4, :], [("Vb", k) for k in range(4)])
                        dump("mMIX", MIX[:, 0, :], [("MIX", 0)])
                        dump("mKMAX", KMAX[:, :], [("KMAX", 0), ("KMAX", 1)])
                        dump("mCSR", CSR[:, :, :], ["CSR"])
                        dump("mQAG", QAG[:, :, :], [("QAG", 0), ("QAG", 1)])
                        dump("mRSTD", RSTD[:, :], ["RSTD"])
                        dump("mCKVT", CKVT[:, 0:1024], [("CKVT", "c"), ("CKVT", 0), ("CKVT", 1)])
                        dump("mRR", RR[:, :], ["RR0"])
                    out_proj(l, j, 1, "WO")
                    if l == 0 and pr == 1 and j == 0:
                        ck('mla_att')

            ck('mla')
            P.barrier()
            for m in range(2):
                for sub in range(2):
                    t = 2 * m + sub
                    def o2(c, tmp, tk, sub=sub, l=l):
                        act(H2[:, c, sub * 512:(sub + 1) * 512], tmp[:, :], AF.Identity, r=[tk, "MOD"], w=[("H2", sub)],
                            bias=MOD[:, l * 48 + 24 + c:l * 48 + 25 + c])
                    norm_tile(l, t, lambda c, l=l: GM2[:, l * 8 + c:l * 8 + c + 1], None, o2, None, SQ2, RSTD2, (TA2, TB2), ("TA2", "TB2"))
                wm.plan([WUP[l, s].rearrange("p c n -> p (c n)") for s in range(8)] + [WDN[l, c].rearrange("p f n -> p (f n)") for c in range(8)])
                for s in range(8):
                    wv, wk = wm.get(s)
                    wv = wv.rearrange("p (c n) -> p c n", c=8)
                    for fi in range(4):
                        f = 4 * s + fi
                        bb = [bank("proj"), bank("proj")]
                        for c in range(8):
                            for half in range(2):
                                mm(PS[bb[half]][:, :], wv[:, c, fi * 128:(fi + 1) * 128], H2[:, c, half * 512:(half + 1) * 512], c == 0, c == 7,
                                   r=[wk, ("H2", half)], w=[("ps", bb[half])])
                        for half in range(2):
                            Tq, tk = (TA2, "TA2") if half == 0 else (TB2, "TB2")
                            act(Tq[:, :], PS[bb[half]][:, :], AF.Relu, r=[("ps", bb[half])], w=[tk])
                            tt("dve" if half == 0 else "pool", AT[:, f, half * 512:(half + 1) * 512], Tq[:, :], Tq[:, :], ALU.mult, r=[tk], w=[("AT", f)])
                for c in range(8):
                    wv, wk = wm.get(8 + c)
                    wv = wv.rearrange("p (f n) -> p f n", f=32)
                    bb = [bank("S"), bank("O")]
                    for f in range(32):
                        for half in range(2):
                            mm(PS[bb[half]][:, :], wv[:, f, :], AT[:, f, half * 512:(half + 1) * 512], f == 0, f == 31,
                               r=[wk, ("AT", f)], w=[("ps", bb[half])])
                    for half in range(2):
                        t = 2 * m + half
                        stt("dve", xT[:, c, tsl(t)], PS[bb[half]][:, :], MOD[:, l * 48 + 40 + c:l * 48 + 41 + c], xT[:, c, tsl(t)], ALU.mult, ALU.add,
                            r=[("ps", bb[half]), "MOD", ("x", c, t)], w=[("x", c, t)])
            P.barrier()
            ck('mlp')

    except _Stop:
        P.barrier()
    P.new_epoch()
    last_st = []
    for t in range(4):
        def o3(c, tmp, tk, t=t):
            last_st.append(dma_st(YT[c * 128:(c + 1) * 128, tsl(t)], tmp[:, :], r=[tk]))
        norm_tile(0, t, lambda c: VEC[:, VEC_FG + c:VEC_FG + c + 1], None, o3, None, SQ2, RSTD2, (TA2, TB2), ("TA2", "TB2"))

    P.barrier()
    P.add("sp", lambda e: e.dma_start(out=ONE1[:, 4:8], in_=CT[:, 0:4]), w=["FIN"], dma="l")
    P.add("pool", lambda e: e.memset(ONE1[:, 0:1], 1.0), r=["FIN"], w=["FIN2"])
    P.emit(nc, es)
    global LAST_COUNTS, LAST_NOPS
    LAST_NOPS = {e: sum(1 for o in P.ops if o.eng == e) for e in P.ENGS}
    LAST_NOPS['waits'] = sum(len(o.deps) for o in P.ops)
    LAST_COUNTS = {k: v for k, v in P.counts.items()}
    es.close()
    return nc


_CACHE = {}


def kernel(**inputs):
    inp = {k: np.asarray(v) for k, v in inputs.items()}
    if "nc" not in _CACHE:
        _CACHE["nc"] = build_program()
    nc = _CACHE["nc"]
    sh = _prep_shared(inp)
    in_maps = []
    for core in range(8):
        d = dict(sh)
        d.update(_prep_core(inp, core))
        in_maps.append({k: np.ascontiguousarray(v, dtype=np.float32) for k, v in d.items()})
    res = run_bass_kernel_spmd(nc, in_maps, core_ids=list(range(8)))
    R = res.results
    y_prompt = np.zeros((32, 256, D), np.float32)
    y_sample = np.zeros((4, NT, D), np.float32)
    ndk = np.zeros((32, L, 256, 4, 64), np.float32)
    ndv = np.zeros((32, L, 256, 4, 64), np.float32)
    nsk = np.zeros((32, L, 256, 2, 64), np.float32)
    nsv = np.zeros((32, L, 256, 2, 64), np.float32)
    nck = np.zeros((32, L, 256, 128), np.float32)
    nkr = np.zeros((32, L, 256, 32), np.float32)
    for core in range(8):
        y = np.asarray(R[core]["YT"]).T
        if core >= 4:
            y_sample[core - 4] = y
            continue
        sl = slice(core * 8, core * 8 + 8)
        y_prompt[sl] = y.reshape(8, 256, D)
        STD_ = np.asarray(R[core]["STD"])
        STS_ = np.asarray(R[core]["STS"])
        STM_ = np.asarray(R[core]["STM"])
        for l in range(L):
            for pr in range(2):
                ndk[sl, l, :, 2 * pr:2 * pr + 2, :] = STD_[l, pr, :, 0:128].reshape(8, 256, 2, 64)
                ndv[sl, l, :, 2 * pr:2 * pr + 2, :] = STD_[l, pr, :, 128:256].reshape(8, 256, 2, 64)
            for g in range(2):
                nsk[sl, l, :, g, :] = STS_[l, g, :, 0:64].reshape(8, 256, 64)
                nsv[sl, l, :, g, :] = STS_[l, g, :, 64:128].reshape(8, 256, 64)
            nck[sl, l] = STM_[l, :, 0:128].reshape(8, 256, 128)
            nkr[sl, l] = STM_[l, :, 128:160].reshape(8, 256, 32)
    return (y_prompt, y_sample, ndk, ndv, nsk, nsv, nck, nkr)
```
